# Optimizing a Trainium2 kernel written in Bass

```python
import math
import jax, jax.numpy as jnp
from jax import lax
import numpy as np

D_MODEL = 1024
BATCH = 1
SEQ = 16384
DEPTH = 1

CHUNK = 64
Q_BLOCK = 128
D_MIX = D_MODEL
DIFF_HEADS = 4
DIFF_QK_DIM = 64
DIFF_V_DIM = 2 * DIFF_QK_DIM
DIFF_WIDTH = DIFF_HEADS * DIFF_V_DIM
MLA_HEADS = 4
MLA_Q_RANK = 256
MLA_KV_RANK = 128
MLA_NOPE_DIM = 128
MLA_ROPE_DIM = 64
MLA_QK_DIM = MLA_NOPE_DIM + MLA_ROPE_DIM
MLA_V_DIM = 128
MLA_WIDTH = MLA_HEADS * MLA_V_DIM
ROPE_THETA = 10000.0
REL_BUCKETS = 32
REL_MAX_DIST = 128
RMS_EPS = 1e-6
NEG_INF = -1e30
COL_DIFF_Q = DIFF_HEADS * 2 * DIFF_QK_DIM
COL_DIFF_K = DIFF_HEADS * 2 * DIFF_QK_DIM
COL_DIFF_V = DIFF_WIDTH
COL_MLA_CQ = MLA_Q_RANK
COL_MLA_CKV = MLA_KV_RANK
COL_MLA_KR = MLA_ROPE_DIM
COL_GATE = D_MIX
IN_COLS = COL_DIFF_Q + COL_DIFF_K + COL_DIFF_V + COL_MLA_CQ + COL_MLA_CKV + COL_MLA_KR + COL_GATE

kernel_name = "hybrid_diffattn_mla_parallel_heads"


def _rms_norm(x, g):
    xf = x.astype(jnp.float32)
    y = xf * lax.rsqrt(jnp.mean(xf * xf, axis=-1, keepdims=True) + RMS_EPS)
    return (y * g.astype(jnp.float32)).astype(x.dtype)


def _rope(x, cos, sin):
    x1, x2 = jnp.split(x, 2, axis=-1)
    return jnp.concatenate([x1 * cos - x2 * sin, x2 * cos + x1 * sin], axis=-1).astype(x.dtype)


def _t5_bucket(rel):
    nb = REL_BUCKETS // 2
    max_exact = nb // 2
    ret = jnp.where(rel > 0, nb, 0)
    n = jnp.abs(rel)
    nf = jnp.maximum(n, 1).astype(jnp.float32)
    large = max_exact + (jnp.log(nf / max_exact) / math.log(REL_MAX_DIST / max_exact)
                         * (nb - max_exact)).astype(jnp.int32)
    large = jnp.minimum(large, nb - 1)
    return ret + jnp.where(n < max_exact, n, large)


def setup_inputs(seed: int = 0) -> dict:
    key = jax.random.key(seed)
    ks = jax.random.split(key, 24)
    f32 = jnp.float32
    nrm = lambda k, shape, s: jax.random.normal(k, shape, f32) * s
    gain = lambda k, shape: 1.0 + 0.1 * jax.random.normal(k, shape, f32)
    x = jax.random.normal(ks[0], (BATCH, SEQ, D_MODEL), f32)
    c = jax.random.normal(ks[1], (BATCH, D_MODEL), f32)
    offset = jax.random.randint(ks[2], (BATCH, 1), 0, 1024, dtype=jnp.int32) * CHUNK
    positions = (offset + jnp.arange(SEQ, dtype=jnp.int32)[None, :]).astype(jnp.int32)
    return {
        "x": x,
        "c": c,
        "positions": positions,
        "rel_bias": nrm(ks[3], (REL_BUCKETS, DIFF_HEADS), 0.5),
        "ada_w": nrm(ks[4], (DEPTH, D_MODEL, 3 * D_MODEL), 0.5 * D_MODEL ** -0.5),
        "ada_b": nrm(ks[5], (DEPTH, 3 * D_MODEL), 0.02),
        "norm_g": gain(ks[6], (DEPTH, D_MODEL)),
        "w_in": nrm(ks[7], (DEPTH, D_MODEL, IN_COLS), D_MODEL ** -0.5),
        "diff_q_norm": gain(ks[8], (DEPTH, DIFF_QK_DIM)),
        "diff_k_norm": gain(ks[9], (DEPTH, DIFF_QK_DIM)),
        "diff_lambda": nrm(ks[10], (DEPTH, 4, DIFF_QK_DIM), 0.1),
        "diff_sub_norm": gain(ks[11], (DEPTH, DIFF_V_DIM)),
        "mla_q_a_norm": gain(ks[12], (DEPTH, MLA_Q_RANK)),
        "mla_kv_a_norm": gain(ks[13], (DEPTH, MLA_KV_RANK)),
        "w_uq": nrm(ks[14], (DEPTH, MLA_Q_RANK, MLA_HEADS * MLA_QK_DIM), MLA_Q_RANK ** -0.5),
        "w_uk": nrm(ks[15], (DEPTH, MLA_KV_RANK, MLA_HEADS * MLA_NOPE_DIM), MLA_KV_RANK ** -0.5),
        "w_uv": nrm(ks[16], (DEPTH, MLA_KV_RANK, MLA_HEADS * MLA_V_DIM), MLA_KV_RANK ** -0.5),
        "mla_q_norm": gain(ks[17], (DEPTH, MLA_QK_DIM)),
        "mla_k_norm": gain(ks[18], (DEPTH, MLA_QK_DIM)),
        "w_out": nrm(ks[19], (DEPTH, D_MIX, D_MODEL), D_MIX ** -0.5),
    }


def _layer(x, c, positions, rel_bias, layer_idx, ada_w, ada_b, norm_g, w_in,
           diff_q_norm, diff_k_norm, diff_lambda, diff_sub_norm,
           mla_q_a_norm, mla_kv_a_norm, w_uq, w_uk, w_uv, mla_q_norm, mla_k_norm, w_out):
    B, S, _ = x.shape
    mod = jax.nn.silu(c) @ ada_w + ada_b
    shift, scale, gate = jnp.split(mod, 3, axis=-1)
    h = _rms_norm(x, norm_g) * (1.0 + scale[:, None, :]) + shift[:, None, :]

    proj = h @ w_in
    cuts = np.cumsum([COL_DIFF_Q, COL_DIFF_K, COL_DIFF_V, COL_MLA_CQ, COL_MLA_CKV, COL_MLA_KR])
    dq, dk, dv, cq, ckv, kr, g = jnp.split(proj, [int(v) for v in cuts], axis=-1)

    dq = _rms_norm(dq.reshape(B, S, DIFF_HEADS, 2, DIFF_QK_DIM), diff_q_norm)
    dk = _rms_norm(dk.reshape(B, S, DIFF_HEADS, 2, DIFF_QK_DIM), diff_k_norm)
    dv = dv.reshape(B, S, DIFF_HEADS, DIFF_V_DIM)
    lam_init = 0.8 - 0.6 * math.exp(-0.3 * (layer_idx - 1))
    lam_f = diff_lambda.astype(jnp.float32)
    lam = (jnp.exp(jnp.sum(lam_f[0] * lam_f[1])) - jnp.exp(jnp.sum(lam_f[2] * lam_f[3]))
           + lam_init)

    half = MLA_ROPE_DIM // 2
    inv_freq = ROPE_THETA ** (-jnp.arange(half, dtype=jnp.float32) / half)
    ang = positions.astype(jnp.float32)[:, :, None] * inv_freq
    cos, sin = jnp.cos(ang), jnp.sin(ang)
    q_m = (_rms_norm(cq, mla_q_a_norm) @ w_uq).reshape(B, S, MLA_HEADS, MLA_QK_DIM)
    q_nope, q_pe = q_m[..., :MLA_NOPE_DIM], q_m[..., MLA_NOPE_DIM:]
    q_pe = _rope(q_pe, cos[:, :, None, :], sin[:, :, None, :])
    ckv_n = _rms_norm(ckv, mla_kv_a_norm)
    k_nope = (ckv_n @ w_uk).reshape(B, S, MLA_HEADS, MLA_NOPE_DIM)
    v_m = (ckv_n @ w_uv).reshape(B, S, MLA_HEADS, MLA_V_DIM)
    k_pe = jnp.broadcast_to(_rope(kr, cos, sin)[:, :, None, :], (B, S, MLA_HEADS, MLA_ROPE_DIM))
    q_m = _rms_norm(jnp.concatenate([q_nope, q_pe], axis=-1), mla_q_norm)
    k_m = _rms_norm(jnp.concatenate([k_nope, k_pe], axis=-1), mla_k_norm)

    key_chunk = positions // CHUNK
    diff_scale = DIFF_QK_DIM ** -0.5
    mla_scale = MLA_QK_DIM ** -0.5

    def attend_block(i):
        q0 = i * Q_BLOCK
        qpos = lax.dynamic_slice_in_dim(positions, q0, Q_BLOCK, axis=1)
        allowed = key_chunk[:, None, :] <= (qpos // CHUNK)[:, :, None]
        bucket = _t5_bucket(positions[:, None, :] - qpos[:, :, None])
        bias = jnp.transpose(rel_bias[bucket], (0, 3, 1, 2)).astype(jnp.float32)
        dq_b = lax.dynamic_slice_in_dim(dq, q0, Q_BLOCK, axis=1)
        logit_d = jnp.einsum('bqhcd,bkhcd->bchqk', dq_b, dk).astype(jnp.float32) * diff_scale
        logit_d = jnp.where(allowed[:, None, None], logit_d + bias[:, None], NEG_INF)
        p_d = jax.nn.softmax(logit_d, axis=-1)
        attn_d = p_d[:, 0] - lam * p_d[:, 1]
        o_d = jnp.einsum('bhqk,bkhd->bqhd', attn_d.astype(dv.dtype), dv)
        o_d = _rms_norm(o_d, diff_sub_norm) * (1.0 - lam_init)
        q_b = lax.dynamic_slice_in_dim(q_m, q0, Q_BLOCK, axis=1)
        logit_m = jnp.einsum('bqhd,bkhd->bhqk', q_b, k_m).astype(jnp.float32) * mla_scale
        logit_m = jnp.where(allowed[:, None], logit_m, NEG_INF)
        p_m = jax.nn.softmax(logit_m, axis=-1)
        o_m = jnp.einsum('bhqk,bkhd->bqhd', p_m.astype(v_m.dtype), v_m)
        return jnp.concatenate([o_d.reshape(B, Q_BLOCK, DIFF_WIDTH),
                                o_m.reshape(B, Q_BLOCK, MLA_WIDTH)], axis=-1)

    n_blocks = S // Q_BLOCK
    mixed = lax.map(attend_block, jnp.arange(n_blocks, dtype=jnp.int32))
    mixed = jnp.transpose(mixed, (1, 0, 2, 3)).reshape(B, S, D_MIX)
    out = (mixed * jax.nn.silu(g)) @ w_out
    return x + gate[:, None, :] * out


def reference(x, c, positions, rel_bias, ada_w, ada_b, norm_g, w_in, diff_q_norm, diff_k_norm,
              diff_lambda, diff_sub_norm, mla_q_a_norm, mla_kv_a_norm, w_uq, w_uk, w_uv,
              mla_q_norm, mla_k_norm, w_out):
    for l in range(DEPTH):
        x = _layer(x, c, positions, rel_bias, l + 1, ada_w[l], ada_b[l], norm_g[l], w_in[l],
                   diff_q_norm[l], diff_k_norm[l], diff_lambda[l], diff_sub_norm[l],
                   mla_q_a_norm[l], mla_kv_a_norm[l], w_uq[l], w_uk[l], w_uv[l],
                   mla_q_norm[l], mla_k_norm[l], w_out[l])
    return x
```

```python
import math
import numpy as np
import ml_dtypes
import concourse.bass as bass
import concourse.mybir as mybir
from concourse.bass_utils import run_bass_kernel_spmd

F32 = mybir.dt.float32
BF16 = mybir.dt.bfloat16
I32 = mybir.dt.int32
AF = mybir.ActivationFunctionType
ALU = mybir.AluOpType
AX = mybir.AxisListType

NCORES = 8
D = 1024
KC = 8
INC = 3008
NA = 1984
C_DQ, C_DK, C_DV, C_CQ, C_CKV, C_KR, C_G = 0, 512, 1024, 1536, 1792, 1920, 1984
VW = 130
EPS = 1e-6
NEG = -30000.0
LAM_INIT = 0.8 - 0.6 * math.exp(-0.3 * 0.0)


class Sem:
    def __init__(self, h, name):
        self.h, self.n, self.name = h, 0, name


class Buf:
    def __init__(self, name, const=False, excl=False):
        self.name, self.w, self.r, self.const, self.excl = name, None, {}, const, excl


class Eng:
    def __init__(self, name, sem):
        self.name, self.sem, self.seen, self.ops = name, sem, {}, []


class Sched:
    def __init__(self):
        self.eng = {}
        self.dma_sems = []
        self.capture = None

    def record(self, f):
        self.capture = []
        f()
        ops, self.capture = self.capture, None
        return ops

    def replay_merged(self, *streams):
        streams = [st for st in streams if st]
        idx = [0] * len(streams)
        while True:
            best, bf = None, None
            for k, st in enumerate(streams):
                if idx[k] < len(st):
                    f = idx[k] / len(st)
                    if bf is None or f < bf:
                        best, bf = k, f
            if best is None:
                break
            self.op(*streams[best][idx[best]])
            idx[best] += 1

    def add_engine(self, name, semh):
        self.eng[name] = Eng(name, Sem(semh, name))

    def new_dma_sem(self, semh, name):
        s = Sem(semh, name)
        self.dma_sems.append(s)
        return s

    def op(self, eng, fn, reads=(), writes=(), dma=None):
        if self.capture is not None:
            self.capture.append((eng, fn, tuple(reads), tuple(writes), dma))
            return None
        E = self.eng[eng]
        waits = {}
        writes = list(writes) + [b for b in reads if b.excl and b not in writes]
        reads = [b for b in reads if not b.excl]

        def need(tok):
            if tok is None:
                return
            s, v = tok
            if waits.get(s, 0) < v:
                waits[s] = v
        for b in reads:
            need(b.w)
        for b in writes:
            need(b.w)
            for s, v in b.r.items():
                need((s, v))
        wl = []
        for s, v in waits.items():
            if eng == "pe" and s is E.sem:
                continue
            if E.seen.get(s, 0) < v:
                E.seen[s] = v
                wl.append((s.h, v))
        if dma is None:
            E.sem.n += 1
            tok = (E.sem, E.sem.n)
            inc = 1
        else:
            dma.n += 16
            tok = (dma, dma.n)
            inc = 16
        semh = tok[0].h

        def emit(e, wl=wl, fn=fn, semh=semh, inc=inc):
            for h, v in wl:
                e.wait_ge(h, v)
            fn(e).then_inc(semh, inc)
        E.ops.append(emit)
        for b in reads:
            if not b.const:
                if b.r.get(tok[0], 0) < tok[1]:
                    b.r[tok[0]] = tok[1]
        for b in writes:
            b.w = tok
            b.r = {}
        return tok

    def wait_all_dma(self, eng):
        E = self.eng[eng]
        wl = []
        for s in self.dma_sems:
            if s.n > 0 and E.seen.get(s, 0) < s.n:
                E.seen[s] = s.n
                wl.append((s.h, s.n))

        def emit(e, wl=wl):
            for h, v in wl:
                e.wait_ge(h, v)
        E.ops.append(emit)

    def flush_block(self, nc):
        self.wait_all_dma("sp")
        self.wait_all_dma("pool")
        with nc.Block() as block:
            for name, deco in (("sp", block.sync), ("pe", block.tensor), ("act", block.scalar),
                               ("dve", block.vector), ("pool", block.gpsimd)):
                ops = self.eng[name].ops
                if not ops:
                    continue

                def body(e, ops=ops):
                    for f in ops:
                        f(e)
                deco(body)
                self.eng[name].ops = []
        for E in self.eng.values():
            for E2 in self.eng.values():
                E.seen[E2.sem] = E2.sem.n
            for s in self.dma_sems:
                E.seen[s] = s.n


def bcast_rows(ap1n, parts=128):
    n = ap1n.shape[-1]
    return bass.AP(ap1n.tensor, ap1n.offset, [[0, parts], [1, n]])


def build(S_len):
    NT = S_len // 128
    OT = NT // NCORES
    GQ = min(4, OT)
    NG = OT // GQ
    CH = min(16, NT)
    ST = 2
    NST = NT // ST
    SP = NT * 128

    nc = bass.Bass("TRN2", target_bir_lowering=False)
    S = Sched()

    def din(name, shape, dt=F32):
        return nc.dram_tensor(name, list(shape), dt, kind="ExternalInput").ap()

    x_pad = din("x_pad", [SP, D])
    x_own = din("x_own", [OT * 128, D])
    pos_pad = din("pos_pad", [NT, 128], I32)
    valid_in = din("valid", [128, NT])
    c_col_in = din("c_col", [128, KC])
    g_col_in = din("g_col", [128, KC])
    ada_w = din("ada_w", [D, 3072])
    ada_b = din("ada_b", [1, 3072])
    w_in = din("w_in", [D, INC])
    w_out = din("w_out", [D, D])
    w_uq = din("w_uq", [256, 768])
    w_uk = din("w_uk", [128, 512])
    w_uv = din("w_uv", [128, 512])
    qa_col_in = din("qa_col", [128, 2])
    kva_col_in = din("kva_col", [128, 1])
    sub_col_in = din("sub_col", [128, 1])
    dqn_in = din("dqn", [1, 64])
    dkn_in = din("dkn", [1, 64])
    mqn_in = din("mqn", [1, 192])
    mkn_in = din("mkn", [1, 192])
    lam_in = din("lam", [1, 256])
    relb_in = din("relb", [1, 128])
    invf_in = din("invf", [1, 32])
    ident_in = din("ident", [128, 128])
    idx_in = din("bidx", [128, 256])
    mmask_in = din("mmask", [128, 128])
    y_own = nc.dram_tensor("y_own", [OT * 128, D], F32, kind="ExternalOutput").ap()

    KdT_d = nc.dram_tensor("KdT_d", [8, 64, SP], BF16).ap()
    KmTn_d = nc.dram_tensor("KmTn_d", [4, 128, SP], BF16).ap()
    KmTp_d = nc.dram_tensor("KmTp_d", [4, 64, SP], BF16).ap()
    Vd_d = nc.dram_tensor("Vd_d", [4, 128, NT, VW], BF16).ap()
    Vm_d = nc.dram_tensor("Vm_d", [4, 128, NT, VW], BF16).ap()

    from contextlib import ExitStack
    sem_stack = ExitStack()
    for en in ("pe", "act", "dve", "pool", "sp"):
        S.add_engine(en, sem_stack.enter_context(nc.semaphore("sem_" + en)))

    def dsem(name):
        return S.new_dma_sem(sem_stack.enter_context(nc.semaphore("d_" + name)), name)

    def sb(stack, name, shape, dt=F32):
        return stack.enter_context(nc.sbuf_tensor("s_" + name, list(shape), dt))

    def dma(q, out, in_, sem, reads=(), writes=()):
        return S.op(q, lambda e: e.dma_start(out=out, in_=in_), reads=reads, writes=writes, dma=sem)

    G = ExitStack()
    PS = [G.enter_context(nc.psum_tensor("ps%d" % i, [128, 512], F32)) for i in range(8)]
    PB = [Buf("psb%d" % i, excl=True) for i in range(8)]

    def ps_bf(i):
        return PS[i][:, :].bitcast(BF16)

    A_col = sb(G, "A_col", [128, KC])
    shT = sb(G, "shT", [128, KC, 128])
    gate_bc = sb(G, "gate_bc", [128, 1024])
    QW = max(OT * 128, 2048)
    QdT = sb(G, "QdT", [128, 8, QW], BF16)
    QmTn = sb(G, "QmTn", [128, 4, QW], BF16)
    QmTp = sb(G, "QmTp", [128, 4, QW], BF16)
    wsc = sb(G, "wsc", [128, 1])
    b_wsc = Buf("wsc", True)
    QdT_f = QdT[:, :, :].rearrange("p a b -> p (a b)").bitcast(F32)
    QmTn_f = QmTn[:, :, :].rearrange("p a b -> p (a b)").bitcast(F32)
    QmTp_f = QmTp[:, :, :].rearrange("p a b -> p (a b)").bitcast(F32)
    ident_f = sb(G, "ident_f", [128, 128])
    ident_b = sb(G, "ident_b", [128, 128], BF16)
    Bd = sb(G, "Bd", [128, 2, 4, 128])
    Bm = sb(G, "Bm", [128, 128])
    nlam = sb(G, "nlam", [128, 1])
    eps_c = sb(G, "eps_c", [128, 1])
    xt = [sb(G, "xt%d" % i, [128, D]) for i in range(2)]
    xs2 = [sb(G, "xs0", [128, D], BF16)] * 3
    xT2 = [sb(G, "xT%d" % i, [128, KC, 128], BF16) for i in range(3)]
    junk = sb(G, "junk", [128, 256], BF16)
    st1 = sb(G, "st1", [128, 12])

    b_Wg, b_bWg, b_gate = Buf("Wg"), Buf("bWg"), Buf("gate", True)
    b_QdT, b_QmTn, b_QmTp = Buf("QdT"), Buf("QmTn"), Buf("QmTp")
    b_identf, b_identb, b_Bd, b_Bm, b_nlam, b_eps = (Buf("idf", True), Buf("idb", True), Buf("Bd", True),
                                                      Buf("Bm", True), Buf("nlam", True), Buf("eps", True))
    b_xt = [Buf("xt0"), Buf("xt1")]
    b_xs2, b_xT2, b_junk, b_st1_2 = [Buf("xs0")] * 3, [Buf("xT0"), Buf("xT1"), Buf("xT2")], Buf("junk"), [Buf("st1a"), Buf("st1b"), Buf("st1c")]
    sem_xt = [dsem("xt0"), dsem("xt1")]
    class _Fresh:
        cnt = 0
    def fresh_sem():
        _Fresh.cnt += 1
        return dsem("m%d" % _Fresh.cnt)

    def rstd_act(out_ap, in_ap, n, tmp_ap, bufs_r, bufs_w, b_tmp):
        S.op("act", lambda e: e.activation(out=tmp_ap, in_=in_ap, func=AF.Ln, scale=1.0 / n, bias=eps_c[:, 0:1]),
             reads=list(bufs_r) + [b_eps], writes=[b_tmp])
        S.op("act", lambda e: e.activation(out=out_ap, in_=tmp_ap, func=AF.Exp, scale=-0.5),
             reads=[b_tmp], writes=list(bufs_w))

    def load_x(src_ap, slot):
        dma("sp", xt[slot][:, :], src_ap, sem_xt[slot], writes=[b_xt[slot]])

    def norm_transpose_x(src_ap, slot, load=True, after_read=None, tslot=None):
        if tslot is None:
            tslot = slot
        xs, xT, b_xs, b_xT, b_st1 = xs2[tslot], xT2[tslot], b_xs2[tslot], b_xT2[tslot], b_st1_2[tslot]
        o = tslot * 4
        if load:
            load_x(src_ap, slot)
        S.op("act", lambda e: e.activation(out=xs[:, :], in_=xt[slot][:, :], func=AF.Square, accum_out=st1[:, o:o + 1]),
             reads=[b_xt[slot]], writes=[b_xs, b_st1])
        rstd_act(st1[:, o + 2:o + 3], st1[:, o:o + 1], D, st1[:, o + 1:o + 2], [b_st1], [b_st1], b_st1)
        S.op("act", lambda e: e.activation(out=xs[:, :], in_=xt[slot][:, :], func=AF.Copy, scale=st1[:, o + 2:o + 3]),
             reads=[b_xt[slot], b_st1], writes=[b_xs])
        if after_read is not None:
            after_read()

        def tr(e):
            ins = None
            for k in range(KC):
                ins = e.transpose(out=ps_bf(0)[:, k * 128:(k + 1) * 128], in_=xs[:, k * 128:(k + 1) * 128], identity=ident_b[:, :])
            return ins
        S.op("pe", tr, reads=[b_xs, b_identb], writes=[PB[0]])
        S.op("dve", lambda e: e.tensor_copy(out=xT[:, :, :].rearrange("p k t -> p (k t)"), in_=ps_bf(0)[:, :]),
             reads=[PB[0]], writes=[b_xT])

    def proj(bank, c0, ncols, W, b_W, wc0, slot):
        xT, b_xT = xT2[slot], b_xT2[slot]

        def mm(e):
            ins = None
            for k in range(KC):
                ins = e.matmul(PS[bank][:, c0:c0 + ncols], lhsT=xT[:, k, :], rhs=W[:, k, wc0:wc0 + ncols],
                               start=(k == 0), stop=(k == KC - 1))
            return ins
        S.op("pe", mm, reads=[b_xT, b_W], writes=[PB[bank]])

    S12 = ExitStack()
    Wa = sb(S12, "Wa", [128, KC, NA], BF16)
    bWa = sb(S12, "bWa", [128, NA])
    wuq_b = sb(S12, "wuq_b", [128, 2, 768], BF16)
    wuk_b = sb(S12, "wuk_b", [128, 512], BF16)
    wuv_b = sb(S12, "wuv_b", [128, 512], BF16)
    gqk_d = sb(S12, "gqk_d", [128, 64])
    gqk_m = sb(S12, "gqk_m", [128, 192])
    cosT = sb(S12, "cosT", [128, NT, 32])
    sinT = sb(S12, "sinT", [128, NT, 32])
    valid = sb(S12, "valid_sb", [128, NT])
    b_Wa, b_bWa, b_wuq, b_wuk, b_wuv = Buf("Wa", True), Buf("bWa", True), Buf("wuq", True), Buf("wuk", True), Buf("wuv", True)
    b_gqkd, b_gqkm, b_cos, b_sin, b_valid = Buf("gqkd", True), Buf("gqkm", True), Buf("cos", True), Buf("sin", True), Buf("valid", True)

    SU = ExitStack()
    class _V:
        def __init__(self, ap):
            self.ap = ap
        def __getitem__(self, k):
            return self.ap[k]
    wbuf = [_V(QmTn_f[:, 0:3072]), _V(QdT_f[:, 0:3072])]
    b_wbuf = [Buf("wbuf0"), Buf("wbuf1")]
    sem_wbuf = [dsem("wbuf0"), dsem("wbuf1")]
    mod_bc = _V(QdT_f[:, 3072:6144])
    adab_bc = _V(QmTp_f[:, 0:3072])
    c_col = sb(SU, "c_col", [128, KC])
    sc_col = sb(SU, "sc_col", [128, KC])
    scb = sb(SU, "scb", [128, KC, 128])
    g_col = sb(SU, "g_col", [128, KC])
    scl_col = sb(SU, "scl_col", [128, KC])
    small = sb(SU, "small", [128, 16])
    vec = sb(SU, "vec", [128, 1024])
    idx = sb(SU, "idx", [128, 256])
    oh = sb(SU, "oh", [128, 256])
    relb = sb(SU, "relb", [128, 128])
    rbs = sb(SU, "rbs", [128, 128])
    posi = sb(SU, "posi", [128, 128], I32)
    posf = sb(SU, "posf", [128, 128])
    posT = sb(SU, "posT", [128, NT])
    invf = sb(SU, "invf", [128, 32])
    ang = _V(QdT_f[:, 0:NT * 32])
    kf = _V(QdT_f[:, 4096:4096 + NT * 32])
    ki = _V(QmTn[:, :, :].rearrange("p a b -> p (a b)").bitcast(I32)[:, 0:NT * 32])
    r2 = _V(QmTp_f[:, 0:NT * 32])
    b_mod, b_adab, b_ccol, b_sccol, b_scb, b_gcol, b_Acol, b_shT, b_sclcol = [Buf(n) for n in
        ("mod", "adab", "ccol", "sccol", "scb", "gcol", "Acol", "shT", "sclcol")]
    b_small, b_vec, b_idx, b_oh, b_relb, b_rbs = [Buf(n) for n in ("small", "vec", "idx", "oh", "relb", "rbs")]
    b_posi, b_posf, b_posT, b_invf, b_ang, b_kf, b_ki, b_r2 = [Buf(n) for n in
        ("posi", "posf", "posT", "invf", "ang", "kf", "ki", "r2")]

    dma("sp", ident_f[:, :], ident_in, fresh_sem(), writes=[b_identf])
    dma("sp", c_col[:, :], c_col_in, fresh_sem(), writes=[b_ccol])
    dma("sp", g_col[:, :], g_col_in, fresh_sem(), writes=[b_gcol])
    dma("sp", adab_bc[:, :], bcast_rows(ada_b), fresh_sem(), writes=[b_adab])
    dma("sp", idx[:, :], idx_in, fresh_sem(), writes=[b_idx])
    dma("sp", Bm[:, :], mmask_in, fresh_sem(), writes=[b_Bm])
    dma("sp", relb[:, :], bcast_rows(relb_in), fresh_sem(), writes=[b_relb])
    dma("sp", valid[:, :], valid_in, fresh_sem(), writes=[b_valid])
    dma("sp", invf[:, :], bcast_rows(invf_in), fresh_sem(), writes=[b_invf])
    dma("sp", posi[0:NT, :], pos_pad, fresh_sem(), writes=[b_posi])
    S.op("dve", lambda e: e.memset(eps_c[:, :], EPS), writes=[b_eps])
    S.op("dve", lambda e: e.tensor_copy(out=ident_b[:, :], in_=ident_f[:, :]), reads=[b_identf], writes=[b_identb])

    S.op("act", lambda e: e.activation(out=sc_col[:, :], in_=c_col[:, :], func=AF.Silu), reads=[b_ccol], writes=[b_sccol])
    S.op("dve", lambda e: e.tensor_copy(out=scb[:, :, :], in_=sc_col[:, :].unsqueeze(2).broadcast_to([128, KC, 128])),
         reads=[b_sccol], writes=[b_scb])
    for k in range(KC):
        sl = k % 2
        dma("sp", wbuf[sl][:, :], ada_w[k * 128:(k + 1) * 128, :], sem_wbuf[sl], writes=[b_wbuf[sl]])

        def mm(e, k=k, sl=sl):
            ins = None
            for j in range(6):
                ins = e.matmul(PS[j][:, :], lhsT=scb[:, k, :], rhs=wbuf[sl][:, j * 512:(j + 1) * 512],
                               start=(k == 0), stop=(k == KC - 1))
            return ins
        S.op("pe", mm, reads=[b_scb, b_wbuf[sl]], writes=PB[0:6])
    for j in range(6):
        S.op("dve", lambda e, j=j: e.tensor_tensor(out=mod_bc[:, j * 512:(j + 1) * 512], in0=PS[j][:, :],
                                                   in1=adab_bc[:, j * 512:(j + 1) * 512], op=ALU.add),
             reads=[PB[j], b_adab], writes=[b_mod])
    S.op("dve", lambda e: e.tensor_copy(out=gate_bc[:, :], in_=mod_bc[:, 2048:3072]), reads=[b_mod], writes=[b_gate])
    for k in range(KC):
        bank = 6 + (k % 2)
        S.op("pe", lambda e, k=k, bank=bank: e.transpose(out=PS[bank][:, 0:128], in_=mod_bc[:, k * 128:(k + 1) * 128], identity=ident_f[:, :]),
             reads=[b_mod, b_identf], writes=[PB[bank]])
        S.op("dve", lambda e, k=k, bank=bank: e.tensor_copy(out=shT[:, k, :], in_=PS[bank][:, 0:128]), reads=[PB[bank]], writes=[b_shT])
        S.op("pe", lambda e, k=k, bank=bank: e.transpose(out=PS[bank][:, 128:256], in_=mod_bc[:, 1024 + k * 128:1024 + (k + 1) * 128], identity=ident_f[:, :]),
             reads=[b_mod, b_identf], writes=[PB[bank]])
        S.op("dve", lambda e, k=k, bank=bank: e.tensor_copy(out=scl_col[:, k:k + 1], in_=PS[bank][:, 128:129]), reads=[PB[bank]], writes=[b_sclcol])
    S.op("dve", lambda e: e.scalar_tensor_tensor(out=A_col[:, :], in0=scl_col[:, :], scalar=1.0, in1=g_col[:, :], op0=ALU.add, op1=ALU.mult),
         reads=[b_sclcol, b_gcol], writes=[b_Acol])

    for k in range(KC):
        sl = k % 2
        dma("sp", wbuf[sl][:, 0:NA], w_in[k * 128:(k + 1) * 128, 0:NA], sem_wbuf[sl], writes=[b_wbuf[sl]])

        def mm(e, k=k, sl=sl):
            ins = None
            for j in range(4):
                w = min(512, NA - j * 512)
                ins = e.matmul(PS[j][:, 0:w], lhsT=shT[:, k, :], rhs=wbuf[sl][:, j * 512:j * 512 + w],
                               start=(k == 0), stop=(k == KC - 1))
            return ins
        S.op("pe", mm, reads=[b_shT, b_wbuf[sl]], writes=PB[0:4])
        S.op("act", lambda e, k=k, sl=sl: e.activation(out=Wa[:, k, :], in_=wbuf[sl][:, 0:NA], func=AF.Copy, scale=A_col[:, k:k + 1]),
             reads=[b_wbuf[sl], b_Acol], writes=[b_Wa])
    for j in range(4):
        w = min(512, NA - j * 512)
        S.op("dve", lambda e, j=j, w=w: e.tensor_copy(out=bWa[:, j * 512:j * 512 + w], in_=PS[j][:, 0:w]), reads=[PB[j]], writes=[b_bWa])

    dma("sp", small[:, 0:1], sub_col_in, fresh_sem(), writes=[b_small])
    dma("sp", small[:, 1:3], qa_col_in, fresh_sem(), writes=[b_small])
    dma("sp", small[:, 3:4], kva_col_in, fresh_sem(), writes=[b_small])
    S.op("dve", lambda e: e.tensor_scalar(out=small[:, 4:5], in0=small[:, 0:1], scalar1=1.0 - LAM_INIT, scalar2=None, op0=ALU.mult),
         reads=[b_small], writes=[b_small])
    S.op("dve", lambda e: e.tensor_copy(out=wsc[:, :], in_=small[:, 4:5]), reads=[b_small], writes=[b_wsc])
    for k in range(2):
        dma("sp", wbuf[k][:, 0:768], w_uq[k * 128:(k + 1) * 128, :], sem_wbuf[k], writes=[b_wbuf[k]])
        S.op("act", lambda e, k=k: e.activation(out=wuq_b[:, k, :], in_=wbuf[k][:, 0:768], func=AF.Copy, scale=small[:, 1 + k:2 + k]),
             reads=[b_wbuf[k], b_small], writes=[b_wuq])
    dma("sp", wbuf[0][:, 0:512], w_uk, sem_wbuf[0], writes=[b_wbuf[0]])
    S.op("act", lambda e: e.activation(out=wuk_b[:, :], in_=wbuf[0][:, 0:512], func=AF.Copy, scale=small[:, 3:4]),
         reads=[b_wbuf[0], b_small], writes=[b_wuk])
    dma("sp", wbuf[1][:, 0:512], w_uv, sem_wbuf[1], writes=[b_wbuf[1]])
    S.op("act", lambda e: e.activation(out=wuv_b[:, :], in_=wbuf[1][:, 0:512], func=AF.Copy, scale=small[:, 3:4]),
         reads=[b_wbuf[1], b_small], writes=[b_wuv])

    dma("sp", vec[:, 0:64], bcast_rows(dqn_in), fresh_sem(), writes=[b_vec])
    dma("sp", vec[:, 64:128], bcast_rows(dkn_in), fresh_sem(), writes=[b_vec])
    dma("sp", vec[:, 128:320], bcast_rows(mqn_in), fresh_sem(), writes=[b_vec])
    dma("sp", vec[:, 320:512], bcast_rows(mkn_in), fresh_sem(), writes=[b_vec])
    dma("sp", vec[:, 512:768], bcast_rows(lam_in), fresh_sem(), writes=[b_vec])
    S.op("dve", lambda e: e.scalar_tensor_tensor(out=gqk_d[:, :], in0=vec[:, 0:64], scalar=64 ** -0.5, in1=vec[:, 64:128], op0=ALU.mult, op1=ALU.mult),
         reads=[b_vec], writes=[b_gqkd])
    S.op("dve", lambda e: e.scalar_tensor_tensor(out=gqk_m[:, :], in0=vec[:, 128:320], scalar=192 ** -0.5, in1=vec[:, 320:512], op0=ALU.mult, op1=ALU.mult),
         reads=[b_vec], writes=[b_gqkm])
    S.op("dve", lambda e: e.tensor_tensor(out=vec[:, 768:832], in0=vec[:, 512:576], in1=vec[:, 576:640], op=ALU.mult), reads=[b_vec], writes=[b_vec])
    S.op("dve", lambda e: e.tensor_tensor(out=vec[:, 832:896], in0=vec[:, 640:704], in1=vec[:, 704:768], op=ALU.mult), reads=[b_vec], writes=[b_vec])
    S.op("dve", lambda e: e.tensor_reduce(out=small[:, 8:10], in_=vec[:, 768:896].rearrange("p (a d) -> p a d", a=2), axis=AX.X, op=ALU.add),
         reads=[b_vec], writes=[b_small])
    S.op("act", lambda e: e.activation(out=small[:, 10:12], in_=small[:, 8:10], func=AF.Exp), reads=[b_small], writes=[b_small])
    S.op("dve", lambda e: e.scalar_tensor_tensor(out=nlam[:, :], in0=small[:, 11:12], scalar=-LAM_INIT, in1=small[:, 10:11], op0=ALU.add, op1=ALU.subtract),
         reads=[b_small], writes=[b_nlam])

    for h in range(4):
        S.op("dve", lambda e, h=h: e.tensor_scalar(out=rbs[:, h * 32:(h + 1) * 32], in0=relb[:, :].rearrange("p (b h) -> p h b", h=4)[:, h, :],
                                                   scalar1=relb[:, 15 * 4 + h:15 * 4 + h + 1], scalar2=None, op0=ALU.subtract),
             reads=[b_relb], writes=[b_rbs])
    S.op("pool", lambda e: e.memset(Bd[:, :, :, :], 0.0), writes=[b_Bd])
    for b in list(range(0, 32)) + [-1]:
        S.op("dve", lambda e, b=b: e.tensor_scalar(out=oh[:, :], in0=idx[:, :], scalar1=float(b), scalar2=None, op0=ALU.is_equal),
             reads=[b_idx], writes=[b_oh])
        for h in range(4):
            for ty in range(2):
                if b == -1:
                    S.op("dve", lambda e, h=h, ty=ty: e.scalar_tensor_tensor(out=Bd[:, ty, h, :], in0=oh[:, ty * 128:(ty + 1) * 128], scalar=NEG,
                                                                             in1=Bd[:, ty, h, :], op0=ALU.mult, op1=ALU.add),
                         reads=[b_oh], writes=[b_Bd])
                else:
                    S.op("dve", lambda e, h=h, ty=ty, b=b: e.scalar_tensor_tensor(out=Bd[:, ty, h, :], in0=oh[:, ty * 128:(ty + 1) * 128],
                                                                                  scalar=rbs[:, h * 32 + b:h * 32 + b + 1],
                                                                                  in1=Bd[:, ty, h, :], op0=ALU.mult, op1=ALU.add),
                         reads=[b_oh, b_rbs], writes=[b_Bd])

    S.flush_block(nc)
    S.op("dve", lambda e: e.tensor_copy(out=posf[0:NT, :], in_=posi[0:NT, :]), reads=[b_posi], writes=[b_posf])
    S.op("pe", lambda e: e.transpose(out=PS[7][:, 0:NT], in_=posf[0:NT, :], identity=ident_f[0:NT, 0:NT]), reads=[b_posf, b_identf], writes=[PB[7]])
    S.op("dve", lambda e: e.tensor_copy(out=posT[:, :], in_=PS[7][:, 0:NT]), reads=[PB[7]], writes=[b_posT])
    ang3 = ang[:, :].rearrange("p (t f) -> p t f", f=32)
    S.op("dve", lambda e: e.tensor_tensor(out=ang3, in0=posT[:, :].unsqueeze(2).broadcast_to([128, NT, 32]),
                                          in1=invf[:, :].unsqueeze(1).broadcast_to([128, NT, 32]), op=ALU.mult),
         reads=[b_posT, b_invf], writes=[b_ang])
    TWO_PI = 2.0 * math.pi
    C1 = 6.28125
    C2 = float(np.float32(0.0019350051879882812))
    C3 = float(np.float32(TWO_PI - C1 - C2))

    def reduce_sin(dst3, shift):
        S.op("dve", lambda e: e.tensor_scalar(out=kf[:, :], in0=ang[:, :], scalar1=1.0 / TWO_PI, scalar2=None, op0=ALU.mult), reads=[b_ang], writes=[b_kf])
        S.op("dve", lambda e: e.tensor_copy(out=ki[:, :], in_=kf[:, :]), reads=[b_kf], writes=[b_ki])
        S.op("dve", lambda e: e.tensor_copy(out=kf[:, :], in_=ki[:, :]), reads=[b_ki], writes=[b_kf])
        S.op("dve", lambda e: e.scalar_tensor_tensor(out=r2[:, :], in0=kf[:, :], scalar=-C1, in1=ang[:, :], op0=ALU.mult, op1=ALU.add), reads=[b_kf, b_ang], writes=[b_r2])
        S.op("dve", lambda e: e.scalar_tensor_tensor(out=r2[:, :], in0=kf[:, :], scalar=-C2, in1=r2[:, :], op0=ALU.mult, op1=ALU.add), reads=[b_kf], writes=[b_r2])
        S.op("dve", lambda e: e.scalar_tensor_tensor(out=r2[:, :], in0=kf[:, :], scalar=-C3, in1=r2[:, :], op0=ALU.mult, op1=ALU.add), reads=[b_kf], writes=[b_r2])
        if shift != 0.0:
            S.op("dve", lambda e: e.tensor_scalar(out=r2[:, :], in0=r2[:, :], scalar1=shift, scalar2=None, op0=ALU.add), reads=[], writes=[b_r2])
        S.op("dve", lambda e: e.tensor_scalar(out=kf[:, :], in0=r2[:, :], scalar1=math.pi, scalar2=-TWO_PI, op0=ALU.is_gt, op1=ALU.mult), reads=[b_r2], writes=[b_kf])
        S.op("dve", lambda e: e.tensor_tensor(out=r2[:, :], in0=r2[:, :], in1=kf[:, :], op=ALU.add), reads=[b_kf], writes=[b_r2])
        S.op("dve", lambda e: e.tensor_scalar(out=kf[:, :], in0=r2[:, :], scalar1=-math.pi, scalar2=TWO_PI, op0=ALU.is_lt, op1=ALU.mult), reads=[b_r2], writes=[b_kf])
        S.op("dve", lambda e: e.tensor_tensor(out=r2[:, :], in0=r2[:, :], in1=kf[:, :], op=ALU.add), reads=[b_kf], writes=[b_r2])
        S.op("dve", lambda e: e.tensor_scalar(out=r2[:, :], in0=r2[:, :], scalar1=-3.1415925, scalar2=3.1415925, op0=ALU.max, op1=ALU.min), reads=[], writes=[b_r2])
        S.op("act", lambda e: e.activation(out=dst3, in_=r2[:, :].rearrange("p (t f) -> p t f", f=32), func=AF.Sin), reads=[b_r2], writes=[b_sin, b_cos])
    reduce_sin(sinT[:, :, :], 0.0)
    reduce_sin(cosT[:, :, :], math.pi / 2)

    S.flush_block(nc)
    SU.close()

    P1 = ExitStack()
    kb2 = [sb(P1, "kb%d" % i, [128, 512]) for i in range(2)]
    cb2 = [sb(P1, "cb%d" % i, [128, 256]) for i in range(2)]
    ckvnT2 = [sb(P1, "ckvnT%d" % i, [128, 2, 128], BF16) for i in range(2)]
    qb = sb(P1, "qb", [128, 512])
    sq = sb(P1, "sq", [128, 768])
    kn = sb(P1, "kn", [128, 8, 64], BF16)
    ckvn = sb(P1, "ckvn", [128, 256], BF16)
    ckvn_f = sb(P1, "ckvn_f", [128, 128], BF16)
    kpe = sb(P1, "kpe", [128, 4, 64])
    rt = sb(P1, "rt", [128, 4, 4, 32])
    km = sb(P1, "km", [128, 4, 192], BF16)
    qm = sb(P1, "qm", [128, 4, 192])
    st2 = sb(P1, "st2", [128, 96])
    stgKd = sb(P1, "stgKd", [64, 8, ST * 128], BF16)
    stgKn = sb(P1, "stgKn", [128, 4, ST * 128], BF16)
    stgKp = sb(P1, "stgKp", [64, 4, ST * 128], BF16)
    stgVd = sb(P1, "stgVd", [128, 2 * ST, 4, VW], BF16)
    stgVm = sb(P1, "stgVm", [128, ST, 4, VW], BF16)
    b_kb2, b_cb2, b_ckvnT2 = [Buf("kb0"), Buf("kb1")], [Buf("cb0"), Buf("cb1")], [Buf("ckT0"), Buf("ckT1")]
    b_qb, b_sq, b_kn, b_ckvn, b_kpe, b_rt, b_km, b_qm = [Buf(n) for n in ("qb", "sq", "kn", "ckvn", "kpe", "rt", "km", "qm")]
    b_ckvn_f = Buf("ckvn_f")
    b_stK, b_stC, b_stM, b_stQ, b_stQc, b_stQm = [Buf(n) for n in ("stK", "stC", "stM", "stQ", "stQc", "stQm")]
    b_sKd, b_sKn, b_sKp, b_sVd, b_sVm = [Buf(n) for n in ("sKd", "sKn", "sKp", "sVd", "sVm")]
    sem_sKd, sem_sKn, sem_sKp, sem_sVd, sem_sVm = [dsem(n) for n in ("sKd", "sKn", "sKp", "sVd", "sVm")]
    b_scr = Buf("scratch")
    S.op("pool", lambda e: e.memset(stgVd[:, :, :, :], 0.0), writes=[b_sVd])
    S.op("pool", lambda e: e.memset(stgVm[:, :, :, :], 0.0), writes=[b_sVm])
    S.op("pool", lambda e: e.memset(stgVd[:, :, :, 128:129], 1.0), writes=[b_sVd])
    S.op("pool", lambda e: e.memset(stgVm[:, :, :, 128:129], 1.0), writes=[b_sVm])

    def group_norm(src_ap, ngroups, gsz, base, b_src, b_st):
        n = ngroups * gsz
        S.op("pool", lambda e: e.tensor_tensor(out=sq[:, 0:n], in0=src_ap, in1=src_ap, op=ALU.mult), reads=[b_src], writes=[b_sq])
        S.op("dve", lambda e: e.tensor_reduce(out=st2[:, base:base + ngroups], in_=sq[:, 0:n].rearrange("p (g d) -> p g d", g=ngroups), axis=AX.X, op=ALU.add),
             reads=[b_sq], writes=[b_st])
        rstd_act(st2[:, base + 16:base + 16 + ngroups], st2[:, base:base + ngroups], gsz, st2[:, base + 8:base + 8 + ngroups], [b_st], [b_st], b_st)

    def rope(dst3, src3, nh, t, b_src, eng):
        cs = cosT[:, t, :].unsqueeze(1).broadcast_to([128, nh, 32])
        sn = sinT[:, t, :].unsqueeze(1).broadcast_to([128, nh, 32])
        x1, x2 = src3[:, :, 0:32], src3[:, :, 32:64]
        S.op(eng, lambda e: e.tensor_tensor(out=rt[:, 0:nh, 0, :], in0=x1, in1=cs, op=ALU.mult), reads=[b_cos, b_src], writes=[b_rt])
        S.op(eng, lambda e: e.tensor_tensor(out=rt[:, 0:nh, 1, :], in0=x2, in1=sn, op=ALU.mult), reads=[b_sin, b_src], writes=[b_rt])
        S.op(eng, lambda e: e.tensor_tensor(out=rt[:, 0:nh, 2, :], in0=x2, in1=cs, op=ALU.mult), reads=[b_cos, b_src], writes=[b_rt])
        S.op(eng, lambda e: e.tensor_tensor(out=rt[:, 0:nh, 3, :], in0=x1, in1=sn, op=ALU.mult), reads=[b_sin, b_src], writes=[b_rt])
        S.op(eng, lambda e: e.tensor_tensor(out=dst3[:, :, 0:32], in0=rt[:, 0:nh, 0, :], in1=rt[:, 0:nh, 1, :], op=ALU.subtract), reads=[b_rt], writes=[b_rt])
        S.op(eng, lambda e: e.tensor_tensor(out=dst3[:, :, 32:64], in0=rt[:, 0:nh, 2, :], in1=rt[:, 0:nh, 3, :], op=ALU.add), reads=[b_rt], writes=[b_rt])

    def trk(e):
        ins = None
        for g in range(8):
            ins = e.transpose(out=ps_bf(6)[0:64, g * 128:(g + 1) * 128], in_=kn[:, g, :], identity=ident_b[:, :])
        return ins

    def trm(e):
        ins = None
        for h in range(4):
            e.transpose(out=ps_bf(7)[:, h * 128:(h + 1) * 128], in_=km[:, h, 0:128], identity=ident_b[:, :])
            ins = e.transpose(out=ps_bf(7)[0:64, 512 + h * 128:512 + (h + 1) * 128], in_=km[:, h, 128:192], identity=ident_b[:, :])
        return ins

    NDUM = NCORES - 1

    def front1(t):
        p = t % 2
        nxt = (lambda: load_x(x_pad[(t + 2) * 128:(t + 3) * 128, :], p)) if t + 2 < NT else None
        norm_transpose_x(None, p, load=False, after_read=nxt, tslot=t % 3)

    def front(t):
        p = t % 2
        slot = t % ST
        vslot = t % (2 * ST)
        kb, cb, ckvnT, b_kb, b_cb, b_ckvnT = kb2[p], cb2[p], ckvnT2[p], b_kb2[p], b_cb2[p], b_ckvnT2[p]
        proj(1, 0, 512, Wa, b_Wa, C_DK, t % 3)
        S.op("dve", lambda e: e.tensor_tensor(out=kb[:, :], in0=PS[1][:, :], in1=bWa[:, C_DK:C_DK + 512], op=ALU.add), reads=[PB[1], b_bWa], writes=[b_kb])
        proj(2, 0, 512, Wa, b_Wa, C_DV, t % 3)
        S.op("dve", lambda e: e.tensor_tensor(out=stgVd[:, vslot, :, 0:128], in0=PS[2][:, :].rearrange("p (h d) -> p h d", h=4),
                                              in1=bWa[:, C_DV:C_DV + 512].rearrange("p (h d) -> p h d", h=4), op=ALU.add),
             reads=[PB[2], b_bWa], writes=[b_sVd])
        if t < NDUM:
            S.op("dve", lambda e: e.tensor_scalar(out=stgVd[:, vslot, :, 0:128], in0=stgVd[:, vslot, :, 0:128], scalar1=valid[:, t:t + 1], scalar2=None, op0=ALU.mult),
                 reads=[b_valid], writes=[b_sVd])
            S.op("dve", lambda e: e.tensor_copy(out=stgVd[:, vslot, :, 128:129], in_=valid[:, t:t + 1].unsqueeze(1).broadcast_to([128, 4, 1])),
                 reads=[b_valid], writes=[b_sVd])
        elif t < NDUM + 2 * ST:
            S.op("dve", lambda e: e.memset(stgVd[:, vslot, :, 128:129], 1.0), writes=[b_sVd])
        proj(3, 0, 192, Wa, b_Wa, C_CKV, t % 3)
        S.op("dve", lambda e: e.tensor_tensor(out=cb[:, 0:192], in0=PS[3][:, 0:192], in1=bWa[:, C_CKV:C_CKV + 192], op=ALU.add), reads=[PB[3], b_bWa], writes=[b_cb])
        S.op("act", lambda e: e.activation(out=junk[:, 0:128], in_=cb[:, 0:128], func=AF.Square, accum_out=st2[:, 24:25]), reads=[b_cb], writes=[b_junk, b_stC])
        rstd_act(st2[:, 26:27], st2[:, 24:25], 128, st2[:, 25:26], [b_stC], [b_stC], b_stC)
        S.op("act", lambda e: e.activation(out=ckvn_f[:, 0:128], in_=cb[:, 0:128], func=AF.Copy, scale=st2[:, 26:27]), reads=[b_cb, b_stC], writes=[b_ckvn_f])
        S.op("pe", lambda e: e.transpose(out=ps_bf(3)[:, 512:640], in_=ckvn_f[:, 0:128], identity=ident_b[:, :]), reads=[b_ckvn_f, b_identb], writes=[PB[3]])
        S.op("dve", lambda e: e.tensor_copy(out=ckvnT[:, 0, :], in_=ps_bf(3)[:, 512:640]), reads=[PB[3]], writes=[b_ckvnT])

    def back_k(t):
        p = t % 2
        slot = t % ST
        stile = t // ST
        kb, cb, ckvnT, b_kb, b_cb, b_ckvnT = kb2[p], cb2[p], ckvnT2[p], b_kb2[p], b_cb2[p], b_ckvnT2[p]
        group_norm(kb[:, :], 8, 64, 0, b_kb, b_stK)
        S.op("pool", lambda e: e.tensor_tensor(out=kn[:, :, :], in0=kb[:, :].rearrange("p (g d) -> p g d", g=8),
                                               in1=st2[:, 16:24].unsqueeze(2).broadcast_to([128, 8, 64]), op=ALU.mult),
             reads=[b_kb, b_stK], writes=[b_kn])
        S.op("pe", trk, reads=[b_kn, b_identb], writes=[PB[6]])
        S.op("dve", lambda e: e.tensor_copy(out=stgKd[:, :, slot * 128:(slot + 1) * 128], in_=ps_bf(6)[0:64, :].rearrange("p (g t) -> p g t", g=8)),
             reads=[PB[6]], writes=[b_sKd])

    def back_m(t):
        p = t % 2
        slot = t % ST
        stile = t // ST
        kb, cb, ckvnT, b_kb, b_cb, b_ckvnT = kb2[p], cb2[p], ckvnT2[p], b_kb2[p], b_cb2[p], b_ckvnT2[p]
        S.op("pe", lambda e: e.matmul(PS[4][:, :], lhsT=ckvnT[:, 0, :], rhs=wuk_b[:, :], start=True, stop=True), reads=[b_ckvnT, b_wuk], writes=[PB[4]])
        S.op("pe", lambda e: e.matmul(PS[5][:, :], lhsT=ckvnT[:, 0, :], rhs=wuv_b[:, :], start=True, stop=True), reads=[b_ckvnT, b_wuv], writes=[PB[5]])
        if t < NDUM:
            S.op("act", lambda e: e.activation(out=stgVm[:, slot, :, 0:128], in_=PS[5][:, :].rearrange("p (h d) -> p h d", h=4),
                                               func=AF.Copy, scale=valid[:, t:t + 1]),
                 reads=[PB[5], b_valid], writes=[b_sVm])
            S.op("dve", lambda e: e.tensor_copy(out=stgVm[:, slot, :, 128:129], in_=valid[:, t:t + 1].unsqueeze(1).broadcast_to([128, 4, 1])),
                 reads=[b_valid], writes=[b_sVm])
        else:
            S.op("act", lambda e: e.activation(out=stgVm[:, slot, :, 0:128], in_=PS[5][:, :].rearrange("p (h d) -> p h d", h=4), func=AF.Copy),
                 reads=[PB[5]], writes=[b_sVm])
            if t < NDUM + ST:
                S.op("dve", lambda e: e.memset(stgVm[:, slot, :, 128:129], 1.0), writes=[b_sVm])
        rope(kpe[:, 0:1, :], cb[:, 128:192].unsqueeze(1), 1, t, b_cb, "dve")
        S.op("act", lambda e: e.activation(out=junk[:, 0:64], in_=kpe[:, 0, :], func=AF.Square, accum_out=st2[:, 27:28]), reads=[b_rt], writes=[b_junk, b_stM])
        for h in range(4):
            S.op("act", lambda e, h=h: e.activation(out=junk[:, 0:128], in_=PS[4][:, h * 128:(h + 1) * 128], func=AF.Square, accum_out=st2[:, 28 + h:29 + h]),
                 reads=[PB[4]], writes=[b_junk, b_stM])
        S.op("dve", lambda e: e.tensor_scalar(out=st2[:, 32:36], in0=st2[:, 28:32], scalar1=st2[:, 27:28], scalar2=None, op0=ALU.add), reads=[b_stM], writes=[b_stM])
        rstd_act(st2[:, 40:44], st2[:, 32:36], 192, st2[:, 36:40], [b_stM], [b_stM], b_stM)
        S.op("dve", lambda e: e.tensor_tensor(out=km[:, :, 0:128], in0=PS[4][:, :].rearrange("p (h d) -> p h d", h=4),
                                              in1=st2[:, 40:44].unsqueeze(2).broadcast_to([128, 4, 128]), op=ALU.mult),
             reads=[PB[4], b_stM], writes=[b_km])
        S.op("dve", lambda e: e.tensor_tensor(out=km[:, :, 128:192], in0=kpe[:, 0:1, :].broadcast_to([128, 4, 64]),
                                              in1=st2[:, 40:44].unsqueeze(2).broadcast_to([128, 4, 64]), op=ALU.mult),
             reads=[b_rt, b_stM], writes=[b_km])
        S.op("pe", trm, reads=[b_km, b_identb], writes=[PB[7]])
        S.op("dve", lambda e: e.tensor_copy(out=stgKn[:, :, slot * 128:(slot + 1) * 128], in_=ps_bf(7)[:, 0:512].rearrange("p (h t) -> p h t", h=4)),
             reads=[PB[7]], writes=[b_sKn])
        S.op("act", lambda e: e.activation(out=stgKp[:, :, slot * 128:(slot + 1) * 128], in_=ps_bf(7)[0:64, 512:1024].rearrange("p (h t) -> p h t", h=4), func=AF.Copy),
             reads=[PB[7]], writes=[b_sKp])


    def back_q(t):
        p = t % 2
        slot = t % ST
        stile = t // ST
        kb, cb, ckvnT, b_kb, b_cb, b_ckvnT = kb2[p], cb2[p], ckvnT2[p], b_kb2[p], b_cb2[p], b_ckvnT2[p]
        if t % NCORES == NCORES - 1:
            j = t // NCORES
            qc = slice(j * 128, (j + 1) * 128)
            proj(4, 0, 512, Wa, b_Wa, C_DQ, t % 3)
            S.op("dve", lambda e: e.tensor_tensor(out=qb[:, :], in0=PS[4][:, :], in1=bWa[:, C_DQ:C_DQ + 512], op=ALU.add), reads=[PB[4], b_bWa], writes=[b_qb])
            group_norm(qb[:, :], 8, 64, 48, b_qb, b_stQ)
            S.op("pool", lambda e: e.tensor_tensor(out=qb[:, :].rearrange("p (g d) -> p g d", g=8), in0=qb[:, :].rearrange("p (g d) -> p g d", g=8),
                                                   in1=st2[:, 64:72].unsqueeze(2).broadcast_to([128, 8, 64]), op=ALU.mult),
                 reads=[b_stQ], writes=[b_qb])
            S.op("pool", lambda e: e.tensor_tensor(out=kn[:, :, :], in0=qb[:, :].rearrange("p (g d) -> p g d", g=8),
                                                   in1=gqk_d[:, :].unsqueeze(1).broadcast_to([128, 8, 64]), op=ALU.mult),
                 reads=[b_qb, b_gqkd], writes=[b_kn])
            S.op("pe", trk, reads=[b_kn, b_identb], writes=[PB[6]])
            S.op("dve", lambda e: e.tensor_copy(out=QdT[0:64, :, qc], in_=ps_bf(6)[0:64, :].rearrange("p (g t) -> p g t", g=8)),
                 reads=[PB[6]], writes=[b_QdT])
            proj(5, 0, 256, Wa, b_Wa, C_CQ, t % 3)
            S.op("dve", lambda e: e.tensor_tensor(out=qb[:, 0:256], in0=PS[5][:, 0:256], in1=bWa[:, C_CQ:C_CQ + 256], op=ALU.add), reads=[PB[5], b_bWa], writes=[b_qb])
            S.op("act", lambda e: e.activation(out=junk[:, 0:256], in_=qb[:, 0:256], func=AF.Square, accum_out=st2[:, 72:73]), reads=[b_qb], writes=[b_junk, b_stQc])
            rstd_act(st2[:, 74:75], st2[:, 72:73], 256, st2[:, 73:74], [b_stQc], [b_stQc], b_stQc)
            S.op("act", lambda e: e.activation(out=ckvn[:, :], in_=qb[:, 0:256], func=AF.Copy, scale=st2[:, 74:75]), reads=[b_qb, b_stQc], writes=[b_ckvn])

            def trq(e):
                e.transpose(out=ps_bf(6)[:, 0:128], in_=ckvn[:, 0:128], identity=ident_b[:, :])
                return e.transpose(out=ps_bf(6)[:, 128:256], in_=ckvn[:, 128:256], identity=ident_b[:, :])
            S.op("pe", trq, reads=[b_ckvn, b_identb], writes=[PB[6]])
            S.op("dve", lambda e: e.tensor_copy(out=ckvnT[:, :, :], in_=ps_bf(6)[:, 0:256].rearrange("p (k t) -> p k t", k=2)), reads=[PB[6]], writes=[b_ckvnT])

            def mq(e):
                e.matmul(PS[4][:, :], lhsT=ckvnT[:, 0, :], rhs=wuq_b[:, 0, 0:512], start=True, stop=False)
                e.matmul(PS[4][:, :], lhsT=ckvnT[:, 1, :], rhs=wuq_b[:, 1, 0:512], start=False, stop=True)
                e.matmul(PS[5][:, 0:256], lhsT=ckvnT[:, 0, :], rhs=wuq_b[:, 0, 512:768], start=True, stop=False)
                return e.matmul(PS[5][:, 0:256], lhsT=ckvnT[:, 1, :], rhs=wuq_b[:, 1, 512:768], start=False, stop=True)
            S.op("pe", mq, reads=[b_ckvnT, b_wuq], writes=[PB[4], PB[5]])
            qmf = qm[:, :, :].rearrange("p h d -> p (h d)")
            S.op("dve", lambda e: e.tensor_copy(out=qmf[:, 0:512], in_=PS[4][:, :]), reads=[PB[4]], writes=[b_qm])
            S.op("dve", lambda e: e.tensor_copy(out=qmf[:, 512:768], in_=PS[5][:, 0:256]), reads=[PB[5]], writes=[b_qm])
            S.op("dve", lambda e: e.tensor_copy(out=kpe[:, :, :], in_=qm[:, :, 128:192]), reads=[b_qm], writes=[b_kpe, b_rt])
            rope(qm[:, :, 128:192], kpe[:, :, :], 4, t, b_kpe, "dve")
            S.op("pool", lambda e: e.tensor_tensor(out=sq[:, 0:768], in0=qmf, in1=qmf, op=ALU.mult), reads=[b_qm, b_rt], writes=[b_sq, b_qm])
            S.op("dve", lambda e: e.tensor_reduce(out=st2[:, 76:80], in_=sq[:, 0:768].rearrange("p (g d) -> p g d", g=4), axis=AX.X, op=ALU.add),
                 reads=[b_sq], writes=[b_stQm])
            rstd_act(st2[:, 84:88], st2[:, 76:80], 192, st2[:, 80:84], [b_stQm], [b_stQm], b_stQm)
            S.op("pool", lambda e: e.tensor_tensor(out=qm[:, :, :], in0=qm[:, :, :], in1=st2[:, 84:88].unsqueeze(2).broadcast_to([128, 4, 192]), op=ALU.mult),
                 reads=[b_stQm], writes=[b_qm])
            S.op("pool", lambda e: e.tensor_tensor(out=km[:, :, :], in0=qm[:, :, :], in1=gqk_m[:, :].unsqueeze(1).broadcast_to([128, 4, 192]), op=ALU.mult),
                 reads=[b_qm, b_gqkm], writes=[b_km])
            S.op("pe", trm, reads=[b_km, b_identb], writes=[PB[7]])
            S.op("dve", lambda e: e.tensor_copy(out=QmTn[:, :, qc], in_=ps_bf(7)[:, 0:512].rearrange("p (h t) -> p h t", h=4)),
                 reads=[PB[7]], writes=[b_QmTn])
            S.op("act", lambda e: e.activation(out=QmTp[0:64, :, qc], in_=ps_bf(7)[0:64, 512:1024].rearrange("p (h t) -> p h t", h=4), func=AF.Copy),
                 reads=[PB[7]], writes=[b_QmTp])


    def back_s(t):
        p = t % 2
        slot = t % ST
        stile = t // ST
        kb, cb, ckvnT, b_kb, b_cb, b_ckvnT = kb2[p], cb2[p], ckvnT2[p], b_kb2[p], b_cb2[p], b_ckvnT2[p]
        if slot == ST - 1:
            c0, c1 = stile * ST * 128, (stile + 1) * ST * 128
            dma("sp", KdT_d[:, :, c0:c1].rearrange("g d t -> d g t"), stgKd[:, :, :], sem_sKd, reads=[b_sKd])
            dma("sp", KmTn_d[:, :, c0:c1].rearrange("h d t -> d h t"), stgKn[:, :, :], sem_sKn, reads=[b_sKn])
            dma("sp", KmTp_d[:, :, c0:c1].rearrange("h d t -> d h t"), stgKp[:, :, :], sem_sKp, reads=[b_sKp])
            for h in range(4):
                dma("sp", Vd_d[h, :, stile * ST:(stile + 1) * ST, :], stgVd[:, (stile % 2) * ST:(stile % 2 + 1) * ST, h, :], sem_sVd, reads=[b_sVd])
                dma("sp", Vm_d[h, :, stile * ST:(stile + 1) * ST, :], stgVm[:, :, h, :], sem_sVm, reads=[b_sVm])

    load_x(x_pad[0:128, :], 0)
    load_x(x_pad[128:256, :], 1)
    front1(0)
    if NT > 1:
        front1(1)
    front(0)
    for t in range(NT):
        f1 = S.record(lambda: front1(t + 2)) if t + 2 < NT else []
        f2 = S.record(lambda: front(t + 1)) if t + 1 < NT else []
        bk = S.record(lambda: back_k(t))
        bm = S.record(lambda: back_m(t))
        S.replay_merged(f1, f2, bk, bm)
        if t % NCORES == NCORES - 1:
            back_q(t)
        back_s(t)

    S.flush_block(nc)
    P1.close()
    S12.close()

    P3 = ExitStack()
    NKB = 3
    kc = [sb(P3, "kc%d" % i, [128, 2 * CH * 128], BF16) for i in range(NKB)]
    vc = [sb(P3, "vc%d" % i, [128, CH, VW], BF16) for i in range(NKB)]
    b_kc = [Buf("kc%d" % i) for i in range(NKB)]
    sem_kc = [dsem("kc%d" % i) for i in range(NKB)]
    NPB = 4
    Pt = [sb(P3, "Pt%d" % i, [128, GQ * 128], BF16) for i in range(NPB)]
    b_Pt = [Buf("Pt%d" % i) for i in range(NPB)]
    btmp = [sb(P3, "btmp%d" % i, [128, 128]) for i in range(2)]
    b_btmp = [Buf("btmp0"), Buf("btmp1")]
    mixed = sb(P3, "mixed", [128, GQ, D])
    b_mixed = Buf("mixed")
    o0 = sb(P3, "o0", [128, 128])
    od = sb(P3, "od", [128, 128])
    st3 = sb(P3, "st3", [128, 16])
    st4 = sb(P3, "st4", [128, 8])
    b_st4 = Buf("st4")
    gb = sb(P3, "gb", [128, D])
    mg = sb(P3, "mg", [128, D], BF16)
    mgT = sb(P3, "mgT", [128, KC, 128], BF16)
    yt = sb(P3, "yt", [128, D])
    b_o0, b_od, b_st3, b_gb, b_mg, b_mgT, b_yt = [Buf(n) for n in ("o0", "od", "st3", "gb", "mg", "mgT", "yt")]
    sem_y = dsem("y")
    wout_b = sb(P3, "wout_b", [128, KC, 1024], BF16)
    b_wout = Buf("wout", True)
    Wg = sb(P3, "Wg", [128, KC, 1024], BF16)
    bWg = sb(P3, "bWg", [128, 1024])
    for k in range(KC):
        sl = k % 2
        dma("sp", xt[sl][:, :], w_in[k * 128:(k + 1) * 128, NA:INC], sem_xt[sl], writes=[b_xt[sl]])

        def mmg(e, k=k, sl=sl):
            e.matmul(PS[0][:, :], lhsT=shT[:, k, :], rhs=xt[sl][:, 0:512], start=(k == 0), stop=(k == KC - 1))
            return e.matmul(PS[1][:, :], lhsT=shT[:, k, :], rhs=xt[sl][:, 512:1024], start=(k == 0), stop=(k == KC - 1))
        S.op("pe", mmg, reads=[b_shT, b_xt[sl]], writes=PB[0:2])
        S.op("act", lambda e, k=k, sl=sl: e.activation(out=Wg[:, k, :], in_=xt[sl][:, :], func=AF.Copy, scale=A_col[:, k:k + 1]),
             reads=[b_xt[sl], b_Acol], writes=[b_Wg])
    for hf in range(2):
        S.op("dve", lambda e, hf=hf: e.tensor_copy(out=bWg[:, hf * 512:(hf + 1) * 512], in_=PS[hf][:, :]), reads=[PB[hf]], writes=[b_bWg])
    for k in range(KC):
        sl = k % 2
        dma("sp", xt[sl][:, :], w_out[k * 128:(k + 1) * 128, :], sem_xt[sl], writes=[b_xt[sl]])
        if k < 4:
            S.op("act", lambda e, k=k, sl=sl: e.activation(out=wout_b[:, k, :], in_=xt[sl][:, :], func=AF.Copy, scale=wsc[:, 0:1]),
                 reads=[b_xt[sl], b_wsc], writes=[b_wout])
        else:
            S.op("act", lambda e, k=k, sl=sl: e.activation(out=wout_b[:, k, :], in_=xt[sl][:, :], func=AF.Copy),
                 reads=[b_xt[sl]], writes=[b_wout])

    def acc_ap(jj, c):
        return PS[4 + jj][:, c * 132:c * 132 + 129]

    units, chunks = [], []
    for g in range(NG):
        jlo, jhi = g * GQ, (g + 1) * GQ
        kt_end = NCORES * (jhi - 1) + NCORES
        for u in range(8):
            is_d = u < 4
            h = u % 4
            nmap = 2 if is_d else 1
            nch = (kt_end + CH - 1) // CH
            for ci in range(nch):
                k0 = ci * CH
                nk = min(CH, kt_end - k0)
                q = len(chunks)
                chunks.append(dict(is_d=is_d, h=h, k0=k0, nk=nk, first=len(units)))
                for kl in range(nk):
                    kt = k0 + kl
                    jmin = max(jlo, (kt - (NCORES - 1) + NCORES - 1) // NCORES)
                    for c in range(nmap):
                        units.append(dict(g=g, jlo=jlo, jhi=jhi, is_d=is_d, h=h, nmap=nmap, q=q, kl=kl, kt=kt, c=c, jmin=jmin,
                                          fin=(c == nmap - 1 and kt == NCORES * jmin + NCORES - 1), last_of_group=False))
                chunks[q]["last"] = len(units) - 1
        units[-1]["last_of_group"] = True
    n_units = len(units)
    LAG = 2
    FILL = 0.0
    WARM = 16
    WARM_F = 8
    DEFER = 4
    pending = []
    load_at = {}
    for q, chd in enumerate(chunks):
        it = 0 if q < NKB else chunks[q - NKB]["last"] + LAG + 1
        assert it <= chd["first"], (q, it, chd)
        load_at.setdefault(it, []).append(q)

    def emit_load(q):
        chd = chunks[q]
        cs, h, k0, nk = q % NKB, chd["h"], chd["k0"], chd["nk"]
        if chd["is_d"]:
            dma("sp", kc[cs][0:64, :].rearrange("p (c t) -> p c t", c=2)[:, :, 0:nk * 128],
                KdT_d[2 * h:2 * h + 2, :, k0 * 128:(k0 + nk) * 128].rearrange("c d t -> d c t"),
                sem_kc[cs], reads=[b_scr], writes=[b_kc[cs]])
            dma("sp", vc[cs][:, 0:nk, :], Vd_d[h, :, k0:k0 + nk, :], sem_kc[cs], reads=[b_scr], writes=[b_kc[cs]])
        else:
            dma("sp", kc[cs][:, 0:nk * 128], KmTn_d[h, :, k0 * 128:(k0 + nk) * 128], sem_kc[cs], reads=[b_scr], writes=[b_kc[cs]])
            dma("sp", kc[cs][0:64, CH * 128:CH * 128 + nk * 128], KmTp_d[h, :, k0 * 128:(k0 + nk) * 128], sem_kc[cs], reads=[b_scr], writes=[b_kc[cs]])
            dma("sp", vc[cs][:, 0:nk, :], Vm_d[h, :, k0:k0 + nk, :], sem_kc[cs], reads=[b_scr], writes=[b_kc[cs]])

    def warm_burst(nburst):
        def burst(e):
            ins = None
            for i in range(nburst):
                ins = e.matmul(PS[3][:, :], lhsT=ident_b[:, :], rhs=wout_b[:, i % KC, 0:512], start=True, stop=True)
            return ins
        S.op("pe", burst, reads=[b_identb, b_wout], writes=[PB[3]])

    def stage_ab(ud, it):
        if WARM and ud["kt"] == 0 and ud["c"] == 0:
            warm_burst(WARM)
        cs, h, c, kt, kl, jmin, jhi, is_d = ud["q"] % NKB, ud["h"], ud["c"], ud["kt"], ud["kl"], ud["jmin"], ud["jhi"], ud["is_d"]
        N = (jhi - jmin) * 128
        qcols = slice(jmin * 128, jhi * 128)
        kcols = slice(kl * 128, (kl + 1) * 128)
        diag_kt = NCORES * jmin + NCORES - 1
        near = kt in (diag_kt, diag_kt - 1)
        ty = 0 if kt == diag_kt else 1
        sb_i = it % 3
        pb_i = it % NPB
        if is_d:
            S.op("pe", lambda e: e.matmul(PS[sb_i][:, 0:N], lhsT=kc[cs][0:64, c * CH * 128 + kl * 128:c * CH * 128 + (kl + 1) * 128],
                                          rhs=QdT[0:64, 2 * h + c, qcols], start=True, stop=True),
                 reads=[b_kc[cs], b_QdT], writes=[PB[sb_i]])
        else:
            def mmm(e):
                e.matmul(PS[sb_i][:, 0:N], lhsT=kc[cs][:, kcols], rhs=QmTn[:, h, qcols], start=True, stop=False)
                return e.matmul(PS[sb_i][:, 0:N], lhsT=kc[cs][0:64, CH * 128 + kl * 128:CH * 128 + (kl + 1) * 128], rhs=QmTp[0:64, h, qcols], start=False, stop=True)
            S.op("pe", mmm, reads=[b_kc[cs], b_QmTn, b_QmTp], writes=[PB[sb_i]])
        if near and (is_d or ty == 0):
            bi = it % 2
            bias_ap = Bd[:, ty, h, :] if is_d else Bm[:, :]
            S.op("dve", lambda e: e.tensor_tensor(out=btmp[bi][:, :], in0=PS[sb_i][:, 0:128], in1=bias_ap, op=ALU.add),
                 reads=[PB[sb_i], b_Bd, b_Bm], writes=[b_btmp[bi]])
            S.op("act", lambda e: e.activation(out=Pt[pb_i][:, 0:128], in_=btmp[bi][:, :], func=AF.Exp),
                 reads=[b_btmp[bi]], writes=[b_Pt[pb_i]])
            if N > 128:
                S.op("act", lambda e: e.activation(out=Pt[pb_i][:, 128:N], in_=PS[sb_i][:, 128:N], func=AF.Exp),
                     reads=[PB[sb_i]], writes=[b_Pt[pb_i]])
        else:
            S.op("act", lambda e: e.activation(out=Pt[pb_i][:, 0:N], in_=PS[sb_i][:, 0:N], func=AF.Exp),
                 reads=[PB[sb_i]], writes=[b_Pt[pb_i]])

    def stage_c(ud, it):
        cs, h, c, kt, kl, jmin, jhi, jlo, is_d, nmap = (ud["q"] % NKB, ud["h"], ud["c"], ud["kt"], ud["kl"], ud["jmin"], ud["jhi"],
                                                        ud["jlo"], ud["is_d"], ud["nmap"])
        nq = jhi - jmin
        pb_i = it % NPB
        first = (kt == 0)
        flags = [(jmin + i - jlo, first and c == 0, kt == NCORES * (jmin + i) + NCORES - 1) for i in range(nq)]

        def pv(e):
            ins = None
            for i, (jj_, st_f, last) in enumerate(flags):
                ins = e.matmul(acc_ap(jj_, c), lhsT=Pt[pb_i][:, i * 128:(i + 1) * 128], rhs=vc[cs][:, kl, 0:129],
                               start=st_f, stop=last, skip_group_check=True)
            return ins
        accs_w = []
        if first:
            accs_w = [PB[4 + jj_] for jj_ in range(GQ)]
        if kt == NCORES * jmin + NCORES - 1 and PB[4 + jmin - jlo] not in accs_w:
            accs_w = accs_w + [PB[4 + jmin - jlo]]
        S.op("pe", pv, reads=[b_Pt[pb_i], b_kc[cs]], writes=accs_w)
        if FILL:
            N_ = nq * 128
            act_ns = (224 + N_) / 1.4
            pe_ns = (1 if is_d else 2) * max(N_ / 2.4, 60.0) + nq * 57.0
            nf = int(min(512, (act_ns - pe_ns) * 2.4 * FILL))
            if nf >= 64:
                S.op("pe", lambda e: e.matmul(PS[3][:, 0:nf], lhsT=ident_b[:, :], rhs=wout_b[:, 0, 0:nf], start=True, stop=True),
                     reads=[b_identb, b_wout], writes=[PB[3]])
        if ud["fin"]:
            if WARM_F:
                warm_burst(WARM_F)
            jj = jmin - jlo
            mcol = (0 if is_d else 512) + h * 128
            if is_d:
                a0, a1 = acc_ap(jj, 0), acc_ap(jj, 1)
                S.op("dve", lambda e: e.reciprocal(out=st3[:, 0:1], in_=a0[:, 128:129]), reads=[PB[4 + jj]], writes=[b_st3])
                S.op("dve", lambda e: e.reciprocal(out=st3[:, 1:2], in_=a1[:, 128:129]), reads=[PB[4 + jj]], writes=[b_st3])
                S.op("dve", lambda e: e.tensor_tensor(out=st3[:, 2:3], in0=st3[:, 1:2], in1=nlam[:, :], op=ALU.mult), reads=[b_st3, b_nlam], writes=[b_st3])
                S.op("dve", lambda e: e.tensor_scalar(out=o0[:, :], in0=a0[:, 0:128], scalar1=st3[:, 0:1], scalar2=None, op0=ALU.mult),
                     reads=[PB[4 + jj], b_st3], writes=[b_o0])
                S.op("dve", lambda e: e.scalar_tensor_tensor(out=od[:, :], in0=a1[:, 0:128], scalar=st3[:, 2:3], in1=o0[:, :], op0=ALU.mult, op1=ALU.add),
                     reads=[PB[4 + jj], b_st3, b_o0], writes=[b_od])
                def fin_act():
                    S.op("act", lambda e: e.activation(out=junk[:, 0:128], in_=od[:, :], func=AF.Square, accum_out=st4[:, 3:4]), reads=[b_od], writes=[b_junk, b_st4])
                    rstd_act(st4[:, 5:6], st4[:, 3:4], 128, st4[:, 4:5], [b_st4], [b_st4], b_st4)
                    S.op("act", lambda e: e.activation(out=mixed[:, jj, mcol:mcol + 128], in_=od[:, :], func=AF.Copy, scale=st4[:, 5:6]),
                         reads=[b_od, b_st4], writes=[b_mixed])
                pending.append([DEFER, fin_act])
            else:
                a0 = acc_ap(jj, 0)
                S.op("dve", lambda e: e.reciprocal(out=st3[:, 8:9], in_=a0[:, 128:129]), reads=[PB[4 + jj]], writes=[b_st3])
                S.op("dve", lambda e: e.tensor_scalar(out=mixed[:, jj, mcol:mcol + 128], in0=a0[:, 0:128], scalar1=st3[:, 8:9], scalar2=None, op0=ALU.mult),
                     reads=[PB[4 + jj], b_st3], writes=[b_mixed])

    def epilogue(g):
        jlo = g * GQ
        for jj in range(GQ):
            j = jlo + jj
            sl = j % 2
            norm_transpose_x(x_own[j * 128:(j + 1) * 128, :], sl)
            proj(0, 0, 512, Wg, b_Wg, 0, sl)
            proj(1, 0, 512, Wg, b_Wg, 512, sl)
            for hf in range(2):
                S.op("dve", lambda e, hf=hf: e.tensor_tensor(out=gb[:, hf * 512:(hf + 1) * 512], in0=PS[hf][:, :], in1=bWg[:, hf * 512:(hf + 1) * 512], op=ALU.add),
                     reads=[PB[hf], b_bWg], writes=[b_gb])
            S.op("act", lambda e: e.activation(out=gb[:, :], in_=gb[:, :], func=AF.Silu), reads=[], writes=[b_gb])
            S.op("pool", lambda e, jj=jj: e.tensor_tensor(out=mg[:, :], in0=gb[:, :], in1=mixed[:, jj, :], op=ALU.mult), reads=[b_gb, b_mixed], writes=[b_mg])

            def trg(e):
                ins = None
                for k in range(KC):
                    ins = e.transpose(out=ps_bf(7)[:, k * 128:(k + 1) * 128], in_=mg[:, k * 128:(k + 1) * 128], identity=ident_b[:, :])
                return ins
            S.op("pe", trg, reads=[b_mg, b_identb], writes=[PB[7]])
            S.op("dve", lambda e: e.tensor_copy(out=mgT[:, :, :].rearrange("p k t -> p (k t)"), in_=ps_bf(7)[:, :]), reads=[PB[7]], writes=[b_mgT])
            for hf in range(2):
                def mo(e, hf=hf):
                    ins = None
                    for k in range(KC):
                        ins = e.matmul(PS[2 + hf][:, :], lhsT=mgT[:, k, :], rhs=wout_b[:, k, hf * 512:(hf + 1) * 512], start=(k == 0), stop=(k == KC - 1))
                    return ins
                S.op("pe", mo, reads=[b_mgT, b_wout], writes=[PB[2 + hf]])
                S.op("dve", lambda e, hf=hf: e.tensor_tensor(out=yt[:, hf * 512:(hf + 1) * 512], in0=PS[2 + hf][:, :], in1=gate_bc[:, hf * 512:(hf + 1) * 512], op=ALU.mult),
                     reads=[PB[2 + hf], b_gate], writes=[b_yt])
            S.op("pool", lambda e, sl=sl: e.tensor_tensor(out=yt[:, :], in0=yt[:, :], in1=xt[sl][:, :], op=ALU.add), reads=[b_xt[sl]], writes=[b_yt])
            dma("sp", y_own[j * 128:(j + 1) * 128, :], yt[:, :], sem_y, reads=[b_yt])

    for it in range(n_units + LAG):
        for q in load_at.get(it, []):
            emit_load(q)
        if it < n_units:
            stage_ab(units[it], it)
        for pd in list(pending):
            pd[0] -= 1
            if pd[0] <= 0:
                pending.remove(pd)
                pd[1]()
        if it >= LAG:
            ud = units[it - LAG]
            stage_c(ud, it - LAG)
            if ud["last_of_group"]:
                for pd in pending:
                    pd[1]()
                del pending[:]
                epilogue(ud["g"])

    S.flush_block(nc)
    P3.close()
    G.close()
    sem_stack.close()
    return nc


def _t5_bucket_np(rel):
    nb, max_exact = 16, 8
    ret = np.where(rel > 0, nb, 0)
    n = np.abs(rel)
    nf = np.maximum(n, 1).astype(np.float32)
    large = max_exact + (np.log(nf / np.float32(max_exact)) / np.float32(math.log(128 / max_exact)) * np.float32(nb - max_exact)).astype(np.int32)
    large = np.minimum(large, nb - 1)
    return ret + np.where(n < max_exact, n, large)


def _constants():
    k = np.arange(128)[:, None]
    q = np.arange(128)[None, :]
    allowed = (k // 64) <= (q // 64)
    diag = np.where(allowed, _t5_bucket_np(k - q), -1)
    sub = _t5_bucket_np(k - 128 - q)
    bidx = np.concatenate([diag, sub], axis=1).astype(np.float32)
    mmask = np.where(allowed, 0.0, NEG).astype(np.float32)
    invf = (np.float32(10000.0) ** (-np.arange(32, dtype=np.float32) / np.float32(32))).astype(np.float32)[None, :]
    ident = np.eye(128, dtype=np.float32)
    return bidx, mmask, invf, ident


def make_in_maps(inp, S_len):
    NT = S_len // 128
    OT = NT // NCORES
    f = lambda a: np.ascontiguousarray(np.asarray(a, dtype=np.float32))
    x = f(inp["x"])[0]
    pos = np.asarray(inp["positions"], dtype=np.int32)[0]
    bidx, mmask, invf, ident = _constants()
    col = lambda v, k: np.ascontiguousarray(f(v).reshape(k, 128).T)
    common = {
        "c_col": col(inp["c"][0], KC), "g_col": col(inp["norm_g"][0], KC),
        "ada_w": f(inp["ada_w"][0]), "ada_b": f(inp["ada_b"][0])[None, :],
        "w_in": f(inp["w_in"][0]), "w_out": f(inp["w_out"][0]), "w_uq": f(inp["w_uq"][0]),
        "w_uk": f(inp["w_uk"][0]), "w_uv": f(inp["w_uv"][0]),
        "qa_col": col(inp["mla_q_a_norm"][0], 2), "kva_col": col(inp["mla_kv_a_norm"][0], 1),
        "sub_col": col(inp["diff_sub_norm"][0], 1),
        "dqn": f(inp["diff_q_norm"][0])[None, :], "dkn": f(inp["diff_k_norm"][0])[None, :],
        "mqn": f(inp["mla_q_norm"][0])[None, :], "mkn": f(inp["mla_k_norm"][0])[None, :],
        "lam": f(inp["diff_lambda"][0]).reshape(1, 256), "relb": f(inp["rel_bias"]).reshape(1, 128),
        "invf": invf, "ident": ident, "bidx": bidx, "mmask": mmask,
    }
    maps = []
    for c in range(NCORES):
        sh = NCORES - 1 - c
        x_pad = np.zeros((NT * 128, D), np.float32)
        x_pad[sh * 128:] = x[:(NT - sh) * 128]
        pos_pad = np.zeros((NT * 128,), np.int32)
        pos_pad[sh * 128:] = pos[:(NT - sh) * 128]
        valid = np.ones((128, NT), np.float32)
        valid[:, :sh] = 0.0
        rows = np.concatenate([np.arange((NCORES * j + c) * 128, (NCORES * j + c + 1) * 128) for j in range(OT)])
        m = dict(common)
        m.update({"x_pad": x_pad, "x_own": np.ascontiguousarray(x[rows]), "pos_pad": pos_pad.reshape(NT, 128), "valid": valid})
        maps.append(m)
    return maps


def gather_out(results, S_len):
    NT = S_len // 128
    OT = NT // NCORES
    out = np.zeros((1, S_len, D), np.float32)
    for c in range(NCORES):
        y = np.asarray(results[c]["y_own"], dtype=np.float32)
        for j in range(OT):
            g = NCORES * j + c
            out[0, g * 128:(g + 1) * 128] = y[j * 128:(j + 1) * 128]
    return out


def kernel(**inputs):
    S_len = int(np.asarray(inputs["x"]).shape[1])
    nc = build(S_len)
    in_maps = make_in_maps(inputs, S_len)
    res = run_bass_kernel_spmd(nc, in_maps, core_ids=list(range(NCORES)))
    return gather_out(res.results, S_len)
```

```python
import math
import numpy as np
import ml_dtypes
import concourse.bass as bass
import concourse.mybir as mybir
from concourse.bass_utils import run_bass_kernel_spmd

F32 = mybir.dt.float32
BF16 = mybir.dt.bfloat16
I32 = mybir.dt.int32
AF = mybir.ActivationFunctionType
ALU = mybir.AluOpType
AX = mybir.AxisListType

NCORES = 8
D = 1024
KC = 8
INC = 3008
NA = 1984
C_DQ, C_DK, C_DV, C_CQ, C_CKV, C_KR, C_G = 0, 512, 1024, 1536, 1792, 1920, 1984
VW = 130
EPS = 1e-6
NEG = -30000.0
LAM_INIT = 0.8 - 0.6 * math.exp(-0.3 * 0.0)


class Sem:
    def __init__(self, h, name):
        self.h, self.n, self.name = h, 0, name


class Buf:
    def __init__(self, name, const=False, excl=False):
        self.name, self.w, self.r, self.const, self.excl = name, None, {}, const, excl


class Eng:
    def __init__(self, name, sem):
        self.name, self.sem, self.seen, self.ops = name, sem, {}, []


class Sched:
    def __init__(self):
        self.eng = {}
        self.dma_sems = []
        self.capture = None

    def record(self, f):
        self.capture = []
        f()
        ops, self.capture = self.capture, None
        return ops

    def replay_merged(self, *streams):
        streams = [st for st in streams if st]
        idx = [0] * len(streams)
        while True:
            best, bf = None, None
            for k, st in enumerate(streams):
                if idx[k] < len(st):
                    f = idx[k] / len(st)
                    if bf is None or f < bf:
                        best, bf = k, f
            if best is None:
                break
            self.op(*streams[best][idx[best]])
            idx[best] += 1

    def add_engine(self, name, semh):
        self.eng[name] = Eng(name, Sem(semh, name))

    def new_dma_sem(self, semh, name):
        s = Sem(semh, name)
        self.dma_sems.append(s)
        return s

    def op(self, eng, fn, reads=(), writes=(), dma=None):
        if self.capture is not None:
            self.capture.append((eng, fn, tuple(reads), tuple(writes), dma))
            return None
        E = self.eng[eng]
        waits = {}
        writes = list(writes) + [b for b in reads if b.excl and b not in writes]
        reads = [b for b in reads if not b.excl]

        def need(tok):
            if tok is None:
                return
            s, v = tok
            if waits.get(s, 0) < v:
                waits[s] = v
        for b in reads:
            need(b.w)
        for b in writes:
            need(b.w)
            for s, v in b.r.items():
                need((s, v))
        wl = []
        for s, v in waits.items():
            if eng == "pe" and s is E.sem:
                continue
            if E.seen.get(s, 0) < v:
                E.seen[s] = v
                wl.append((s.h, v))
        if dma is None:
            E.sem.n += 1
            tok = (E.sem, E.sem.n)
            inc = 1
        else:
            dma.n += 16
            tok = (dma, dma.n)
            inc = 16
        semh = tok[0].h

        def emit(e, wl=wl, fn=fn, semh=semh, inc=inc):
            for h, v in wl:
                e.wait_ge(h, v)
            fn(e).then_inc(semh, inc)
        E.ops.append(emit)
        for b in reads:
            if not b.const:
                if b.r.get(tok[0], 0) < tok[1]:
                    b.r[tok[0]] = tok[1]
        for b in writes:
            b.w = tok
            b.r = {}
        return tok

    def wait_all_dma(self, eng):
        E = self.eng[eng]
        wl = []
        for s in self.dma_sems:
            if s.n > 0 and E.seen.get(s, 0) < s.n:
                E.seen[s] = s.n
                wl.append((s.h, s.n))

        def emit(e, wl=wl):
            for h, v in wl:
                e.wait_ge(h, v)
        E.ops.append(emit)

    def flush_block(self, nc):
        self.wait_all_dma("sp")
        self.wait_all_dma("pool")
        with nc.Block() as block:
            for name, deco in (("sp", block.sync), ("pe", block.tensor), ("act", block.scalar),
                               ("dve", block.vector), ("pool", block.gpsimd)):
                ops = self.eng[name].ops
                if not ops:
                    continue

                def body(e, ops=ops):
                    for f in ops:
                        f(e)
                deco(body)
                self.eng[name].ops = []
        for E in self.eng.values():
            for E2 in self.eng.values():
                E.seen[E2.sem] = E2.sem.n
            for s in self.dma_sems:
                E.seen[s] = s.n


def bcast_rows(ap1n, parts=128):
    n = ap1n.shape[-1]
    return bass.AP(ap1n.tensor, ap1n.offset, [[0, parts], [1, n]])


def build(S_len):
    NT = S_len // 128
    OT = NT // NCORES
    GQ = min(4, OT)
    NG = OT // GQ
    CH = min(16, NT)
    ST = 2
    NST = NT // ST
    SP = NT * 128

    nc = bass.Bass("TRN2", target_bir_lowering=False)
    S = Sched()

    def din(name, shape, dt=F32):
        return nc.dram_tensor(name, list(shape), dt, kind="ExternalInput").ap()

    x_pad = din("x_pad", [SP, D])
    x_own = din("x_own", [OT * 128, D])
    pos_pad = din("pos_pad", [NT, 128], I32)
    valid_in = din("valid", [128, NT])
    c_col_in = din("c_col", [128, KC])
    g_col_in = din("g_col", [128, KC])
    ada_w = din("ada_w", [D, 3072])
    ada_b = din("ada_b", [1, 3072])
    w_in = din("w_in", [D, INC])
    w_out = din("w_out", [D, D])
    w_uq = din("w_uq", [256, 768])
    w_uk = din("w_uk", [128, 512])
    w_uv = din("w_uv", [128, 512])
    qa_col_in = din("qa_col", [128, 2])
    kva_col_in = din("kva_col", [128, 1])
    sub_col_in = din("sub_col", [128, 1])
    dqn_in = din("dqn", [1, 64])
    dkn_in = din("dkn", [1, 64])
    mqn_in = din("mqn", [1, 192])
    mkn_in = din("mkn", [1, 192])
    lam_in = din("lam", [1, 256])
    relb_in = din("relb", [1, 128])
    invf_in = din("invf", [1, 32])
    ident_in = din("ident", [128, 128])
    idx_in = din("bidx", [128, 256])
    mmask_in = din("mmask", [128, 128])
    y_own = nc.dram_tensor("y_own", [OT * 128, D], F32, kind="ExternalOutput").ap()

    KdT_d = nc.dram_tensor("KdT_d", [8, 64, SP], BF16).ap()
    KmTn_d = nc.dram_tensor("KmTn_d", [4, 128, SP], BF16).ap()
    KmTp_d = nc.dram_tensor("KmTp_d", [4, 64, SP], BF16).ap()
    Vd_d = nc.dram_tensor("Vd_d", [4, 128, NT, VW], BF16).ap()
    Vm_d = nc.dram_tensor("Vm_d", [4, 128, NT, VW], BF16).ap()

    from contextlib import ExitStack
    sem_stack = ExitStack()
    for en in ("pe", "act", "dve", "pool", "sp"):
        S.add_engine(en, sem_stack.enter_context(nc.semaphore("sem_" + en)))

    def dsem(name):
        return S.new_dma_sem(sem_stack.enter_context(nc.semaphore("d_" + name)), name)

    def sb(stack, name, shape, dt=F32):
        return stack.enter_context(nc.sbuf_tensor("s_" + name, list(shape), dt))

    def dma(q, out, in_, sem, reads=(), writes=()):
        return S.op(q, lambda e: e.dma_start(out=out, in_=in_), reads=reads, writes=writes, dma=sem)

    G = ExitStack()
    PS = [G.enter_context(nc.psum_tensor("ps%d" % i, [128, 512], F32)) for i in range(8)]
    PB = [Buf("psb%d" % i, excl=True) for i in range(8)]

    def ps_bf(i):
        return PS[i][:, :].bitcast(BF16)

    A_col = sb(G, "A_col", [128, KC])
    shT = sb(G, "shT", [128, KC, 128])
    gate_bc = sb(G, "gate_bc", [128, 1024])
    QW = max(OT * 128, 2048)
    QdT = sb(G, "QdT", [128, 8, QW], BF16)
    QmTn = sb(G, "QmTn", [128, 4, QW], BF16)
    QmTp = sb(G, "QmTp", [128, 4, QW], BF16)
    wsc = sb(G, "wsc", [128, 1])
    b_wsc = Buf("wsc", True)
    QdT_f = QdT[:, :, :].rearrange("p a b -> p (a b)").bitcast(F32)
    QmTn_f = QmTn[:, :, :].rearrange("p a b -> p (a b)").bitcast(F32)
    QmTp_f = QmTp[:, :, :].rearrange("p a b -> p (a b)").bitcast(F32)
    ident_f = sb(G, "ident_f", [128, 128])
    ident_b = sb(G, "ident_b", [128, 128], BF16)
    Bd = sb(G, "Bd", [128, 2, 4, 128])
    Bm = sb(G, "Bm", [128, 128])
    nlam = sb(G, "nlam", [128, 1])
    eps_c = sb(G, "eps_c", [128, 1])
    xt = [sb(G, "xt%d" % i, [128, D]) for i in range(2)]
    xs2 = [sb(G, "xs0", [128, D], BF16)] * 3
    xT2 = [sb(G, "xT%d" % i, [128, KC, 128], BF16) for i in range(3)]
    junk = sb(G, "junk", [128, 256], BF16)
    st1 = sb(G, "st1", [128, 12])

    b_Wg, b_bWg, b_gate = Buf("Wg"), Buf("bWg"), Buf("gate", True)
    b_QdT, b_QmTn, b_QmTp = Buf("QdT"), Buf("QmTn"), Buf("QmTp")
    b_identf, b_identb, b_Bd, b_Bm, b_nlam, b_eps = (Buf("idf", True), Buf("idb", True), Buf("Bd", True),
                                                      Buf("Bm", True), Buf("nlam", True), Buf("eps", True))
    b_xt = [Buf("xt0"), Buf("xt1")]
    b_xs2, b_xT2, b_junk, b_st1_2 = [Buf("xs0")] * 3, [Buf("xT0"), Buf("xT1"), Buf("xT2")], Buf("junk"), [Buf("st1a"), Buf("st1b"), Buf("st1c")]
    sem_xt = [dsem("xt0"), dsem("xt1")]
    class _Fresh:
        cnt = 0
    def fresh_sem():
        _Fresh.cnt += 1
        return dsem("m%d" % _Fresh.cnt)

    def rstd_act(out_ap, in_ap, n, tmp_ap, bufs_r, bufs_w, b_tmp):
        S.op("act", lambda e: e.activation(out=tmp_ap, in_=in_ap, func=AF.Ln, scale=1.0 / n, bias=eps_c[:, 0:1]),
             reads=list(bufs_r) + [b_eps], writes=[b_tmp])
        S.op("act", lambda e: e.activation(out=out_ap, in_=tmp_ap, func=AF.Exp, scale=-0.5),
             reads=[b_tmp], writes=list(bufs_w))

    def load_x(src_ap, slot):
        dma("sp", xt[slot][:, :], src_ap, sem_xt[slot], writes=[b_xt[slot]])

    def norm_transpose_x(src_ap, slot, load=True, after_read=None, tslot=None):
        if tslot is None:
            tslot = slot
        xs, xT, b_xs, b_xT, b_st1 = xs2[tslot], xT2[tslot], b_xs2[tslot], b_xT2[tslot], b_st1_2[tslot]
        o = tslot * 4
        if load:
            load_x(src_ap, slot)
        S.op("act", lambda e: e.activation(out=xs[:, :], in_=xt[slot][:, :], func=AF.Square, accum_out=st1[:, o:o + 1]),
             reads=[b_xt[slot]], writes=[b_xs, b_st1])
        rstd_act(st1[:, o + 2:o + 3], st1[:, o:o + 1], D, st1[:, o + 1:o + 2], [b_st1], [b_st1], b_st1)
        S.op("act", lambda e: e.activation(out=xs[:, :], in_=xt[slot][:, :], func=AF.Copy, scale=st1[:, o + 2:o + 3]),
             reads=[b_xt[slot], b_st1], writes=[b_xs])
        if after_read is not None:
            after_read()

        def tr(e):
            ins = None
            for k in range(KC):
                ins = e.transpose(out=ps_bf(0)[:, k * 128:(k + 1) * 128], in_=xs[:, k * 128:(k + 1) * 128], identity=ident_b[:, :])
            return ins
        S.op("pe", tr, reads=[b_xs, b_identb], writes=[PB[0]])
        S.op("dve", lambda e: e.tensor_copy(out=xT[:, :, :].rearrange("p k t -> p (k t)"), in_=ps_bf(0)[:, :]),
             reads=[PB[0]], writes=[b_xT])

    def proj(bank, c0, ncols, W, b_W, wc0, slot):
        xT, b_xT = xT2[slot], b_xT2[slot]

        def mm(e):
            ins = None
            for k in range(KC):
                ins = e.matmul(PS[bank][:, c0:c0 + ncols], lhsT=xT[:, k, :], rhs=W[:, k, wc0:wc0 + ncols],
                               start=(k == 0), stop=(k == KC - 1))
            return ins
        S.op("pe", mm, reads=[b_xT, b_W], writes=[PB[bank]])

    S12 = ExitStack()
    Wa = sb(S12, "Wa", [128, KC, NA], BF16)
    bWa = sb(S12, "bWa", [128, NA])
    wuq_b = sb(S12, "wuq_b", [128, 2, 768], BF16)
    wuk_b = sb(S12, "wuk_b", [128, 512], BF16)
    wuv_b = sb(S12, "wuv_b", [128, 512], BF16)
    gqk_d = sb(S12, "gqk_d", [128, 64])
    gqk_m = sb(S12, "gqk_m", [128, 192])
    cosT = sb(S12, "cosT", [128, NT, 32])
    sinT = sb(S12, "sinT", [128, NT, 32])
    valid = sb(S12, "valid_sb", [128, NT])
    b_Wa, b_bWa, b_wuq, b_wuk, b_wuv = Buf("Wa", True), Buf("bWa", True), Buf("wuq", True), Buf("wuk", True), Buf("wuv", True)
    b_gqkd, b_gqkm, b_cos, b_sin, b_valid = Buf("gqkd", True), Buf("gqkm", True), Buf("cos", True), Buf("sin", True), Buf("valid", True)

    SU = ExitStack()
    class _V:
        def __init__(self, ap):
            self.ap = ap
        def __getitem__(self, k):
            return self.ap[k]
    wbuf = [_V(QmTn_f[:, 0:3072]), _V(QdT_f[:, 0:3072])]
    b_wbuf = [Buf("wbuf0"), Buf("wbuf1")]
    sem_wbuf = [dsem("wbuf0"), dsem("wbuf1")]
    mod_bc = _V(QdT_f[:, 3072:6144])
    adab_bc = _V(QmTp_f[:, 0:3072])
    c_col = sb(SU, "c_col", [128, KC])
    sc_col = sb(SU, "sc_col", [128, KC])
    scb = sb(SU, "scb", [128, KC, 128])
    g_col = sb(SU, "g_col", [128, KC])
    scl_col = sb(SU, "scl_col", [128, KC])
    small = sb(SU, "small", [128, 16])
    vec = sb(SU, "vec", [128, 1024])
    idx = sb(SU, "idx", [128, 256])
    oh = sb(SU, "oh", [128, 256])
    relb = sb(SU, "relb", [128, 128])
    rbs = sb(SU, "rbs", [128, 128])
    posi = sb(SU, "posi", [128, 128], I32)
    posf = sb(SU, "posf", [128, 128])
    posT = sb(SU, "posT", [128, NT])
    invf = sb(SU, "invf", [128, 32])
    ang = _V(QdT_f[:, 0:NT * 32])
    kf = _V(QdT_f[:, 4096:4096 + NT * 32])
    ki = _V(QmTn[:, :, :].rearrange("p a b -> p (a b)").bitcast(I32)[:, 0:NT * 32])
    r2 = _V(QmTp_f[:, 0:NT * 32])
    b_mod, b_adab, b_ccol, b_sccol, b_scb, b_gcol, b_Acol, b_shT, b_sclcol = [Buf(n) for n in
        ("mod", "adab", "ccol", "sccol", "scb", "gcol", "Acol", "shT", "sclcol")]
    b_small, b_vec, b_idx, b_oh, b_relb, b_rbs = [Buf(n) for n in ("small", "vec", "idx", "oh", "relb", "rbs")]
    b_posi, b_posf, b_posT, b_invf, b_ang, b_kf, b_ki, b_r2 = [Buf(n) for n in
        ("posi", "posf", "posT", "invf", "ang", "kf", "ki", "r2")]

    dma("sp", ident_f[:, :], ident_in, fresh_sem(), writes=[b_identf])
    dma("sp", c_col[:, :], c_col_in, fresh_sem(), writes=[b_ccol])
    dma("sp", g_col[:, :], g_col_in, fresh_sem(), writes=[b_gcol])
    dma("sp", adab_bc[:, :], bcast_rows(ada_b), fresh_sem(), writes=[b_adab])
    dma("sp", idx[:, :], idx_in, fresh_sem(), writes=[b_idx])
    dma("sp", Bm[:, :], mmask_in, fresh_sem(), writes=[b_Bm])
    dma("sp", relb[:, :], bcast_rows(relb_in), fresh_sem(), writes=[b_relb])
    dma("sp", valid[:, :], valid_in, fresh_sem(), writes=[b_valid])
    dma("sp", invf[:, :], bcast_rows(invf_in), fresh_sem(), writes=[b_invf])
    dma("sp", posi[0:NT, :], pos_pad, fresh_sem(), writes=[b_posi])
    S.op("dve", lambda e: e.memset(eps_c[:, :], EPS), writes=[b_eps])
    S.op("dve", lambda e: e.tensor_copy(out=ident_b[:, :], in_=ident_f[:, :]), reads=[b_identf], writes=[b_identb])

    S.op("act", lambda e: e.activation(out=sc_col[:, :], in_=c_col[:, :], func=AF.Silu), reads=[b_ccol], writes=[b_sccol])
    S.op("dve", lambda e: e.tensor_copy(out=scb[:, :, :], in_=sc_col[:, :].unsqueeze(2).broadcast_to([128, KC, 128])),
         reads=[b_sccol], writes=[b_scb])
    for k in range(KC):
        sl = k % 2
        dma("sp", wbuf[sl][:, :], ada_w[k * 128:(k + 1) * 128, :], sem_wbuf[sl], writes=[b_wbuf[sl]])

        def mm(e, k=k, sl=sl):
            ins = None
            for j in range(6):
                ins = e.matmul(PS[j][:, :], lhsT=scb[:, k, :], rhs=wbuf[sl][:, j * 512:(j + 1) * 512],
                               start=(k == 0), stop=(k == KC - 1))
            return ins
        S.op("pe", mm, reads=[b_scb, b_wbuf[sl]], writes=PB[0:6])
    for j in range(6):
        S.op("dve", lambda e, j=j: e.tensor_tensor(out=mod_bc[:, j * 512:(j + 1) * 512], in0=PS[j][:, :],
                                                   in1=adab_bc[:, j * 512:(j + 1) * 512], op=ALU.add),
             reads=[PB[j], b_adab], writes=[b_mod])
    S.op("dve", lambda e: e.tensor_copy(out=gate_bc[:, :], in_=mod_bc[:, 2048:3072]), reads=[b_mod], writes=[b_gate])
    for k in range(KC):
        bank = 6 + (k % 2)
        S.op("pe", lambda e, k=k, bank=bank: e.transpose(out=PS[bank][:, 0:128], in_=mod_bc[:, k * 128:(k + 1) * 128], identity=ident_f[:, :]),
             reads=[b_mod, b_identf], writes=[PB[bank]])
        S.op("dve", lambda e, k=k, bank=bank: e.tensor_copy(out=shT[:, k, :], in_=PS[bank][:, 0:128]), reads=[PB[bank]], writes=[b_shT])
        S.op("pe", lambda e, k=k, bank=bank: e.transpose(out=PS[bank][:, 128:256], in_=mod_bc[:, 1024 + k * 128:1024 + (k + 1) * 128], identity=ident_f[:, :]),
             reads=[b_mod, b_identf], writes=[PB[bank]])
        S.op("dve", lambda e, k=k, bank=bank: e.tensor_copy(out=scl_col[:, k:k + 1], in_=PS[bank][:, 128:129]), reads=[PB[bank]], writes=[b_sclcol])
    S.op("dve", lambda e: e.scalar_tensor_tensor(out=A_col[:, :], in0=scl_col[:, :], scalar=1.0, in1=g_col[:, :], op0=ALU.add, op1=ALU.mult),
         reads=[b_sclcol, b_gcol], writes=[b_Acol])

    for k in range(KC):
        sl = k % 2
        dma("sp", wbuf[sl][:, 0:NA], w_in[k * 128:(k + 1) * 128, 0:NA], sem_wbuf[sl], writes=[b_wbuf[sl]])

        def mm(e, k=k, sl=sl):
            ins = None
            for j in range(4):
                w = min(512, NA - j * 512)
                ins = e.matmul(PS[j][:, 0:w], lhsT=shT[:, k, :], rhs=wbuf[sl][:, j * 512:j * 512 + w],
                               start=(k == 0), stop=(k == KC - 1))
            return ins
        S.op("pe", mm, reads=[b_shT, b_wbuf[sl]], writes=PB[0:4])
        S.op("act", lambda e, k=k, sl=sl: e.activation(out=Wa[:, k, :], in_=wbuf[sl][:, 0:NA], func=AF.Copy, scale=A_col[:, k:k + 1]),
             reads=[b_wbuf[sl], b_Acol], writes=[b_Wa])
    for j in range(4):
        w = min(512, NA - j * 512)
        S.op("dve", lambda e, j=j, w=w: e.tensor_copy(out=bWa[:, j * 512:j * 512 + w], in_=PS[j][:, 0:w]), reads=[PB[j]], writes=[b_bWa])

    dma("sp", small[:, 0:1], sub_col_in, fresh_sem(), writes=[b_small])
    dma("sp", small[:, 1:3], qa_col_in, fresh_sem(), writes=[b_small])
    dma("sp", small[:, 3:4], kva_col_in, fresh_sem(), writes=[b_small])
    S.op("dve", lambda e: e.tensor_scalar(out=small[:, 4:5], in0=small[:, 0:1], scalar1=1.0 - LAM_INIT, scalar2=None, op0=ALU.mult),
         reads=[b_small], writes=[b_small])
    S.op("dve", lambda e: e.tensor_copy(out=wsc[:, :], in_=small[:, 4:5]), reads=[b_small], writes=[b_wsc])
    for k in range(2):
        dma("sp", wbuf[k][:, 0:768], w_uq[k * 128:(k + 1) * 128, :], sem_wbuf[k], writes=[b_wbuf[k]])
        S.op("act", lambda e, k=k: e.activation(out=wuq_b[:, k, :], in_=wbuf[k][:, 0:768], func=AF.Copy, scale=small[:, 1 + k:2 + k]),
             reads=[b_wbuf[k], b_small], writes=[b_wuq])
    dma("sp", wbuf[0][:, 0:512], w_uk, sem_wbuf[0], writes=[b_wbuf[0]])
    S.op("act", lambda e: e.activation(out=wuk_b[:, :], in_=wbuf[0][:, 0:512], func=AF.Copy, scale=small[:, 3:4]),
         reads=[b_wbuf[0], b_small], writes=[b_wuk])
    dma("sp", wbuf[1][:, 0:512], w_uv, sem_wbuf[1], writes=[b_wbuf[1]])
    S.op("act", lambda e: e.activation(out=wuv_b[:, :], in_=wbuf[1][:, 0:512], func=AF.Copy, scale=small[:, 3:4]),
         reads=[b_wbuf[1], b_small], writes=[b_wuv])

    dma("sp", vec[:, 0:64], bcast_rows(dqn_in), fresh_sem(), writes=[b_vec])
    dma("sp", vec[:, 64:128], bcast_rows(dkn_in), fresh_sem(), writes=[b_vec])
    dma("sp", vec[:, 128:320], bcast_rows(mqn_in), fresh_sem(), writes=[b_vec])
    dma("sp", vec[:, 320:512], bcast_rows(mkn_in), fresh_sem(), writes=[b_vec])
    dma("sp", vec[:, 512:768], bcast_rows(lam_in), fresh_sem(), writes=[b_vec])
    S.op("dve", lambda e: e.scalar_tensor_tensor(out=gqk_d[:, :], in0=vec[:, 0:64], scalar=64 ** -0.5, in1=vec[:, 64:128], op0=ALU.mult, op1=ALU.mult),
         reads=[b_vec], writes=[b_gqkd])
    S.op("dve", lambda e: e.scalar_tensor_tensor(out=gqk_m[:, :], in0=vec[:, 128:320], scalar=192 ** -0.5, in1=vec[:, 320:512], op0=ALU.mult, op1=ALU.mult),
         reads=[b_vec], writes=[b_gqkm])
    S.op("dve", lambda e: e.tensor_tensor(out=vec[:, 768:832], in0=vec[:, 512:576], in1=vec[:, 576:640], op=ALU.mult), reads=[b_vec], writes=[b_vec])
    S.op("dve", lambda e: e.tensor_tensor(out=vec[:, 832:896], in0=vec[:, 640:704], in1=vec[:, 704:768], op=ALU.mult), reads=[b_vec], writes=[b_vec])
    S.op("dve", lambda e: e.tensor_reduce(out=small[:, 8:10], in_=vec[:, 768:896].rearrange("p (a d) -> p a d", a=2), axis=AX.X, op=ALU.add),
         reads=[b_vec], writes=[b_small])
    S.op("act", lambda e: e.activation(out=small[:, 10:12], in_=small[:, 8:10], func=AF.Exp), reads=[b_small], writes=[b_small])
    S.op("dve", lambda e: e.scalar_tensor_tensor(out=nlam[:, :], in0=small[:, 11:12], scalar=-LAM_INIT, in1=small[:, 10:11], op0=ALU.add, op1=ALU.subtract),
         reads=[b_small], writes=[b_nlam])

    for h in range(4):
        S.op("dve", lambda e, h=h: e.tensor_scalar(out=rbs[:, h * 32:(h + 1) * 32], in0=relb[:, :].rearrange("p (b h) -> p h b", h=4)[:, h, :],
                                                   scalar1=relb[:, 15 * 4 + h:15 * 4 + h + 1], scalar2=None, op0=ALU.subtract),
             reads=[b_relb], writes=[b_rbs])
    S.op("pool", lambda e: e.memset(Bd[:, :, :, :], 0.0), writes=[b_Bd])
    for b in list(range(0, 32)) + [-1]:
        S.op("dve", lambda e, b=b: e.tensor_scalar(out=oh[:, :], in0=idx[:, :], scalar1=float(b), scalar2=None, op0=ALU.is_equal),
             reads=[b_idx], writes=[b_oh])
        for h in range(4):
            for ty in range(2):
                if b == -1:
                    S.op("dve", lambda e, h=h, ty=ty: e.scalar_tensor_tensor(out=Bd[:, ty, h, :], in0=oh[:, ty * 128:(ty + 1) * 128], scalar=NEG,
                                                                             in1=Bd[:, ty, h, :], op0=ALU.mult, op1=ALU.add),
                         reads=[b_oh], writes=[b_Bd])
                else:
                    S.op("dve", lambda e, h=h, ty=ty, b=b: e.scalar_tensor_tensor(out=Bd[:, ty, h, :], in0=oh[:, ty * 128:(ty + 1) * 128],
                                                                                  scalar=rbs[:, h * 32 + b:h * 32 + b + 1],
                                                                                  in1=Bd[:, ty, h, :], op0=ALU.mult, op1=ALU.add),
                         reads=[b_oh, b_rbs], writes=[b_Bd])

    S.flush_block(nc)
    S.op("dve", lambda e: e.tensor_copy(out=posf[0:NT, :], in_=posi[0:NT, :]), reads=[b_posi], writes=[b_posf])
    S.op("pe", lambda e: e.transpose(out=PS[7][:, 0:NT], in_=posf[0:NT, :], identity=ident_f[0:NT, 0:NT]), reads=[b_posf, b_identf], writes=[PB[7]])
    S.op("dve", lambda e: e.tensor_copy(out=posT[:, :], in_=PS[7][:, 0:NT]), reads=[PB[7]], writes=[b_posT])
    ang3 = ang[:, :].rearrange("p (t f) -> p t f", f=32)
    S.op("dve", lambda e: e.tensor_tensor(out=ang3, in0=posT[:, :].unsqueeze(2).broadcast_to([128, NT, 32]),
                                          in1=invf[:, :].unsqueeze(1).broadcast_to([128, NT, 32]), op=ALU.mult),
         reads=[b_posT, b_invf], writes=[b_ang])
    TWO_PI = 2.0 * math.pi
    C1 = 6.28125
    C2 = float(np.float32(0.0019350051879882812))
    C3 = float(np.float32(TWO_PI - C1 - C2))

    def reduce_sin(dst3, shift):
        S.op("dve", lambda e: e.tensor_scalar(out=kf[:, :], in0=ang[:, :], scalar1=1.0 / TWO_PI, scalar2=None, op0=ALU.mult), reads=[b_ang], writes=[b_kf])
        S.op("dve", lambda e: e.tensor_copy(out=ki[:, :], in_=kf[:, :]), reads=[b_kf], writes=[b_ki])
        S.op("dve", lambda e: e.tensor_copy(out=kf[:, :], in_=ki[:, :]), reads=[b_ki], writes=[b_kf])
        S.op("dve", lambda e: e.scalar_tensor_tensor(out=r2[:, :], in0=kf[:, :], scalar=-C1, in1=ang[:, :], op0=ALU.mult, op1=ALU.add), reads=[b_kf, b_ang], writes=[b_r2])
        S.op("dve", lambda e: e.scalar_tensor_tensor(out=r2[:, :], in0=kf[:, :], scalar=-C2, in1=r2[:, :], op0=ALU.mult, op1=ALU.add), reads=[b_kf], writes=[b_r2])
        S.op("dve", lambda e: e.scalar_tensor_tensor(out=r2[:, :], in0=kf[:, :], scalar=-C3, in1=r2[:, :], op0=ALU.mult, op1=ALU.add), reads=[b_kf], writes=[b_r2])
        if shift != 0.0:
            S.op("dve", lambda e: e.tensor_scalar(out=r2[:, :], in0=r2[:, :], scalar1=shift, scalar2=None, op0=ALU.add), reads=[], writes=[b_r2])
        S.op("dve", lambda e: e.tensor_scalar(out=kf[:, :], in0=r2[:, :], scalar1=math.pi, scalar2=-TWO_PI, op0=ALU.is_gt, op1=ALU.mult), reads=[b_r2], writes=[b_kf])
        S.op("dve", lambda e: e.tensor_tensor(out=r2[:, :], in0=r2[:, :], in1=kf[:, :], op=ALU.add), reads=[b_kf], writes=[b_r2])
        S.op("dve", lambda e: e.tensor_scalar(out=kf[:, :], in0=r2[:, :], scalar1=-math.pi, scalar2=TWO_PI, op0=ALU.is_lt, op1=ALU.mult), reads=[b_r2], writes=[b_kf])
        S.op("dve", lambda e: e.tensor_tensor(out=r2[:, :], in0=r2[:, :], in1=kf[:, :], op=ALU.add), reads=[b_kf], writes=[b_r2])
        S.op("dve", lambda e: e.tensor_scalar(out=r2[:, :], in0=r2[:, :], scalar1=-3.1415925, scalar2=3.1415925, op0=ALU.max, op1=ALU.min), reads=[], writes=[b_r2])
        S.op("act", lambda e: e.activation(out=dst3, in_=r2[:, :].rearrange("p (t f) -> p t f", f=32), func=AF.Sin), reads=[b_r2], writes=[b_sin, b_cos])
    reduce_sin(sinT[:, :, :], 0.0)
    reduce_sin(cosT[:, :, :], math.pi / 2)

    S.flush_block(nc)
    SU.close()

    P1 = ExitStack()
    kb2 = [sb(P1, "kb%d" % i, [128, 512]) for i in range(2)]
    cb2 = [sb(P1, "cb%d" % i, [128, 256]) for i in range(2)]
    ckvnT2 = [sb(P1, "ckvnT%d" % i, [128, 2, 128], BF16) for i in range(2)]
    qb = sb(P1, "qb", [128, 512])
    sq = sb(P1, "sq", [128, 768])
    kn = sb(P1, "kn", [128, 8, 64], BF16)
    ckvn = sb(P1, "ckvn", [128, 256], BF16)
    ckvn_f = sb(P1, "ckvn_f", [128, 128], BF16)
    kpe = sb(P1, "kpe", [128, 4, 64])
    rt = sb(P1, "rt", [128, 4, 4, 32])
    km = sb(P1, "km", [128, 4, 192], BF16)
    qm = sb(P1, "qm", [128, 4, 192])
    st2 = sb(P1, "st2", [128, 96])
    stgKd = sb(P1, "stgKd", [64, 8, ST * 128], BF16)
    stgKn = sb(P1, "stgKn", [128, 4, ST * 128], BF16)
    stgKp = sb(P1, "stgKp", [64, 4, ST * 128], BF16)
    stgVd = sb(P1, "stgVd", [128, 2 * ST, 4, VW], BF16)
    stgVm = sb(P1, "stgVm", [128, ST, 4, VW], BF16)
    b_kb2, b_cb2, b_ckvnT2 = [Buf("kb0"), Buf("kb1")], [Buf("cb0"), Buf("cb1")], [Buf("ckT0"), Buf("ckT1")]
    b_qb, b_sq, b_kn, b_ckvn, b_kpe, b_rt, b_km, b_qm = [Buf(n) for n in ("qb", "sq", "kn", "ckvn", "kpe", "rt", "km", "qm")]
    b_ckvn_f = Buf("ckvn_f")
    b_stK, b_stC, b_stM, b_stQ, b_stQc, b_stQm = [Buf(n) for n in ("stK", "stC", "stM", "stQ", "stQc", "stQm")]
    b_sKd, b_sKn, b_sKp, b_sVd, b_sVm = [Buf(n) for n in ("sKd", "sKn", "sKp", "sVd", "sVm")]
    sem_sKd, sem_sKn, sem_sKp, sem_sVd, sem_sVm = [dsem(n) for n in ("sKd", "sKn", "sKp", "sVd", "sVm")]
    b_scr = Buf("scratch")
    S.op("pool", lambda e: e.memset(stgVd[:, :, :, :], 0.0), writes=[b_sVd])
    S.op("pool", lambda e: e.memset(stgVm[:, :, :, :], 0.0), writes=[b_sVm])
    S.op("pool", lambda e: e.memset(stgVd[:, :, :, 128:129], 1.0), writes=[b_sVd])
    S.op("pool", lambda e: e.memset(stgVm[:, :, :, 128:129], 1.0), writes=[b_sVm])

    def group_norm(src_ap, ngroups, gsz, base, b_src, b_st):
        n = ngroups * gsz
        S.op("pool", lambda e: e.tensor_tensor(out=sq[:, 0:n], in0=src_ap, in1=src_ap, op=ALU.mult), reads=[b_src], writes=[b_sq])
        S.op("dve", lambda e: e.tensor_reduce(out=st2[:, base:base + ngroups], in_=sq[:, 0:n].rearrange("p (g d) -> p g d", g=ngroups), axis=AX.X, op=ALU.add),
             reads=[b_sq], writes=[b_st])
        rstd_act(st2[:, base + 16:base + 16 + ngroups], st2[:, base:base + ngroups], gsz, st2[:, base + 8:base + 8 + ngroups], [b_st], [b_st], b_st)

    def rope(dst3, src3, nh, t, b_src, eng):
        cs = cosT[:, t, :].unsqueeze(1).broadcast_to([128, nh, 32])
        sn = sinT[:, t, :].unsqueeze(1).broadcast_to([128, nh, 32])
        x1, x2 = src3[:, :, 0:32], src3[:, :, 32:64]
        S.op(eng, lambda e: e.tensor_tensor(out=rt[:, 0:nh, 0, :], in0=x1, in1=cs, op=ALU.mult), reads=[b_cos, b_src], writes=[b_rt])
        S.op(eng, lambda e: e.tensor_tensor(out=rt[:, 0:nh, 1, :], in0=x2, in1=sn, op=ALU.mult), reads=[b_sin, b_src], writes=[b_rt])
        S.op(eng, lambda e: e.tensor_tensor(out=rt[:, 0:nh, 2, :], in0=x2, in1=cs, op=ALU.mult), reads=[b_cos, b_src], writes=[b_rt])
        S.op(eng, lambda e: e.tensor_tensor(out=rt[:, 0:nh, 3, :], in0=x1, in1=sn, op=ALU.mult), reads=[b_sin, b_src], writes=[b_rt])
        S.op(eng, lambda e: e.tensor_tensor(out=dst3[:, :, 0:32], in0=rt[:, 0:nh, 0, :], in1=rt[:, 0:nh, 1, :], op=ALU.subtract), reads=[b_rt], writes=[b_rt])
        S.op(eng, lambda e: e.tensor_tensor(out=dst3[:, :, 32:64], in0=rt[:, 0:nh, 2, :], in1=rt[:, 0:nh, 3, :], op=ALU.add), reads=[b_rt], writes=[b_rt])

    def trk(e):
        ins = None
        for g in range(8):
            ins = e.transpose(out=ps_bf(6)[0:64, g * 128:(g + 1) * 128], in_=kn[:, g, :], identity=ident_b[:, :])
        return ins

    def trm(e):
        ins = None
        for h in range(4):
            e.transpose(out=ps_bf(7)[:, h * 128:(h + 1) * 128], in_=km[:, h, 0:128], identity=ident_b[:, :])
            ins = e.transpose(out=ps_bf(7)[0:64, 512 + h * 128:512 + (h + 1) * 128], in_=km[:, h, 128:192], identity=ident_b[:, :])
        return ins

    NDUM = NCORES - 1

    def front1(t):
        p = t % 2
        nxt = (lambda: load_x(x_pad[(t + 2) * 128:(t + 3) * 128, :], p)) if t + 2 < NT else None
        norm_transpose_x(None, p, load=False, after_read=nxt, tslot=t % 3)

    def front(t):
        p = t % 2
        slot = t % ST
        vslot = t % (2 * ST)
        kb, cb, ckvnT, b_kb, b_cb, b_ckvnT = kb2[p], cb2[p], ckvnT2[p], b_kb2[p], b_cb2[p], b_ckvnT2[p]
        proj(1, 0, 512, Wa, b_Wa, C_DK, t % 3)
        S.op("dve", lambda e: e.tensor_tensor(out=kb[:, :], in0=PS[1][:, :], in1=bWa[:, C_DK:C_DK + 512], op=ALU.add), reads=[PB[1], b_bWa], writes=[b_kb])
        proj(2, 0, 512, Wa, b_Wa, C_DV, t % 3)
        S.op("dve", lambda e: e.tensor_tensor(out=stgVd[:, vslot, :, 0:128], in0=PS[2][:, :].rearrange("p (h d) -> p h d", h=4),
                                              in1=bWa[:, C_DV:C_DV + 512].rearrange("p (h d) -> p h d", h=4), op=ALU.add),
             reads=[PB[2], b_bWa], writes=[b_sVd])
        if t < NDUM:
            S.op("dve", lambda e: e.tensor_scalar(out=stgVd[:, vslot, :, 0:128], in0=stgVd[:, vslot, :, 0:128], scalar1=valid[:, t:t + 1], scalar2=None, op0=ALU.mult),
                 reads=[b_valid], writes=[b_sVd])
            S.op("dve", lambda e: e.tensor_copy(out=stgVd[:, vslot, :, 128:129], in_=valid[:, t:t + 1].unsqueeze(1).broadcast_to([128, 4, 1])),
                 reads=[b_valid], writes=[b_sVd])
        elif t < NDUM + 2 * ST:
            S.op("dve", lambda e: e.memset(stgVd[:, vslot, :, 128:129], 1.0), writes=[b_sVd])
        proj(3, 0, 192, Wa, b_Wa, C_CKV, t % 3)
        S.op("dve", lambda e: e.tensor_tensor(out=cb[:, 0:192], in0=PS[3][:, 0:192], in1=bWa[:, C_CKV:C_CKV + 192], op=ALU.add), reads=[PB[3], b_bWa], writes=[b_cb])
        S.op("act", lambda e: e.activation(out=junk[:, 0:128], in_=cb[:, 0:128], func=AF.Square, accum_out=st2[:, 24:25]), reads=[b_cb], writes=[b_junk, b_stC])
        rstd_act(st2[:, 26:27], st2[:, 24:25], 128, st2[:, 25:26], [b_stC], [b_stC], b_stC)
        S.op("act", lambda e: e.activation(out=ckvn_f[:, 0:128], in_=cb[:, 0:128], func=AF.Copy, scale=st2[:, 26:27]), reads=[b_cb, b_stC], writes=[b_ckvn_f])
        S.op("pe", lambda e: e.transpose(out=ps_bf(3)[:, 512:640], in_=ckvn_f[:, 0:128], identity=ident_b[:, :]), reads=[b_ckvn_f, b_identb], writes=[PB[3]])
        S.op("dve", lambda e: e.tensor_copy(out=ckvnT[:, 0, :], in_=ps_bf(3)[:, 512:640]), reads=[PB[3]], writes=[b_ckvnT])

    def back_k(t):
        p = t % 2
        slot = t % ST
        stile = t // ST
        kb, cb, ckvnT, b_kb, b_cb, b_ckvnT = kb2[p], cb2[p], ckvnT2[p], b_kb2[p], b_cb2[p], b_ckvnT2[p]
        group_norm(kb[:, :], 8, 64, 0, b_kb, b_stK)
        S.op("pool", lambda e: e.tensor_tensor(out=kn[:, :, :], in0=kb[:, :].rearrange("p (g d) -> p g d", g=8),
                                               in1=st2[:, 16:24].unsqueeze(2).broadcast_to([128, 8, 64]), op=ALU.mult),
             reads=[b_kb, b_stK], writes=[b_kn])
        S.op("pe", trk, reads=[b_kn, b_identb], writes=[PB[6]])
        S.op("dve", lambda e: e.tensor_copy(out=stgKd[:, :, slot * 128:(slot + 1) * 128], in_=ps_bf(6)[0:64, :].rearrange("p (g t) -> p g t", g=8)),
             reads=[PB[6]], writes=[b_sKd])

    def back_m(t):
        p = t % 2
        slot = t % ST
        stile = t // ST
        kb, cb, ckvnT, b_kb, b_cb, b_ckvnT = kb2[p], cb2[p], ckvnT2[p], b_kb2[p], b_cb2[p], b_ckvnT2[p]
        S.op("pe", lambda e: e.matmul(PS[4][:, :], lhsT=ckvnT[:, 0, :], rhs=wuk_b[:, :], start=True, stop=True), reads=[b_ckvnT, b_wuk], writes=[PB[4]])
        S.op("pe", lambda e: e.matmul(PS[5][:, :], lhsT=ckvnT[:, 0, :], rhs=wuv_b[:, :], start=True, stop=True), reads=[b_ckvnT, b_wuv], writes=[PB[5]])
        if t < NDUM:
            S.op("act", lambda e: e.activation(out=stgVm[:, slot, :, 0:128], in_=PS[5][:, :].rearrange("p (h d) -> p h d", h=4),
                                               func=AF.Copy, scale=valid[:, t:t + 1]),
                 reads=[PB[5], b_valid], writes=[b_sVm])
            S.op("dve", lambda e: e.tensor_copy(out=stgVm[:, slot, :, 128:129], in_=valid[:, t:t + 1].unsqueeze(1).broadcast_to([128, 4, 1])),
                 reads=[b_valid], writes=[b_sVm])
        else:
            S.op("act", lambda e: e.activation(out=stgVm[:, slot, :, 0:128], in_=PS[5][:, :].rearrange("p (h d) -> p h d", h=4), func=AF.Copy),
                 reads=[PB[5]], writes=[b_sVm])
            if t < NDUM + ST:
                S.op("dve", lambda e: e.memset(stgVm[:, slot, :, 128:129], 1.0), writes=[b_sVm])
        rope(kpe[:, 0:1, :], cb[:, 128:192].unsqueeze(1), 1, t, b_cb, "dve")
        S.op("act", lambda e: e.activation(out=junk[:, 0:64], in_=kpe[:, 0, :], func=AF.Square, accum_out=st2[:, 27:28]), reads=[b_rt], writes=[b_junk, b_stM])
        for h in range(4):
            S.op("act", lambda e, h=h: e.activation(out=junk[:, 0:128], in_=PS[4][:, h * 128:(h + 1) * 128], func=AF.Square, accum_out=st2[:, 28 + h:29 + h]),
                 reads=[PB[4]], writes=[b_junk, b_stM])
        S.op("dve", lambda e: e.tensor_scalar(out=st2[:, 32:36], in0=st2[:, 28:32], scalar1=st2[:, 27:28], scalar2=None, op0=ALU.add), reads=[b_stM], writes=[b_stM])
        rstd_act(st2[:, 40:44], st2[:, 32:36], 192, st2[:, 36:40], [b_stM], [b_stM], b_stM)
        S.op("dve", lambda e: e.tensor_tensor(out=km[:, :, 0:128], in0=PS[4][:, :].rearrange("p (h d) -> p h d", h=4),
                                              in1=st2[:, 40:44].unsqueeze(2).broadcast_to([128, 4, 128]), op=ALU.mult),
             reads=[PB[4], b_stM], writes=[b_km])
        S.op("dve", lambda e: e.tensor_tensor(out=km[:, :, 128:192], in0=kpe[:, 0:1, :].broadcast_to([128, 4, 64]),
                                              in1=st2[:, 40:44].unsqueeze(2).broadcast_to([128, 4, 64]), op=ALU.mult),
             reads=[b_rt, b_stM], writes=[b_km])
        S.op("pe", trm, reads=[b_km, b_identb], writes=[PB[7]])
        S.op("dve", lambda e: e.tensor_copy(out=stgKn[:, :, slot * 128:(slot + 1) * 128], in_=ps_bf(7)[:, 0:512].rearrange("p (h t) -> p h t", h=4)),
             reads=[PB[7]], writes=[b_sKn])
        S.op("act", lambda e: e.activation(out=stgKp[:, :, slot * 128:(slot + 1) * 128], in_=ps_bf(7)[0:64, 512:1024].rearrange("p (h t) -> p h t", h=4), func=AF.Copy),
             reads=[PB[7]], writes=[b_sKp])


    def back_q(t):
        p = t % 2
        slot = t % ST
        stile = t // ST
        kb, cb, ckvnT, b_kb, b_cb, b_ckvnT = kb2[p], cb2[p], ckvnT2[p], b_kb2[p], b_cb2[p], b_ckvnT2[p]
        if t % NCORES == NCORES - 1:
            j = t // NCORES
            qc = slice(j * 128, (j + 1) * 128)
            proj(4, 0, 512, Wa, b_Wa, C_DQ, t % 3)
            S.op("dve", lambda e: e.tensor_tensor(out=qb[:, :], in0=PS[4][:, :], in1=bWa[:, C_DQ:C_DQ + 512], op=ALU.add), reads=[PB[4], b_bWa], writes=[b_qb])
            group_norm(qb[:, :], 8, 64, 48, b_qb, b_stQ)
            S.op("pool", lambda e: e.tensor_tensor(out=qb[:, :].rearrange("p (g d) -> p g d", g=8), in0=qb[:, :].rearrange("p (g d) -> p g d", g=8),
                                                   in1=st2[:, 64:72].unsqueeze(2).broadcast_to([128, 8, 64]), op=ALU.mult),
                 reads=[b_stQ], writes=[b_qb])
            S.op("pool", lambda e: e.tensor_tensor(out=kn[:, :, :], in0=qb[:, :].rearrange("p (g d) -> p g d", g=8),
                                                   in1=gqk_d[:, :].unsqueeze(1).broadcast_to([128, 8, 64]), op=ALU.mult),
                 reads=[b_qb, b_gqkd], writes=[b_kn])
            S.op("pe", trk, reads=[b_kn, b_identb], writes=[PB[6]])
            S.op("dve", lambda e: e.tensor_copy(out=QdT[0:64, :, qc], in_=ps_bf(6)[0:64, :].rearrange("p (g t) -> p g t", g=8)),
                 reads=[PB[6]], writes=[b_QdT])
            proj(5, 0, 256, Wa, b_Wa, C_CQ, t % 3)
            S.op("dve", lambda e: e.tensor_tensor(out=qb[:, 0:256], in0=PS[5][:, 0:256], in1=bWa[:, C_CQ:C_CQ + 256], op=ALU.add), reads=[PB[5], b_bWa], writes=[b_qb])
            S.op("act", lambda e: e.activation(out=junk[:, 0:256], in_=qb[:, 0:256], func=AF.Square, accum_out=st2[:, 72:73]), reads=[b_qb], writes=[b_junk, b_stQc])
            rstd_act(st2[:, 74:75], st2[:, 72:73], 256, st2[:, 73:74], [b_stQc], [b_stQc], b_stQc)
            S.op("act", lambda e: e.activation(out=ckvn[:, :], in_=qb[:, 0:256], func=AF.Copy, scale=st2[:, 74:75]), reads=[b_qb, b_stQc], writes=[b_ckvn])

            def trq(e):
                e.transpose(out=ps_bf(6)[:, 0:128], in_=ckvn[:, 0:128], identity=ident_b[:, :])
                return e.transpose(out=ps_bf(6)[:, 128:256], in_=ckvn[:, 128:256], identity=ident_b[:, :])
            S.op("pe", trq, reads=[b_ckvn, b_identb], writes=[PB[6]])
            S.op("dve", lambda e: e.tensor_copy(out=ckvnT[:, :, :], in_=ps_bf(6)[:, 0:256].rearrange("p (k t) -> p k t", k=2)), reads=[PB[6]], writes=[b_ckvnT])

            def mq(e):
                e.matmul(PS[4][:, :], lhsT=ckvnT[:, 0, :], rhs=wuq_b[:, 0, 0:512], start=True, stop=False)
                e.matmul(PS[4][:, :], lhsT=ckvnT[:, 1, :], rhs=wuq_b[:, 1, 0:512], start=False, stop=True)
                e.matmul(PS[5][:, 0:256], lhsT=ckvnT[:, 0, :], rhs=wuq_b[:, 0, 512:768], start=True, stop=False)
                return e.matmul(PS[5][:, 0:256], lhsT=ckvnT[:, 1, :], rhs=wuq_b[:, 1, 512:768], start=False, stop=True)
            S.op("pe", mq, reads=[b_ckvnT, b_wuq], writes=[PB[4], PB[5]])
            qmf = qm[:, :, :].rearrange("p h d -> p (h d)")
            S.op("dve", lambda e: e.tensor_copy(out=qmf[:, 0:512], in_=PS[4][:, :]), reads=[PB[4]], writes=[b_qm])
            S.op("dve", lambda e: e.tensor_copy(out=qmf[:, 512:768], in_=PS[5][:, 0:256]), reads=[PB[5]], writes=[b_qm])
            S.op("dve", lambda e: e.tensor_copy(out=kpe[:, :, :], in_=qm[:, :, 128:192]), reads=[b_qm], writes=[b_kpe, b_rt])
            rope(qm[:, :, 128:192], kpe[:, :, :], 4, t, b_kpe, "dve")
            S.op("pool", lambda e: e.tensor_tensor(out=sq[:, 0:768], in0=qmf, in1=qmf, op=ALU.mult), reads=[b_qm, b_rt], writes=[b_sq, b_qm])
            S.op("dve", lambda e: e.tensor_reduce(out=st2[:, 76:80], in_=sq[:, 0:768].rearrange("p (g d) -> p g d", g=4), axis=AX.X, op=ALU.add),
                 reads=[b_sq], writes=[b_stQm])
            rstd_act(st2[:, 84:88], st2[:, 76:80], 192, st2[:, 80:84], [b_stQm], [b_stQm], b_stQm)
            S.op("pool", lambda e: e.tensor_tensor(out=qm[:, :, :], in0=qm[:, :, :], in1=st2[:, 84:88].unsqueeze(2).broadcast_to([128, 4, 192]), op=ALU.mult),
                 reads=[b_stQm], writes=[b_qm])
            S.op("pool", lambda e: e.tensor_tensor(out=km[:, :, :], in0=qm[:, :, :], in1=gqk_m[:, :].unsqueeze(1).broadcast_to([128, 4, 192]), op=ALU.mult),
                 reads=[b_qm, b_gqkm], writes=[b_km])
            S.op("pe", trm, reads=[b_km, b_identb], writes=[PB[7]])
            S.op("dve", lambda e: e.tensor_copy(out=QmTn[:, :, qc], in_=ps_bf(7)[:, 0:512].rearrange("p (h t) -> p h t", h=4)),
                 reads=[PB[7]], writes=[b_QmTn])
            S.op("act", lambda e: e.activation(out=QmTp[0:64, :, qc], in_=ps_bf(7)[0:64, 512:1024].rearrange("p (h t) -> p h t", h=4), func=AF.Copy),
                 reads=[PB[7]], writes=[b_QmTp])


    def back_s(t):
        p = t % 2
        slot = t % ST
        stile = t // ST
        kb, cb, ckvnT, b_kb, b_cb, b_ckvnT = kb2[p], cb2[p], ckvnT2[p], b_kb2[p], b_cb2[p], b_ckvnT2[p]
        if slot == ST - 1:
            c0, c1 = stile * ST * 128, (stile + 1) * ST * 128
            dma("sp", KdT_d[:, :, c0:c1].rearrange("g d t -> d g t"), stgKd[:, :, :], sem_sKd, reads=[b_sKd])
            dma("sp", KmTn_d[:, :, c0:c1].rearrange("h d t -> d h t"), stgKn[:, :, :], sem_sKn, reads=[b_sKn])
            dma("sp", KmTp_d[:, :, c0:c1].rearrange("h d t -> d h t"), stgKp[:, :, :], sem_sKp, reads=[b_sKp])
            for h in range(4):
                dma("sp", Vd_d[h, :, stile * ST:(stile + 1) * ST, :], stgVd[:, (stile % 2) * ST:(stile % 2 + 1) * ST, h, :], sem_sVd, reads=[b_sVd])
                dma("sp", Vm_d[h, :, stile * ST:(stile + 1) * ST, :], stgVm[:, :, h, :], sem_sVm, reads=[b_sVm])

    load_x(x_pad[0:128, :], 0)
    load_x(x_pad[128:256, :], 1)
    front1(0)
    if NT > 1:
        front1(1)
    front(0)
    for t in range(NT):
        f1 = S.record(lambda: front1(t + 2)) if t + 2 < NT else []
        f2 = S.record(lambda: front(t + 1)) if t + 1 < NT else []
        bk = S.record(lambda: back_k(t))
        bm = S.record(lambda: back_m(t))
        S.replay_merged(f1, f2, bk, bm)
        if t % NCORES == NCORES - 1:
            back_q(t)
        back_s(t)

    S.flush_block(nc)
    P1.close()
    S12.close()

    P3 = ExitStack()
    NKB = 3
    kc = [sb(P3, "kc%d" % i, [128, 2 * CH * 128], BF16) for i in range(NKB)]
    vc = [sb(P3, "vc%d" % i, [128, CH, VW], BF16) for i in range(NKB)]
    b_kc = [Buf("kc%d" % i) for i in range(NKB)]
    sem_kc = [dsem("kc%d" % i) for i in range(NKB)]
    NPB = 4
    Pt = [sb(P3, "Pt%d" % i, [128, GQ * 128], BF16) for i in range(NPB)]
    b_Pt = [Buf("Pt%d" % i) for i in range(NPB)]
    btmp = [sb(P3, "btmp%d" % i, [128, 128]) for i in range(2)]
    b_btmp = [Buf("btmp0"), Buf("btmp1")]
    mixed = sb(P3, "mixed", [128, GQ, D])
    b_mixed = Buf("mixed")
    o0 = sb(P3, "o0", [128, 128])
    od = sb(P3, "od", [128, 128])
    st3 = sb(P3, "st3", [128, 16])
    gb = sb(P3, "gb", [128, D])
    mg = sb(P3, "mg", [128, D], BF16)
    mgT = sb(P3, "mgT", [128, KC, 128], BF16)
    yt = sb(P3, "yt", [128, D])
    b_o0, b_od, b_st3, b_gb, b_mg, b_mgT, b_yt = [Buf(n) for n in ("o0", "od", "st3", "gb", "mg", "mgT", "yt")]
    sem_y = dsem("y")
    wout_b = sb(P3, "wout_b", [128, KC, 1024], BF16)
    b_wout = Buf("wout", True)
    Wg = sb(P3, "Wg", [128, KC, 1024], BF16)
    bWg = sb(P3, "bWg", [128, 1024])
    for k in range(KC):
        sl = k % 2
        dma("sp", xt[sl][:, :], w_in[k * 128:(k + 1) * 128, NA:INC], sem_xt[sl], writes=[b_xt[sl]])

        def mmg(e, k=k, sl=sl):
            e.matmul(PS[0][:, :], lhsT=shT[:, k, :], rhs=xt[sl][:, 0:512], start=(k == 0), stop=(k == KC - 1))
            return e.matmul(PS[1][:, :], lhsT=shT[:, k, :], rhs=xt[sl][:, 512:1024], start=(k == 0), stop=(k == KC - 1))
        S.op("pe", mmg, reads=[b_shT, b_xt[sl]], writes=PB[0:2])
        S.op("act", lambda e, k=k, sl=sl: e.activation(out=Wg[:, k, :], in_=xt[sl][:, :], func=AF.Copy, scale=A_col[:, k:k + 1]),
             reads=[b_xt[sl], b_Acol], writes=[b_Wg])
    for hf in range(2):
        S.op("dve", lambda e, hf=hf: e.tensor_copy(out=bWg[:, hf * 512:(hf + 1) * 512], in_=PS[hf][:, :]), reads=[PB[hf]], writes=[b_bWg])
    for k in range(KC):
        sl = k % 2
        dma("sp", xt[sl][:, :], w_out[k * 128:(k + 1) * 128, :], sem_xt[sl], writes=[b_xt[sl]])
        if k < 4:
            S.op("act", lambda e, k=k, sl=sl: e.activation(out=wout_b[:, k, :], in_=xt[sl][:, :], func=AF.Copy, scale=wsc[:, 0:1]),
                 reads=[b_xt[sl], b_wsc], writes=[b_wout])
        else:
            S.op("act", lambda e, k=k, sl=sl: e.activation(out=wout_b[:, k, :], in_=xt[sl][:, :], func=AF.Copy),
                 reads=[b_xt[sl]], writes=[b_wout])

    def acc_ap(jj, c):
        return PS[4 + jj][:, c * 132:c * 132 + 129]

    units, chunks = [], []
    for g in range(NG):
        jlo, jhi = g * GQ, (g + 1) * GQ
        kt_end = NCORES * (jhi - 1) + NCORES
        for u in range(8):
            is_d = u < 4
            h = u % 4
            nmap = 2 if is_d else 1
            nch = (kt_end + CH - 1) // CH
            for ci in range(nch):
                k0 = ci * CH
                nk = min(CH, kt_end - k0)
                q = len(chunks)
                chunks.append(dict(is_d=is_d, h=h, k0=k0, nk=nk, first=len(units)))
                for kl in range(nk):
                    kt = k0 + kl
                    jmin = max(jlo, (kt - (NCORES - 1) + NCORES - 1) // NCORES)
                    for c in range(nmap):
                        units.append(dict(g=g, jlo=jlo, jhi=jhi, is_d=is_d, h=h, nmap=nmap, q=q, kl=kl, kt=kt, c=c, jmin=jmin,
                                          fin=(c == nmap - 1 and kt == NCORES * jmin + NCORES - 1), last_of_group=False))
                chunks[q]["last"] = len(units) - 1
        units[-1]["last_of_group"] = True
    n_units = len(units)
    LAG = 2
    FILL = 0.0
    WARM = 20
    WARM_F = 12
    load_at = {}
    for q, chd in enumerate(chunks):
        it = 0 if q < NKB else chunks[q - NKB]["last"] + LAG + 1
        assert it <= chd["first"], (q, it, chd)
        load_at.setdefault(it, []).append(q)

    def emit_load(q):
        chd = chunks[q]
        cs, h, k0, nk = q % NKB, chd["h"], chd["k0"], chd["nk"]
        if chd["is_d"]:
            dma("sp", kc[cs][0:64, :].rearrange("p (c t) -> p c t", c=2)[:, :, 0:nk * 128],
                KdT_d[2 * h:2 * h + 2, :, k0 * 128:(k0 + nk) * 128].rearrange("c d t -> d c t"),
                sem_kc[cs], reads=[b_scr], writes=[b_kc[cs]])
            dma("sp", vc[cs][:, 0:nk, :], Vd_d[h, :, k0:k0 + nk, :], sem_kc[cs], reads=[b_scr], writes=[b_kc[cs]])
        else:
            dma("sp", kc[cs][:, 0:nk * 128], KmTn_d[h, :, k0 * 128:(k0 + nk) * 128], sem_kc[cs], reads=[b_scr], writes=[b_kc[cs]])
            dma("sp", kc[cs][0:64, CH * 128:CH * 128 + nk * 128], KmTp_d[h, :, k0 * 128:(k0 + nk) * 128], sem_kc[cs], reads=[b_scr], writes=[b_kc[cs]])
            dma("sp", vc[cs][:, 0:nk, :], Vm_d[h, :, k0:k0 + nk, :], sem_kc[cs], reads=[b_scr], writes=[b_kc[cs]])

    def warm_burst(nburst):
        def burst(e):
            ins = None
            for i in range(nburst):
                ins = e.matmul(PS[3][:, :], lhsT=ident_b[:, :], rhs=wout_b[:, i % KC, 0:512], start=True, stop=True)
            return ins
        S.op("pe", burst, reads=[b_identb, b_wout], writes=[PB[3]])

    def stage_ab(ud, it):
        if WARM and ud["kt"] == 0 and ud["c"] == 0:
            warm_burst(WARM)
        cs, h, c, kt, kl, jmin, jhi, is_d = ud["q"] % NKB, ud["h"], ud["c"], ud["kt"], ud["kl"], ud["jmin"], ud["jhi"], ud["is_d"]
        N = (jhi - jmin) * 128
        qcols = slice(jmin * 128, jhi * 128)
        kcols = slice(kl * 128, (kl + 1) * 128)
        diag_kt = NCORES * jmin + NCORES - 1
        near = kt in (diag_kt, diag_kt - 1)
        ty = 0 if kt == diag_kt else 1
        sb_i = it % 3
        pb_i = it % NPB
        if is_d:
            S.op("pe", lambda e: e.matmul(PS[sb_i][:, 0:N], lhsT=kc[cs][0:64, c * CH * 128 + kl * 128:c * CH * 128 + (kl + 1) * 128],
                                          rhs=QdT[0:64, 2 * h + c, qcols], start=True, stop=True),
                 reads=[b_kc[cs], b_QdT], writes=[PB[sb_i]])
        else:
            def mmm(e):
                e.matmul(PS[sb_i][:, 0:N], lhsT=kc[cs][:, kcols], rhs=QmTn[:, h, qcols], start=True, stop=False)
                return e.matmul(PS[sb_i][:, 0:N], lhsT=kc[cs][0:64, CH * 128 + kl * 128:CH * 128 + (kl + 1) * 128], rhs=QmTp[0:64, h, qcols], start=False, stop=True)
            S.op("pe", mmm, reads=[b_kc[cs], b_QmTn, b_QmTp], writes=[PB[sb_i]])
        if near and (is_d or ty == 0):
            bi = it % 2
            bias_ap = Bd[:, ty, h, :] if is_d else Bm[:, :]
            S.op("dve", lambda e: e.tensor_tensor(out=btmp[bi][:, :], in0=PS[sb_i][:, 0:128], in1=bias_ap, op=ALU.add),
                 reads=[PB[sb_i], b_Bd, b_Bm], writes=[b_btmp[bi]])
            S.op("act", lambda e: e.activation(out=Pt[pb_i][:, 0:128], in_=btmp[bi][:, :], func=AF.Exp),
                 reads=[b_btmp[bi]], writes=[b_Pt[pb_i]])
            if N > 128:
                S.op("act", lambda e: e.activation(out=Pt[pb_i][:, 128:N], in_=PS[sb_i][:, 128:N], func=AF.Exp),
                     reads=[PB[sb_i]], writes=[b_Pt[pb_i]])
        else:
            S.op("act", lambda e: e.activation(out=Pt[pb_i][:, 0:N], in_=PS[sb_i][:, 0:N], func=AF.Exp),
                 reads=[PB[sb_i]], writes=[b_Pt[pb_i]])

    def stage_c(ud, it):
        cs, h, c, kt, kl, jmin, jhi, jlo, is_d, nmap = (ud["q"] % NKB, ud["h"], ud["c"], ud["kt"], ud["kl"], ud["jmin"], ud["jhi"],
                                                        ud["jlo"], ud["is_d"], ud["nmap"])
        nq = jhi - jmin
        pb_i = it % NPB
        first = (kt == 0)
        flags = [(jmin + i - jlo, first and c == 0, kt == NCORES * (jmin + i) + NCORES - 1) for i in range(nq)]

        def pv(e):
            ins = None
            for i, (jj_, st_f, last) in enumerate(flags):
                ins = e.matmul(acc_ap(jj_, c), lhsT=Pt[pb_i][:, i * 128:(i + 1) * 128], rhs=vc[cs][:, kl, 0:129],
                               start=st_f, stop=last, skip_group_check=True)
            return ins
        accs_w = []
        if first:
            accs_w = [PB[4 + jj_] for jj_ in range(GQ)]
        if kt == NCORES * jmin + NCORES - 1 and PB[4 + jmin - jlo] not in accs_w:
            accs_w = accs_w + [PB[4 + jmin - jlo]]
        S.op("pe", pv, reads=[b_Pt[pb_i], b_kc[cs]], writes=accs_w)
        if FILL:
            N_ = nq * 128
            act_ns = (224 + N_) / 1.4
            pe_ns = (1 if is_d else 2) * max(N_ / 2.4, 60.0) + nq * 57.0
            nf = int(min(512, (act_ns - pe_ns) * 2.4 * FILL))
            if nf >= 64:
                S.op("pe", lambda e: e.matmul(PS[3][:, 0:nf], lhsT=ident_b[:, :], rhs=wout_b[:, 0, 0:nf], start=True, stop=True),
                     reads=[b_identb, b_wout], writes=[PB[3]])
        if ud["fin"]:
            if WARM_F:
                warm_burst(WARM_F)
            jj = jmin - jlo
            mcol = (0 if is_d else 512) + h * 128
            if is_d:
                a0, a1 = acc_ap(jj, 0), acc_ap(jj, 1)
                S.op("dve", lambda e: e.reciprocal(out=st3[:, 0:1], in_=a0[:, 128:129]), reads=[PB[4 + jj]], writes=[b_st3])
                S.op("dve", lambda e: e.reciprocal(out=st3[:, 1:2], in_=a1[:, 128:129]), reads=[PB[4 + jj]], writes=[b_st3])
                S.op("dve", lambda e: e.tensor_tensor(out=st3[:, 2:3], in0=st3[:, 1:2], in1=nlam[:, :], op=ALU.mult), reads=[b_st3, b_nlam], writes=[b_st3])
                S.op("dve", lambda e: e.tensor_scalar(out=o0[:, :], in0=a0[:, 0:128], scalar1=st3[:, 0:1], scalar2=None, op0=ALU.mult),
                     reads=[PB[4 + jj], b_st3], writes=[b_o0])
                S.op("dve", lambda e: e.scalar_tensor_tensor(out=od[:, :], in0=a1[:, 0:128], scalar=st3[:, 2:3], in1=o0[:, :], op0=ALU.mult, op1=ALU.add),
                     reads=[PB[4 + jj], b_st3, b_o0], writes=[b_od])
                S.op("act", lambda e: e.activation(out=junk[:, 0:128], in_=od[:, :], func=AF.Square, accum_out=st3[:, 3:4]), reads=[b_od], writes=[b_junk, b_st3])
                rstd_act(st3[:, 5:6], st3[:, 3:4], 128, st3[:, 4:5], [b_st3], [b_st3], b_st3)
                S.op("act", lambda e: e.activation(out=mixed[:, jj, mcol:mcol + 128], in_=od[:, :], func=AF.Copy, scale=st3[:, 5:6]),
                     reads=[b_od, b_st3], writes=[b_mixed])
            else:
                a0 = acc_ap(jj, 0)
                S.op("dve", lambda e: e.reciprocal(out=st3[:, 8:9], in_=a0[:, 128:129]), reads=[PB[4 + jj]], writes=[b_st3])
                S.op("dve", lambda e: e.tensor_scalar(out=mixed[:, jj, mcol:mcol + 128], in0=a0[:, 0:128], scalar1=st3[:, 8:9], scalar2=None, op0=ALU.mult),
                     reads=[PB[4 + jj], b_st3], writes=[b_mixed])

    def epilogue(g):
        jlo = g * GQ
        for jj in range(GQ):
            j = jlo + jj
            sl = j % 2
            norm_transpose_x(x_own[j * 128:(j + 1) * 128, :], sl)
            proj(0, 0, 512, Wg, b_Wg, 0, sl)
            proj(1, 0, 512, Wg, b_Wg, 512, sl)
            for hf in range(2):
                S.op("dve", lambda e, hf=hf: e.tensor_tensor(out=gb[:, hf * 512:(hf + 1) * 512], in0=PS[hf][:, :], in1=bWg[:, hf * 512:(hf + 1) * 512], op=ALU.add),
                     reads=[PB[hf], b_bWg], writes=[b_gb])
            S.op("act", lambda e: e.activation(out=gb[:, :], in_=gb[:, :], func=AF.Silu), reads=[], writes=[b_gb])
            S.op("pool", lambda e, jj=jj: e.tensor_tensor(out=mg[:, :], in0=gb[:, :], in1=mixed[:, jj, :], op=ALU.mult), reads=[b_gb, b_mixed], writes=[b_mg])

            def trg(e):
                ins = None
                for k in range(KC):
                    ins = e.transpose(out=ps_bf(7)[:, k * 128:(k + 1) * 128], in_=mg[:, k * 128:(k + 1) * 128], identity=ident_b[:, :])
                return ins
            S.op("pe", trg, reads=[b_mg, b_identb], writes=[PB[7]])
            S.op("dve", lambda e: e.tensor_copy(out=mgT[:, :, :].rearrange("p k t -> p (k t)"), in_=ps_bf(7)[:, :]), reads=[PB[7]], writes=[b_mgT])
            for hf in range(2):
                def mo(e, hf=hf):
                    ins = None
                    for k in range(KC):
                        ins = e.matmul(PS[2 + hf][:, :], lhsT=mgT[:, k, :], rhs=wout_b[:, k, hf * 512:(hf + 1) * 512], start=(k == 0), stop=(k == KC - 1))
                    return ins
                S.op("pe", mo, reads=[b_mgT, b_wout], writes=[PB[2 + hf]])
                S.op("dve", lambda e, hf=hf: e.tensor_tensor(out=yt[:, hf * 512:(hf + 1) * 512], in0=PS[2 + hf][:, :], in1=gate_bc[:, hf * 512:(hf + 1) * 512], op=ALU.mult),
                     reads=[PB[2 + hf], b_gate], writes=[b_yt])
            S.op("pool", lambda e, sl=sl: e.tensor_tensor(out=yt[:, :], in0=yt[:, :], in1=xt[sl][:, :], op=ALU.add), reads=[b_xt[sl]], writes=[b_yt])
            dma("sp", y_own[j * 128:(j + 1) * 128, :], yt[:, :], sem_y, reads=[b_yt])

    for it in range(n_units + LAG):
        for q in load_at.get(it, []):
            emit_load(q)
        if it < n_units:
            stage_ab(units[it], it)
        if it >= LAG:
            ud = units[it - LAG]
            stage_c(ud, it - LAG)
            if ud["last_of_group"]:
                epilogue(ud["g"])

    S.flush_block(nc)
    P3.close()
    G.close()
    sem_stack.close()
    return nc


def _t5_bucket_np(rel):
    nb, max_exact = 16, 8
    ret = np.where(rel > 0, nb, 0)
    n = np.abs(rel)
    nf = np.maximum(n, 1).astype(np.float32)
    large = max_exact + (np.log(nf / np.float32(max_exact)) / np.float32(math.log(128 / max_exact)) * np.float32(nb - max_exact)).astype(np.int32)
    large = np.minimum(large, nb - 1)
    return ret + np.where(n < max_exact, n, large)


def _constants():
    k = np.arange(128)[:, None]
    q = np.arange(128)[None, :]
    allowed = (k // 64) <= (q // 64)
    diag = np.where(allowed, _t5_bucket_np(k - q), -1)
    sub = _t5_bucket_np(k - 128 - q)
    bidx = np.concatenate([diag, sub], axis=1).astype(np.float32)
    mmask = np.where(allowed, 0.0, NEG).astype(np.float32)
    invf = (np.float32(10000.0) ** (-np.arange(32, dtype=np.float32) / np.float32(32))).astype(np.float32)[None, :]
    ident = np.eye(128, dtype=np.float32)
    return bidx, mmask, invf, ident


def make_in_maps(inp, S_len):
    NT = S_len // 128
    OT = NT // NCORES
    f = lambda a: np.ascontiguousarray(np.asarray(a, dtype=np.float32))
    x = f(inp["x"])[0]
    pos = np.asarray(inp["positions"], dtype=np.int32)[0]
    bidx, mmask, invf, ident = _constants()
    col = lambda v, k: np.ascontiguousarray(f(v).reshape(k, 128).T)
    common = {
        "c_col": col(inp["c"][0], KC), "g_col": col(inp["norm_g"][0], KC),
        "ada_w": f(inp["ada_w"][0]), "ada_b": f(inp["ada_b"][0])[None, :],
        "w_in": f(inp["w_in"][0]), "w_out": f(inp["w_out"][0]), "w_uq": f(inp["w_uq"][0]),
        "w_uk": f(inp["w_uk"][0]), "w_uv": f(inp["w_uv"][0]),
        "qa_col": col(inp["mla_q_a_norm"][0], 2), "kva_col": col(inp["mla_kv_a_norm"][0], 1),
        "sub_col": col(inp["diff_sub_norm"][0], 1),
        "dqn": f(inp["diff_q_norm"][0])[None, :], "dkn": f(inp["diff_k_norm"][0])[None, :],
        "mqn": f(inp["mla_q_norm"][0])[None, :], "mkn": f(inp["mla_k_norm"][0])[None, :],
        "lam": f(inp["diff_lambda"][0]).reshape(1, 256), "relb": f(inp["rel_bias"]).reshape(1, 128),
        "invf": invf, "ident": ident, "bidx": bidx, "mmask": mmask,
    }
    maps = []
    for c in range(NCORES):
        sh = NCORES - 1 - c
        x_pad = np.zeros((NT * 128, D), np.float32)
        x_pad[sh * 128:] = x[:(NT - sh) * 128]
        pos_pad = np.zeros((NT * 128,), np.int32)
        pos_pad[sh * 128:] = pos[:(NT - sh) * 128]
        valid = np.ones((128, NT), np.float32)
        valid[:, :sh] = 0.0
        rows = np.concatenate([np.arange((NCORES * j + c) * 128, (NCORES * j + c + 1) * 128) for j in range(OT)])
        m = dict(common)
        m.update({"x_pad": x_pad, "x_own": np.ascontiguousarray(x[rows]), "pos_pad": pos_pad.reshape(NT, 128), "valid": valid})
        maps.append(m)
    return maps


def gather_out(results, S_len):
    NT = S_len // 128
    OT = NT // NCORES
    out = np.zeros((1, S_len, D), np.float32)
    for c in range(NCORES):
        y = np.asarray(results[c]["y_own"], dtype=np.float32)
        for j in range(OT):
            g = NCORES * j + c
            out[0, g * 128:(g + 1) * 128] = y[j * 128:(j + 1) * 128]
    return out


def kernel(**inputs):
    S_len = int(np.asarray(inputs["x"]).shape[1])
    nc = build(S_len)
    in_maps = make_in_maps(inputs, S_len)
    res = run_bass_kernel_spmd(nc, in_maps, core_ids=list(range(NCORES)))
    return gather_out(res.results, S_len)
```

```python
import math
import numpy as np
import ml_dtypes
import concourse.bass as bass
import concourse.mybir as mybir
from concourse.bass_utils import run_bass_kernel_spmd

F32 = mybir.dt.float32
BF16 = mybir.dt.bfloat16
I32 = mybir.dt.int32
AF = mybir.ActivationFunctionType
ALU = mybir.AluOpType
AX = mybir.AxisListType

NCORES = 8
D = 1024
KC = 8
INC = 3008
NA = 1984
C_DQ, C_DK, C_DV, C_CQ, C_CKV, C_KR, C_G = 0, 512, 1024, 1536, 1792, 1920, 1984
VW = 130
EPS = 1e-6
NEG = -30000.0
LAM_INIT = 0.8 - 0.6 * math.exp(-0.3 * 0.0)


class Sem:
    def __init__(self, h, name):
        self.h, self.n, self.name = h, 0, name


class Buf:
    def __init__(self, name, const=False, excl=False):
        self.name, self.w, self.r, self.const, self.excl = name, None, {}, const, excl


class Eng:
    def __init__(self, name, sem):
        self.name, self.sem, self.seen, self.ops = name, sem, {}, []


class Sched:
    def __init__(self):
        self.eng = {}
        self.dma_sems = []
        self.capture = None

    def record(self, f):
        self.capture = []
        f()
        ops, self.capture = self.capture, None
        return ops

    def replay_merged(self, *streams):
        streams = [st for st in streams if st]
        idx = [0] * len(streams)
        while True:
            best, bf = None, None
            for k, st in enumerate(streams):
                if idx[k] < len(st):
                    f = idx[k] / len(st)
                    if bf is None or f < bf:
                        best, bf = k, f
            if best is None:
                break
            self.op(*streams[best][idx[best]])
            idx[best] += 1

    def add_engine(self, name, semh):
        self.eng[name] = Eng(name, Sem(semh, name))

    def new_dma_sem(self, semh, name):
        s = Sem(semh, name)
        self.dma_sems.append(s)
        return s

    def op(self, eng, fn, reads=(), writes=(), dma=None):
        if self.capture is not None:
            self.capture.append((eng, fn, tuple(reads), tuple(writes), dma))
            return None
        E = self.eng[eng]
        waits = {}
        writes = list(writes) + [b for b in reads if b.excl and b not in writes]
        reads = [b for b in reads if not b.excl]

        def need(tok):
            if tok is None:
                return
            s, v = tok
            if waits.get(s, 0) < v:
                waits[s] = v
        for b in reads:
            need(b.w)
        for b in writes:
            need(b.w)
            for s, v in b.r.items():
                need((s, v))
        wl = []
        for s, v in waits.items():
            if eng == "pe" and s is E.sem:
                continue
            if E.seen.get(s, 0) < v:
                E.seen[s] = v
                wl.append((s.h, v))
        if dma is None:
            E.sem.n += 1
            tok = (E.sem, E.sem.n)
            inc = 1
        else:
            dma.n += 16
            tok = (dma, dma.n)
            inc = 16
        semh = tok[0].h

        def emit(e, wl=wl, fn=fn, semh=semh, inc=inc):
            for h, v in wl:
                e.wait_ge(h, v)
            fn(e).then_inc(semh, inc)
        E.ops.append(emit)
        for b in reads:
            if not b.const:
                if b.r.get(tok[0], 0) < tok[1]:
                    b.r[tok[0]] = tok[1]
        for b in writes:
            b.w = tok
            b.r = {}
        return tok

    def wait_all_dma(self, eng):
        E = self.eng[eng]
        wl = []
        for s in self.dma_sems:
            if s.n > 0 and E.seen.get(s, 0) < s.n:
                E.seen[s] = s.n
                wl.append((s.h, s.n))

        def emit(e, wl=wl):
            for h, v in wl:
                e.wait_ge(h, v)
        E.ops.append(emit)

    def flush_block(self, nc):
        self.wait_all_dma("sp")
        self.wait_all_dma("pool")
        with nc.Block() as block:
            for name, deco in (("sp", block.sync), ("pe", block.tensor), ("act", block.scalar),
                               ("dve", block.vector), ("pool", block.gpsimd)):
                ops = self.eng[name].ops
                if not ops:
                    continue

                def body(e, ops=ops):
                    for f in ops:
                        f(e)
                deco(body)
                self.eng[name].ops = []
        for E in self.eng.values():
            for E2 in self.eng.values():
                E.seen[E2.sem] = E2.sem.n
            for s in self.dma_sems:
                E.seen[s] = s.n


def bcast_rows(ap1n, parts=128):
    n = ap1n.shape[-1]
    return bass.AP(ap1n.tensor, ap1n.offset, [[0, parts], [1, n]])


def build(S_len):
    NT = S_len // 128
    OT = NT // NCORES
    GQ = min(4, OT)
    NG = OT // GQ
    CH = min(16, NT)
    ST = 2
    NST = NT // ST
    SP = NT * 128

    nc = bass.Bass("TRN2", target_bir_lowering=False)
    S = Sched()

    def din(name, shape, dt=F32):
        return nc.dram_tensor(name, list(shape), dt, kind="ExternalInput").ap()

    x_pad = din("x_pad", [SP, D])
    x_own = din("x_own", [OT * 128, D])
    pos_pad = din("pos_pad", [NT, 128], I32)
    valid_in = din("valid", [128, NT])
    c_col_in = din("c_col", [128, KC])
    g_col_in = din("g_col", [128, KC])
    ada_w = din("ada_w", [D, 3072])
    ada_b = din("ada_b", [1, 3072])
    w_in = din("w_in", [D, INC])
    w_out = din("w_out", [D, D])
    w_uq = din("w_uq", [256, 768])
    w_uk = din("w_uk", [128, 512])
    w_uv = din("w_uv", [128, 512])
    qa_col_in = din("qa_col", [128, 2])
    kva_col_in = din("kva_col", [128, 1])
    sub_col_in = din("sub_col", [128, 1])
    dqn_in = din("dqn", [1, 64])
    dkn_in = din("dkn", [1, 64])
    mqn_in = din("mqn", [1, 192])
    mkn_in = din("mkn", [1, 192])
    lam_in = din("lam", [1, 256])
    relb_in = din("relb", [1, 128])
    invf_in = din("invf", [1, 32])
    ident_in = din("ident", [128, 128])
    idx_in = din("bidx", [128, 256])
    mmask_in = din("mmask", [128, 128])
    y_own = nc.dram_tensor("y_own", [OT * 128, D], F32, kind="ExternalOutput").ap()

    KdT_d = nc.dram_tensor("KdT_d", [8, 64, SP], BF16).ap()
    KmTn_d = nc.dram_tensor("KmTn_d", [4, 128, SP], BF16).ap()
    KmTp_d = nc.dram_tensor("KmTp_d", [4, 64, SP], BF16).ap()
    Vd_d = nc.dram_tensor("Vd_d", [4, 128, NT, VW], BF16).ap()
    Vm_d = nc.dram_tensor("Vm_d", [4, 128, NT, VW], BF16).ap()

    from contextlib import ExitStack
    sem_stack = ExitStack()
    for en in ("pe", "act", "dve", "pool", "sp"):
        S.add_engine(en, sem_stack.enter_context(nc.semaphore("sem_" + en)))

    def dsem(name):
        return S.new_dma_sem(sem_stack.enter_context(nc.semaphore("d_" + name)), name)

    def sb(stack, name, shape, dt=F32):
        return stack.enter_context(nc.sbuf_tensor("s_" + name, list(shape), dt))

    def dma(q, out, in_, sem, reads=(), writes=()):
        return S.op(q, lambda e: e.dma_start(out=out, in_=in_), reads=reads, writes=writes, dma=sem)

    G = ExitStack()
    PS = [G.enter_context(nc.psum_tensor("ps%d" % i, [128, 512], F32)) for i in range(8)]
    PB = [Buf("psb%d" % i, excl=True) for i in range(8)]

    def ps_bf(i):
        return PS[i][:, :].bitcast(BF16)

    A_col = sb(G, "A_col", [128, KC])
    shT = sb(G, "shT", [128, KC, 128])
    gate_bc = sb(G, "gate_bc", [128, 1024])
    QW = max(OT * 128, 2048)
    QdT = sb(G, "QdT", [128, 8, QW], BF16)
    QmTn = sb(G, "QmTn", [128, 4, QW], BF16)
    QmTp = sb(G, "QmTp", [128, 4, QW], BF16)
    wsc = sb(G, "wsc", [128, 1])
    b_wsc = Buf("wsc", True)
    QdT_f = QdT[:, :, :].rearrange("p a b -> p (a b)").bitcast(F32)
    QmTn_f = QmTn[:, :, :].rearrange("p a b -> p (a b)").bitcast(F32)
    QmTp_f = QmTp[:, :, :].rearrange("p a b -> p (a b)").bitcast(F32)
    ident_f = sb(G, "ident_f", [128, 128])
    ident_b = sb(G, "ident_b", [128, 128], BF16)
    Bd = sb(G, "Bd", [128, 2, 4, 128])
    Bm = sb(G, "Bm", [128, 128])
    nlam = sb(G, "nlam", [128, 1])
    eps_c = sb(G, "eps_c", [128, 1])
    xt = [sb(G, "xt%d" % i, [128, D]) for i in range(2)]
    xs2 = [sb(G, "xs0", [128, D], BF16)] * 3
    xT2 = [sb(G, "xT%d" % i, [128, KC, 128], BF16) for i in range(3)]
    junk = sb(G, "junk", [128, 256], BF16)
    st1 = sb(G, "st1", [128, 12])

    b_Wg, b_bWg, b_gate = Buf("Wg"), Buf("bWg"), Buf("gate", True)
    b_QdT, b_QmTn, b_QmTp = Buf("QdT"), Buf("QmTn"), Buf("QmTp")
    b_identf, b_identb, b_Bd, b_Bm, b_nlam, b_eps = (Buf("idf", True), Buf("idb", True), Buf("Bd", True),
                                                      Buf("Bm", True), Buf("nlam", True), Buf("eps", True))
    b_xt = [Buf("xt0"), Buf("xt1")]
    b_xs2, b_xT2, b_junk, b_st1_2 = [Buf("xs0")] * 3, [Buf("xT0"), Buf("xT1"), Buf("xT2")], Buf("junk"), [Buf("st1a"), Buf("st1b"), Buf("st1c")]
    sem_xt = [dsem("xt0"), dsem("xt1")]
    class _Fresh:
        cnt = 0
    def fresh_sem():
        _Fresh.cnt += 1
        return dsem("m%d" % _Fresh.cnt)

    def rstd_act(out_ap, in_ap, n, tmp_ap, bufs_r, bufs_w, b_tmp):
        S.op("act", lambda e: e.activation(out=tmp_ap, in_=in_ap, func=AF.Ln, scale=1.0 / n, bias=eps_c[:, 0:1]),
             reads=list(bufs_r) + [b_eps], writes=[b_tmp])
        S.op("act", lambda e: e.activation(out=out_ap, in_=tmp_ap, func=AF.Exp, scale=-0.5),
             reads=[b_tmp], writes=list(bufs_w))

    def load_x(src_ap, slot, q="sp"):
        dma(q, xt[slot][:, :], src_ap, sem_xt[slot], writes=[b_xt[slot]])

    def norm_transpose_x(src_ap, slot, load=True, after_read=None, tslot=None):
        if tslot is None:
            tslot = slot
        xs, xT, b_xs, b_xT, b_st1 = xs2[tslot], xT2[tslot], b_xs2[tslot], b_xT2[tslot], b_st1_2[tslot]
        o = tslot * 4
        if load:
            load_x(src_ap, slot)
        S.op("act", lambda e: e.activation(out=xs[:, :], in_=xt[slot][:, :], func=AF.Square, accum_out=st1[:, o:o + 1]),
             reads=[b_xt[slot]], writes=[b_xs, b_st1])
        rstd_act(st1[:, o + 2:o + 3], st1[:, o:o + 1], D, st1[:, o + 1:o + 2], [b_st1], [b_st1], b_st1)
        S.op("act", lambda e: e.activation(out=xs[:, :], in_=xt[slot][:, :], func=AF.Copy, scale=st1[:, o + 2:o + 3]),
             reads=[b_xt[slot], b_st1], writes=[b_xs])
        if after_read is not None:
            after_read()

        def tr(e):
            ins = None
            for k in range(KC):
                ins = e.transpose(out=ps_bf(0)[:, k * 128:(k + 1) * 128], in_=xs[:, k * 128:(k + 1) * 128], identity=ident_b[:, :])
            return ins
        S.op("pe", tr, reads=[b_xs, b_identb], writes=[PB[0]])
        S.op("dve", lambda e: e.tensor_copy(out=xT[:, :, :].rearrange("p k t -> p (k t)"), in_=ps_bf(0)[:, :]),
             reads=[PB[0]], writes=[b_xT])

    def proj(bank, c0, ncols, W, b_W, wc0, slot):
        xT, b_xT = xT2[slot], b_xT2[slot]

        def mm(e):
            ins = None
            for k in range(KC):
                ins = e.matmul(PS[bank][:, c0:c0 + ncols], lhsT=xT[:, k, :], rhs=W[:, k, wc0:wc0 + ncols],
                               start=(k == 0), stop=(k == KC - 1))
            return ins
        S.op("pe", mm, reads=[b_xT, b_W], writes=[PB[bank]])

    S12 = ExitStack()
    Wa = sb(S12, "Wa", [128, KC, NA], BF16)
    bWa = sb(S12, "bWa", [128, NA])
    wuq_b = sb(S12, "wuq_b", [128, 2, 768], BF16)
    wuk_b = sb(S12, "wuk_b", [128, 512], BF16)
    wuv_b = sb(S12, "wuv_b", [128, 512], BF16)
    gqk_d = sb(S12, "gqk_d", [128, 64])
    gqk_m = sb(S12, "gqk_m", [128, 192])
    cosT = sb(S12, "cosT", [128, NT, 32])
    sinT = sb(S12, "sinT", [128, NT, 32])
    valid = sb(S12, "valid_sb", [128, NT])
    b_Wa, b_bWa, b_wuq, b_wuk, b_wuv = Buf("Wa", True), Buf("bWa", True), Buf("wuq", True), Buf("wuk", True), Buf("wuv", True)
    b_gqkd, b_gqkm, b_cos, b_sin, b_valid = Buf("gqkd", True), Buf("gqkm", True), Buf("cos", True), Buf("sin", True), Buf("valid", True)

    SU = ExitStack()
    class _V:
        def __init__(self, ap):
            self.ap = ap
        def __getitem__(self, k):
            return self.ap[k]
    wbuf = [_V(QmTn_f[:, 0:3072]), _V(QdT_f[:, 0:3072])]
    b_wbuf = [Buf("wbuf0"), Buf("wbuf1")]
    sem_wbuf = [dsem("wbuf0"), dsem("wbuf1")]
    mod_bc = _V(QdT_f[:, 3072:6144])
    adab_bc = _V(QmTp_f[:, 0:3072])
    c_col = sb(SU, "c_col", [128, KC])
    sc_col = sb(SU, "sc_col", [128, KC])
    scb = sb(SU, "scb", [128, KC, 128])
    g_col = sb(SU, "g_col", [128, KC])
    scl_col = sb(SU, "scl_col", [128, KC])
    small = sb(SU, "small", [128, 16])
    vec = sb(SU, "vec", [128, 1024])
    idx = sb(SU, "idx", [128, 256])
    oh = sb(SU, "oh", [128, 256])
    relb = sb(SU, "relb", [128, 128])
    rbs = sb(SU, "rbs", [128, 128])
    posi = sb(SU, "posi", [128, 128], I32)
    posf = sb(SU, "posf", [128, 128])
    posT = sb(SU, "posT", [128, NT])
    invf = sb(SU, "invf", [128, 32])
    ang = _V(QdT_f[:, 0:NT * 32])
    kf = _V(QdT_f[:, 4096:4096 + NT * 32])
    ki = _V(QmTn[:, :, :].rearrange("p a b -> p (a b)").bitcast(I32)[:, 0:NT * 32])
    r2 = _V(QmTp_f[:, 0:NT * 32])
    b_mod, b_adab, b_ccol, b_sccol, b_scb, b_gcol, b_Acol, b_shT, b_sclcol = [Buf(n) for n in
        ("mod", "adab", "ccol", "sccol", "scb", "gcol", "Acol", "shT", "sclcol")]
    b_small, b_vec, b_idx, b_oh, b_relb, b_rbs = [Buf(n) for n in ("small", "vec", "idx", "oh", "relb", "rbs")]
    b_posi, b_posf, b_posT, b_invf, b_ang, b_kf, b_ki, b_r2 = [Buf(n) for n in
        ("posi", "posf", "posT", "invf", "ang", "kf", "ki", "r2")]

    dma("sp", ident_f[:, :], ident_in, fresh_sem(), writes=[b_identf])
    dma("sp", c_col[:, :], c_col_in, fresh_sem(), writes=[b_ccol])
    dma("sp", g_col[:, :], g_col_in, fresh_sem(), writes=[b_gcol])
    dma("sp", adab_bc[:, :], bcast_rows(ada_b), fresh_sem(), writes=[b_adab])
    dma("sp", idx[:, :], idx_in, fresh_sem(), writes=[b_idx])
    dma("sp", Bm[:, :], mmask_in, fresh_sem(), writes=[b_Bm])
    dma("sp", relb[:, :], bcast_rows(relb_in), fresh_sem(), writes=[b_relb])
    dma("sp", valid[:, :], valid_in, fresh_sem(), writes=[b_valid])
    dma("sp", invf[:, :], bcast_rows(invf_in), fresh_sem(), writes=[b_invf])
    dma("sp", posi[0:NT, :], pos_pad, fresh_sem(), writes=[b_posi])
    S.op("dve", lambda e: e.memset(eps_c[:, :], EPS), writes=[b_eps])
    S.op("dve", lambda e: e.tensor_copy(out=ident_b[:, :], in_=ident_f[:, :]), reads=[b_identf], writes=[b_identb])

    S.op("act", lambda e: e.activation(out=sc_col[:, :], in_=c_col[:, :], func=AF.Silu), reads=[b_ccol], writes=[b_sccol])
    S.op("dve", lambda e: e.tensor_copy(out=scb[:, :, :], in_=sc_col[:, :].unsqueeze(2).broadcast_to([128, KC, 128])),
         reads=[b_sccol], writes=[b_scb])
    for k in range(KC):
        sl = k % 2
        dma("sp", wbuf[sl][:, :], ada_w[k * 128:(k + 1) * 128, :], sem_wbuf[sl], writes=[b_wbuf[sl]])

        def mm(e, k=k, sl=sl):
            ins = None
            for j in range(6):
                ins = e.matmul(PS[j][:, :], lhsT=scb[:, k, :], rhs=wbuf[sl][:, j * 512:(j + 1) * 512],
                               start=(k == 0), stop=(k == KC - 1))
            return ins
        S.op("pe", mm, reads=[b_scb, b_wbuf[sl]], writes=PB[0:6])
    for j in range(6):
        S.op("dve", lambda e, j=j: e.tensor_tensor(out=mod_bc[:, j * 512:(j + 1) * 512], in0=PS[j][:, :],
                                                   in1=adab_bc[:, j * 512:(j + 1) * 512], op=ALU.add),
             reads=[PB[j], b_adab], writes=[b_mod])
    S.op("dve", lambda e: e.tensor_copy(out=gate_bc[:, :], in_=mod_bc[:, 2048:3072]), reads=[b_mod], writes=[b_gate])
    for k in range(KC):
        bank = 6 + (k % 2)
        S.op("pe", lambda e, k=k, bank=bank: e.transpose(out=PS[bank][:, 0:128], in_=mod_bc[:, k * 128:(k + 1) * 128], identity=ident_f[:, :]),
             reads=[b_mod, b_identf], writes=[PB[bank]])
        S.op("dve", lambda e, k=k, bank=bank: e.tensor_copy(out=shT[:, k, :], in_=PS[bank][:, 0:128]), reads=[PB[bank]], writes=[b_shT])
        S.op("pe", lambda e, k=k, bank=bank: e.transpose(out=PS[bank][:, 128:256], in_=mod_bc[:, 1024 + k * 128:1024 + (k + 1) * 128], identity=ident_f[:, :]),
             reads=[b_mod, b_identf], writes=[PB[bank]])
        S.op("dve", lambda e, k=k, bank=bank: e.tensor_copy(out=scl_col[:, k:k + 1], in_=PS[bank][:, 128:129]), reads=[PB[bank]], writes=[b_sclcol])
    S.op("dve", lambda e: e.scalar_tensor_tensor(out=A_col[:, :], in0=scl_col[:, :], scalar=1.0, in1=g_col[:, :], op0=ALU.add, op1=ALU.mult),
         reads=[b_sclcol, b_gcol], writes=[b_Acol])

    for k in range(KC):
        sl = k % 2
        dma("sp", wbuf[sl][:, 0:NA], w_in[k * 128:(k + 1) * 128, 0:NA], sem_wbuf[sl], writes=[b_wbuf[sl]])

        def mm(e, k=k, sl=sl):
            ins = None
            for j in range(4):
                w = min(512, NA - j * 512)
                ins = e.matmul(PS[j][:, 0:w], lhsT=shT[:, k, :], rhs=wbuf[sl][:, j * 512:j * 512 + w],
                               start=(k == 0), stop=(k == KC - 1))
            return ins
        S.op("pe", mm, reads=[b_shT, b_wbuf[sl]], writes=PB[0:4])
        S.op("act", lambda e, k=k, sl=sl: e.activation(out=Wa[:, k, :], in_=wbuf[sl][:, 0:NA], func=AF.Copy, scale=A_col[:, k:k + 1]),
             reads=[b_wbuf[sl], b_Acol], writes=[b_Wa])
    for j in range(4):
        w = min(512, NA - j * 512)
        S.op("dve", lambda e, j=j, w=w: e.tensor_copy(out=bWa[:, j * 512:j * 512 + w], in_=PS[j][:, 0:w]), reads=[PB[j]], writes=[b_bWa])

    dma("sp", small[:, 0:1], sub_col_in, fresh_sem(), writes=[b_small])
    dma("sp", small[:, 1:3], qa_col_in, fresh_sem(), writes=[b_small])
    dma("sp", small[:, 3:4], kva_col_in, fresh_sem(), writes=[b_small])
    S.op("dve", lambda e: e.tensor_scalar(out=small[:, 4:5], in0=small[:, 0:1], scalar1=1.0 - LAM_INIT, scalar2=None, op0=ALU.mult),
         reads=[b_small], writes=[b_small])
    S.op("dve", lambda e: e.tensor_copy(out=wsc[:, :], in_=small[:, 4:5]), reads=[b_small], writes=[b_wsc])
    for k in range(2):
        dma("sp", wbuf[k][:, 0:768], w_uq[k * 128:(k + 1) * 128, :], sem_wbuf[k], writes=[b_wbuf[k]])
        S.op("act", lambda e, k=k: e.activation(out=wuq_b[:, k, :], in_=wbuf[k][:, 0:768], func=AF.Copy, scale=small[:, 1 + k:2 + k]),
             reads=[b_wbuf[k], b_small], writes=[b_wuq])
    dma("sp", wbuf[0][:, 0:512], w_uk, sem_wbuf[0], writes=[b_wbuf[0]])
    S.op("act", lambda e: e.activation(out=wuk_b[:, :], in_=wbuf[0][:, 0:512], func=AF.Copy, scale=small[:, 3:4]),
         reads=[b_wbuf[0], b_small], writes=[b_wuk])
    dma("sp", wbuf[1][:, 0:512], w_uv, sem_wbuf[1], writes=[b_wbuf[1]])
    S.op("act", lambda e: e.activation(out=wuv_b[:, :], in_=wbuf[1][:, 0:512], func=AF.Copy, scale=small[:, 3:4]),
         reads=[b_wbuf[1], b_small], writes=[b_wuv])

    dma("sp", vec[:, 0:64], bcast_rows(dqn_in), fresh_sem(), writes=[b_vec])
    dma("sp", vec[:, 64:128], bcast_rows(dkn_in), fresh_sem(), writes=[b_vec])
    dma("sp", vec[:, 128:320], bcast_rows(mqn_in), fresh_sem(), writes=[b_vec])
    dma("sp", vec[:, 320:512], bcast_rows(mkn_in), fresh_sem(), writes=[b_vec])
    dma("sp", vec[:, 512:768], bcast_rows(lam_in), fresh_sem(), writes=[b_vec])
    S.op("dve", lambda e: e.scalar_tensor_tensor(out=gqk_d[:, :], in0=vec[:, 0:64], scalar=64 ** -0.5, in1=vec[:, 64:128], op0=ALU.mult, op1=ALU.mult),
         reads=[b_vec], writes=[b_gqkd])
    S.op("dve", lambda e: e.scalar_tensor_tensor(out=gqk_m[:, :], in0=vec[:, 128:320], scalar=192 ** -0.5, in1=vec[:, 320:512], op0=ALU.mult, op1=ALU.mult),
         reads=[b_vec], writes=[b_gqkm])
    S.op("dve", lambda e: e.tensor_tensor(out=vec[:, 768:832], in0=vec[:, 512:576], in1=vec[:, 576:640], op=ALU.mult), reads=[b_vec], writes=[b_vec])
    S.op("dve", lambda e: e.tensor_tensor(out=vec[:, 832:896], in0=vec[:, 640:704], in1=vec[:, 704:768], op=ALU.mult), reads=[b_vec], writes=[b_vec])
    S.op("dve", lambda e: e.tensor_reduce(out=small[:, 8:10], in_=vec[:, 768:896].rearrange("p (a d) -> p a d", a=2), axis=AX.X, op=ALU.add),
         reads=[b_vec], writes=[b_small])
    S.op("act", lambda e: e.activation(out=small[:, 10:12], in_=small[:, 8:10], func=AF.Exp), reads=[b_small], writes=[b_small])
    S.op("dve", lambda e: e.scalar_tensor_tensor(out=nlam[:, :], in0=small[:, 11:12], scalar=-LAM_INIT, in1=small[:, 10:11], op0=ALU.add, op1=ALU.subtract),
         reads=[b_small], writes=[b_nlam])

    for h in range(4):
        S.op("dve", lambda e, h=h: e.tensor_scalar(out=rbs[:, h * 32:(h + 1) * 32], in0=relb[:, :].rearrange("p (b h) -> p h b", h=4)[:, h, :],
                                                   scalar1=relb[:, 15 * 4 + h:15 * 4 + h + 1], scalar2=None, op0=ALU.subtract),
             reads=[b_relb], writes=[b_rbs])
    S.op("pool", lambda e: e.memset(Bd[:, :, :, :], 0.0), writes=[b_Bd])
    for b in list(range(0, 32)) + [-1]:
        S.op("dve", lambda e, b=b: e.tensor_scalar(out=oh[:, :], in0=idx[:, :], scalar1=float(b), scalar2=None, op0=ALU.is_equal),
             reads=[b_idx], writes=[b_oh])
        for h in range(4):
            for ty in range(2):
                if b == -1:
                    S.op("dve", lambda e, h=h, ty=ty: e.scalar_tensor_tensor(out=Bd[:, ty, h, :], in0=oh[:, ty * 128:(ty + 1) * 128], scalar=NEG,
                                                                             in1=Bd[:, ty, h, :], op0=ALU.mult, op1=ALU.add),
                         reads=[b_oh], writes=[b_Bd])
                else:
                    S.op("dve", lambda e, h=h, ty=ty, b=b: e.scalar_tensor_tensor(out=Bd[:, ty, h, :], in0=oh[:, ty * 128:(ty + 1) * 128],
                                                                                  scalar=rbs[:, h * 32 + b:h * 32 + b + 1],
                                                                                  in1=Bd[:, ty, h, :], op0=ALU.mult, op1=ALU.add),
                         reads=[b_oh, b_rbs], writes=[b_Bd])

    S.flush_block(nc)
    S.op("dve", lambda e: e.tensor_copy(out=posf[0:NT, :], in_=posi[0:NT, :]), reads=[b_posi], writes=[b_posf])
    S.op("pe", lambda e: e.transpose(out=PS[7][:, 0:NT], in_=posf[0:NT, :], identity=ident_f[0:NT, 0:NT]), reads=[b_posf, b_identf], writes=[PB[7]])
    S.op("dve", lambda e: e.tensor_copy(out=posT[:, :], in_=PS[7][:, 0:NT]), reads=[PB[7]], writes=[b_posT])
    ang3 = ang[:, :].rearrange("p (t f) -> p t f", f=32)
    S.op("dve", lambda e: e.tensor_tensor(out=ang3, in0=posT[:, :].unsqueeze(2).broadcast_to([128, NT, 32]),
                                          in1=invf[:, :].unsqueeze(1).broadcast_to([128, NT, 32]), op=ALU.mult),
         reads=[b_posT, b_invf], writes=[b_ang])
    TWO_PI = 2.0 * math.pi
    C1 = 6.28125
    C2 = float(np.float32(0.0019350051879882812))
    C3 = float(np.float32(TWO_PI - C1 - C2))

    def reduce_sin(dst3, shift):
        S.op("dve", lambda e: e.tensor_scalar(out=kf[:, :], in0=ang[:, :], scalar1=1.0 / TWO_PI, scalar2=None, op0=ALU.mult), reads=[b_ang], writes=[b_kf])
        S.op("dve", lambda e: e.tensor_copy(out=ki[:, :], in_=kf[:, :]), reads=[b_kf], writes=[b_ki])
        S.op("dve", lambda e: e.tensor_copy(out=kf[:, :], in_=ki[:, :]), reads=[b_ki], writes=[b_kf])
        S.op("dve", lambda e: e.scalar_tensor_tensor(out=r2[:, :], in0=kf[:, :], scalar=-C1, in1=ang[:, :], op0=ALU.mult, op1=ALU.add), reads=[b_kf, b_ang], writes=[b_r2])
        S.op("dve", lambda e: e.scalar_tensor_tensor(out=r2[:, :], in0=kf[:, :], scalar=-C2, in1=r2[:, :], op0=ALU.mult, op1=ALU.add), reads=[b_kf], writes=[b_r2])
        S.op("dve", lambda e: e.scalar_tensor_tensor(out=r2[:, :], in0=kf[:, :], scalar=-C3, in1=r2[:, :], op0=ALU.mult, op1=ALU.add), reads=[b_kf], writes=[b_r2])
        if shift != 0.0:
            S.op("dve", lambda e: e.tensor_scalar(out=r2[:, :], in0=r2[:, :], scalar1=shift, scalar2=None, op0=ALU.add), reads=[], writes=[b_r2])
        S.op("dve", lambda e: e.tensor_scalar(out=kf[:, :], in0=r2[:, :], scalar1=math.pi, scalar2=-TWO_PI, op0=ALU.is_gt, op1=ALU.mult), reads=[b_r2], writes=[b_kf])
        S.op("dve", lambda e: e.tensor_tensor(out=r2[:, :], in0=r2[:, :], in1=kf[:, :], op=ALU.add), reads=[b_kf], writes=[b_r2])
        S.op("dve", lambda e: e.tensor_scalar(out=kf[:, :], in0=r2[:, :], scalar1=-math.pi, scalar2=TWO_PI, op0=ALU.is_lt, op1=ALU.mult), reads=[b_r2], writes=[b_kf])
        S.op("dve", lambda e: e.tensor_tensor(out=r2[:, :], in0=r2[:, :], in1=kf[:, :], op=ALU.add), reads=[b_kf], writes=[b_r2])
        S.op("dve", lambda e: e.tensor_scalar(out=r2[:, :], in0=r2[:, :], scalar1=-3.1415925, scalar2=3.1415925, op0=ALU.max, op1=ALU.min), reads=[], writes=[b_r2])
        S.op("act", lambda e: e.activation(out=dst3, in_=r2[:, :].rearrange("p (t f) -> p t f", f=32), func=AF.Sin), reads=[b_r2], writes=[b_sin, b_cos])
    reduce_sin(sinT[:, :, :], 0.0)
    reduce_sin(cosT[:, :, :], math.pi / 2)

    S.flush_block(nc)
    SU.close()

    P1 = ExitStack()
    kb2 = [sb(P1, "kb%d" % i, [128, 512]) for i in range(2)]
    cb2 = [sb(P1, "cb%d" % i, [128, 256]) for i in range(2)]
    ckvnT2 = [sb(P1, "ckvnT%d" % i, [128, 2, 128], BF16) for i in range(2)]
    qb = sb(P1, "qb", [128, 512])
    sq = sb(P1, "sq", [128, 768])
    kn = sb(P1, "kn", [128, 8, 64], BF16)
    ckvn = sb(P1, "ckvn", [128, 256], BF16)
    ckvn_f = sb(P1, "ckvn_f", [128, 128], BF16)
    kpe = sb(P1, "kpe", [128, 4, 64])
    rt = sb(P1, "rt", [128, 4, 4, 32])
    km = sb(P1, "km", [128, 4, 192], BF16)
    qm = sb(P1, "qm", [128, 4, 192])
    st2 = sb(P1, "st2", [128, 96])
    stgKd = sb(P1, "stgKd", [64, 8, ST * 128], BF16)
    stgKn = sb(P1, "stgKn", [128, 4, ST * 128], BF16)
    stgKp = sb(P1, "stgKp", [64, 4, ST * 128], BF16)
    stgVd = sb(P1, "stgVd", [128, 2 * ST, 4, VW], BF16)
    stgVm = sb(P1, "stgVm", [128, ST, 4, VW], BF16)
    b_kb2, b_cb2, b_ckvnT2 = [Buf("kb0"), Buf("kb1")], [Buf("cb0"), Buf("cb1")], [Buf("ckT0"), Buf("ckT1")]
    b_qb, b_sq, b_kn, b_ckvn, b_kpe, b_rt, b_km, b_qm = [Buf(n) for n in ("qb", "sq", "kn", "ckvn", "kpe", "rt", "km", "qm")]
    b_ckvn_f = Buf("ckvn_f")
    b_stK, b_stC, b_stM, b_stQ, b_stQc, b_stQm = [Buf(n) for n in ("stK", "stC", "stM", "stQ", "stQc", "stQm")]
    b_sKd, b_sKn, b_sKp, b_sVd, b_sVm = [Buf(n) for n in ("sKd", "sKn", "sKp", "sVd", "sVm")]
    sem_sKd, sem_sKn, sem_sKp, sem_sVd, sem_sVm = [dsem(n) for n in ("sKd", "sKn", "sKp", "sVd", "sVm")]
    b_scr = Buf("scratch")
    S.op("pool", lambda e: e.memset(stgVd[:, :, :, :], 0.0), writes=[b_sVd])
    S.op("pool", lambda e: e.memset(stgVm[:, :, :, :], 0.0), writes=[b_sVm])
    S.op("pool", lambda e: e.memset(stgVd[:, :, :, 128:129], 1.0), writes=[b_sVd])
    S.op("pool", lambda e: e.memset(stgVm[:, :, :, 128:129], 1.0), writes=[b_sVm])

    def group_norm(src_ap, ngroups, gsz, base, b_src, b_st):
        n = ngroups * gsz
        S.op("pool", lambda e: e.tensor_tensor(out=sq[:, 0:n], in0=src_ap, in1=src_ap, op=ALU.mult), reads=[b_src], writes=[b_sq])
        S.op("dve", lambda e: e.tensor_reduce(out=st2[:, base:base + ngroups], in_=sq[:, 0:n].rearrange("p (g d) -> p g d", g=ngroups), axis=AX.X, op=ALU.add),
             reads=[b_sq], writes=[b_st])
        rstd_act(st2[:, base + 16:base + 16 + ngroups], st2[:, base:base + ngroups], gsz, st2[:, base + 8:base + 8 + ngroups], [b_st], [b_st], b_st)

    def rope(dst3, src3, nh, t, b_src, eng):
        cs = cosT[:, t, :].unsqueeze(1).broadcast_to([128, nh, 32])
        sn = sinT[:, t, :].unsqueeze(1).broadcast_to([128, nh, 32])
        x1, x2 = src3[:, :, 0:32], src3[:, :, 32:64]
        S.op(eng, lambda e: e.tensor_tensor(out=rt[:, 0:nh, 0, :], in0=x1, in1=cs, op=ALU.mult), reads=[b_cos, b_src], writes=[b_rt])
        S.op(eng, lambda e: e.tensor_tensor(out=rt[:, 0:nh, 1, :], in0=x2, in1=sn, op=ALU.mult), reads=[b_sin, b_src], writes=[b_rt])
        S.op(eng, lambda e: e.tensor_tensor(out=rt[:, 0:nh, 2, :], in0=x2, in1=cs, op=ALU.mult), reads=[b_cos, b_src], writes=[b_rt])
        S.op(eng, lambda e: e.tensor_tensor(out=rt[:, 0:nh, 3, :], in0=x1, in1=sn, op=ALU.mult), reads=[b_sin, b_src], writes=[b_rt])
        S.op(eng, lambda e: e.tensor_tensor(out=dst3[:, :, 0:32], in0=rt[:, 0:nh, 0, :], in1=rt[:, 0:nh, 1, :], op=ALU.subtract), reads=[b_rt], writes=[b_rt])
        S.op(eng, lambda e: e.tensor_tensor(out=dst3[:, :, 32:64], in0=rt[:, 0:nh, 2, :], in1=rt[:, 0:nh, 3, :], op=ALU.add), reads=[b_rt], writes=[b_rt])

    def trk(e):
        ins = None
        for g in range(8):
            ins = e.transpose(out=ps_bf(6)[0:64, g * 128:(g + 1) * 128], in_=kn[:, g, :], identity=ident_b[:, :])
        return ins

    def trm(e):
        ins = None
        for h in range(4):
            e.transpose(out=ps_bf(7)[:, h * 128:(h + 1) * 128], in_=km[:, h, 0:128], identity=ident_b[:, :])
            ins = e.transpose(out=ps_bf(7)[0:64, 512 + h * 128:512 + (h + 1) * 128], in_=km[:, h, 128:192], identity=ident_b[:, :])
        return ins

    NDUM = NCORES - 1

    def front1(t):
        p = t % 2
        nxt = (lambda: load_x(x_pad[(t + 2) * 128:(t + 3) * 128, :], p, "act")) if t + 2 < NT else None
        norm_transpose_x(None, p, load=False, after_read=nxt, tslot=t % 3)

    def front(t):
        p = t % 2
        slot = t % ST
        vslot = t % (2 * ST)
        kb, cb, ckvnT, b_kb, b_cb, b_ckvnT = kb2[p], cb2[p], ckvnT2[p], b_kb2[p], b_cb2[p], b_ckvnT2[p]
        proj(1, 0, 512, Wa, b_Wa, C_DK, t % 3)
        S.op("dve", lambda e: e.tensor_tensor(out=kb[:, :], in0=PS[1][:, :], in1=bWa[:, C_DK:C_DK + 512], op=ALU.add), reads=[PB[1], b_bWa], writes=[b_kb])
        proj(2, 0, 512, Wa, b_Wa, C_DV, t % 3)
        S.op("dve", lambda e: e.tensor_tensor(out=stgVd[:, vslot, :, 0:128], in0=PS[2][:, :].rearrange("p (h d) -> p h d", h=4),
                                              in1=bWa[:, C_DV:C_DV + 512].rearrange("p (h d) -> p h d", h=4), op=ALU.add),
             reads=[PB[2], b_bWa], writes=[b_sVd])
        if t < NDUM:
            S.op("dve", lambda e: e.tensor_scalar(out=stgVd[:, vslot, :, 0:128], in0=stgVd[:, vslot, :, 0:128], scalar1=valid[:, t:t + 1], scalar2=None, op0=ALU.mult),
                 reads=[b_valid], writes=[b_sVd])
            S.op("dve", lambda e: e.tensor_copy(out=stgVd[:, vslot, :, 128:129], in_=valid[:, t:t + 1].unsqueeze(1).broadcast_to([128, 4, 1])),
                 reads=[b_valid], writes=[b_sVd])
        elif t < NDUM + 2 * ST:
            S.op("dve", lambda e: e.memset(stgVd[:, vslot, :, 128:129], 1.0), writes=[b_sVd])
        proj(3, 0, 192, Wa, b_Wa, C_CKV, t % 3)
        S.op("dve", lambda e: e.tensor_tensor(out=cb[:, 0:192], in0=PS[3][:, 0:192], in1=bWa[:, C_CKV:C_CKV + 192], op=ALU.add), reads=[PB[3], b_bWa], writes=[b_cb])
        S.op("act", lambda e: e.activation(out=junk[:, 0:128], in_=cb[:, 0:128], func=AF.Square, accum_out=st2[:, 24:25]), reads=[b_cb], writes=[b_junk, b_stC])
        rstd_act(st2[:, 26:27], st2[:, 24:25], 128, st2[:, 25:26], [b_stC], [b_stC], b_stC)
        S.op("act", lambda e: e.activation(out=ckvn_f[:, 0:128], in_=cb[:, 0:128], func=AF.Copy, scale=st2[:, 26:27]), reads=[b_cb, b_stC], writes=[b_ckvn_f])
        S.op("pe", lambda e: e.transpose(out=ps_bf(3)[:, 512:640], in_=ckvn_f[:, 0:128], identity=ident_b[:, :]), reads=[b_ckvn_f, b_identb], writes=[PB[3]])
        S.op("dve", lambda e: e.tensor_copy(out=ckvnT[:, 0, :], in_=ps_bf(3)[:, 512:640]), reads=[PB[3]], writes=[b_ckvnT])

    def back_k(t):
        p = t % 2
        slot = t % ST
        stile = t // ST
        kb, cb, ckvnT, b_kb, b_cb, b_ckvnT = kb2[p], cb2[p], ckvnT2[p], b_kb2[p], b_cb2[p], b_ckvnT2[p]
        group_norm(kb[:, :], 8, 64, 0, b_kb, b_stK)
        S.op("pool", lambda e: e.tensor_tensor(out=kn[:, :, :], in0=kb[:, :].rearrange("p (g d) -> p g d", g=8),
                                               in1=st2[:, 16:24].unsqueeze(2).broadcast_to([128, 8, 64]), op=ALU.mult),
             reads=[b_kb, b_stK], writes=[b_kn])
        S.op("pe", trk, reads=[b_kn, b_identb], writes=[PB[6]])
        S.op("dve", lambda e: e.tensor_copy(out=stgKd[:, :, slot * 128:(slot + 1) * 128], in_=ps_bf(6)[0:64, :].rearrange("p (g t) -> p g t", g=8)),
             reads=[PB[6]], writes=[b_sKd])

    def back_m(t):
        p = t % 2
        slot = t % ST
        stile = t // ST
        kb, cb, ckvnT, b_kb, b_cb, b_ckvnT = kb2[p], cb2[p], ckvnT2[p], b_kb2[p], b_cb2[p], b_ckvnT2[p]
        S.op("pe", lambda e: e.matmul(PS[4][:, :], lhsT=ckvnT[:, 0, :], rhs=wuk_b[:, :], start=True, stop=True), reads=[b_ckvnT, b_wuk], writes=[PB[4]])
        S.op("pe", lambda e: e.matmul(PS[5][:, :], lhsT=ckvnT[:, 0, :], rhs=wuv_b[:, :], start=True, stop=True), reads=[b_ckvnT, b_wuv], writes=[PB[5]])
        if t < NDUM:
            S.op("act", lambda e: e.activation(out=stgVm[:, slot, :, 0:128], in_=PS[5][:, :].rearrange("p (h d) -> p h d", h=4),
                                               func=AF.Copy, scale=valid[:, t:t + 1]),
                 reads=[PB[5], b_valid], writes=[b_sVm])
            S.op("dve", lambda e: e.tensor_copy(out=stgVm[:, slot, :, 128:129], in_=valid[:, t:t + 1].unsqueeze(1).broadcast_to([128, 4, 1])),
                 reads=[b_valid], writes=[b_sVm])
        else:
            S.op("act", lambda e: e.activation(out=stgVm[:, slot, :, 0:128], in_=PS[5][:, :].rearrange("p (h d) -> p h d", h=4), func=AF.Copy),
                 reads=[PB[5]], writes=[b_sVm])
            if t < NDUM + ST:
                S.op("dve", lambda e: e.memset(stgVm[:, slot, :, 128:129], 1.0), writes=[b_sVm])
        rope(kpe[:, 0:1, :], cb[:, 128:192].unsqueeze(1), 1, t, b_cb, "dve")
        S.op("act", lambda e: e.activation(out=junk[:, 0:64], in_=kpe[:, 0, :], func=AF.Square, accum_out=st2[:, 27:28]), reads=[b_rt], writes=[b_junk, b_stM])
        for h in range(4):
            S.op("act", lambda e, h=h: e.activation(out=junk[:, 0:128], in_=PS[4][:, h * 128:(h + 1) * 128], func=AF.Square, accum_out=st2[:, 28 + h:29 + h]),
                 reads=[PB[4]], writes=[b_junk, b_stM])
        S.op("dve", lambda e: e.tensor_scalar(out=st2[:, 32:36], in0=st2[:, 28:32], scalar1=st2[:, 27:28], scalar2=None, op0=ALU.add), reads=[b_stM], writes=[b_stM])
        rstd_act(st2[:, 40:44], st2[:, 32:36], 192, st2[:, 36:40], [b_stM], [b_stM], b_stM)
        S.op("dve", lambda e: e.tensor_tensor(out=km[:, :, 0:128], in0=PS[4][:, :].rearrange("p (h d) -> p h d", h=4),
                                              in1=st2[:, 40:44].unsqueeze(2).broadcast_to([128, 4, 128]), op=ALU.mult),
             reads=[PB[4], b_stM], writes=[b_km])
        S.op("dve", lambda e: e.tensor_tensor(out=km[:, :, 128:192], in0=kpe[:, 0:1, :].broadcast_to([128, 4, 64]),
                                              in1=st2[:, 40:44].unsqueeze(2).broadcast_to([128, 4, 64]), op=ALU.mult),
             reads=[b_rt, b_stM], writes=[b_km])
        S.op("pe", trm, reads=[b_km, b_identb], writes=[PB[7]])
        S.op("dve", lambda e: e.tensor_copy(out=stgKn[:, :, slot * 128:(slot + 1) * 128], in_=ps_bf(7)[:, 0:512].rearrange("p (h t) -> p h t", h=4)),
             reads=[PB[7]], writes=[b_sKn])
        S.op("act", lambda e: e.activation(out=stgKp[:, :, slot * 128:(slot + 1) * 128], in_=ps_bf(7)[0:64, 512:1024].rearrange("p (h t) -> p h t", h=4), func=AF.Copy),
             reads=[PB[7]], writes=[b_sKp])


    def back_q(t):
        p = t % 2
        slot = t % ST
        stile = t // ST
        kb, cb, ckvnT, b_kb, b_cb, b_ckvnT = kb2[p], cb2[p], ckvnT2[p], b_kb2[p], b_cb2[p], b_ckvnT2[p]
        if t % NCORES == NCORES - 1:
            j = t // NCORES
            qc = slice(j * 128, (j + 1) * 128)
            proj(4, 0, 512, Wa, b_Wa, C_DQ, t % 3)
            S.op("dve", lambda e: e.tensor_tensor(out=qb[:, :], in0=PS[4][:, :], in1=bWa[:, C_DQ:C_DQ + 512], op=ALU.add), reads=[PB[4], b_bWa], writes=[b_qb])
            group_norm(qb[:, :], 8, 64, 48, b_qb, b_stQ)
            S.op("pool", lambda e: e.tensor_tensor(out=qb[:, :].rearrange("p (g d) -> p g d", g=8), in0=qb[:, :].rearrange("p (g d) -> p g d", g=8),
                                                   in1=st2[:, 64:72].unsqueeze(2).broadcast_to([128, 8, 64]), op=ALU.mult),
                 reads=[b_stQ], writes=[b_qb])
            S.op("pool", lambda e: e.tensor_tensor(out=kn[:, :, :], in0=qb[:, :].rearrange("p (g d) -> p g d", g=8),
                                                   in1=gqk_d[:, :].unsqueeze(1).broadcast_to([128, 8, 64]), op=ALU.mult),
                 reads=[b_qb, b_gqkd], writes=[b_kn])
            S.op("pe", trk, reads=[b_kn, b_identb], writes=[PB[6]])
            S.op("dve", lambda e: e.tensor_copy(out=QdT[0:64, :, qc], in_=ps_bf(6)[0:64, :].rearrange("p (g t) -> p g t", g=8)),
                 reads=[PB[6]], writes=[b_QdT])
            proj(5, 0, 256, Wa, b_Wa, C_CQ, t % 3)
            S.op("dve", lambda e: e.tensor_tensor(out=qb[:, 0:256], in0=PS[5][:, 0:256], in1=bWa[:, C_CQ:C_CQ + 256], op=ALU.add), reads=[PB[5], b_bWa], writes=[b_qb])
            S.op("act", lambda e: e.activation(out=junk[:, 0:256], in_=qb[:, 0:256], func=AF.Square, accum_out=st2[:, 72:73]), reads=[b_qb], writes=[b_junk, b_stQc])
            rstd_act(st2[:, 74:75], st2[:, 72:73], 256, st2[:, 73:74], [b_stQc], [b_stQc], b_stQc)
            S.op("act", lambda e: e.activation(out=ckvn[:, :], in_=qb[:, 0:256], func=AF.Copy, scale=st2[:, 74:75]), reads=[b_qb, b_stQc], writes=[b_ckvn])

            def trq(e):
                e.transpose(out=ps_bf(6)[:, 0:128], in_=ckvn[:, 0:128], identity=ident_b[:, :])
                return e.transpose(out=ps_bf(6)[:, 128:256], in_=ckvn[:, 128:256], identity=ident_b[:, :])
            S.op("pe", trq, reads=[b_ckvn, b_identb], writes=[PB[6]])
            S.op("dve", lambda e: e.tensor_copy(out=ckvnT[:, :, :], in_=ps_bf(6)[:, 0:256].rearrange("p (k t) -> p k t", k=2)), reads=[PB[6]], writes=[b_ckvnT])

            def mq(e):
                e.matmul(PS[4][:, :], lhsT=ckvnT[:, 0, :], rhs=wuq_b[:, 0, 0:512], start=True, stop=False)
                e.matmul(PS[4][:, :], lhsT=ckvnT[:, 1, :], rhs=wuq_b[:, 1, 0:512], start=False, stop=True)
                e.matmul(PS[5][:, 0:256], lhsT=ckvnT[:, 0, :], rhs=wuq_b[:, 0, 512:768], start=True, stop=False)
                return e.matmul(PS[5][:, 0:256], lhsT=ckvnT[:, 1, :], rhs=wuq_b[:, 1, 512:768], start=False, stop=True)
            S.op("pe", mq, reads=[b_ckvnT, b_wuq], writes=[PB[4], PB[5]])
            qmf = qm[:, :, :].rearrange("p h d -> p (h d)")
            S.op("dve", lambda e: e.tensor_copy(out=qmf[:, 0:512], in_=PS[4][:, :]), reads=[PB[4]], writes=[b_qm])
            S.op("dve", lambda e: e.tensor_copy(out=qmf[:, 512:768], in_=PS[5][:, 0:256]), reads=[PB[5]], writes=[b_qm])
            S.op("dve", lambda e: e.tensor_copy(out=kpe[:, :, :], in_=qm[:, :, 128:192]), reads=[b_qm], writes=[b_kpe, b_rt])
            rope(qm[:, :, 128:192], kpe[:, :, :], 4, t, b_kpe, "dve")
            S.op("pool", lambda e: e.tensor_tensor(out=sq[:, 0:768], in0=qmf, in1=qmf, op=ALU.mult), reads=[b_qm, b_rt], writes=[b_sq, b_qm])
            S.op("dve", lambda e: e.tensor_reduce(out=st2[:, 76:80], in_=sq[:, 0:768].rearrange("p (g d) -> p g d", g=4), axis=AX.X, op=ALU.add),
                 reads=[b_sq], writes=[b_stQm])
            rstd_act(st2[:, 84:88], st2[:, 76:80], 192, st2[:, 80:84], [b_stQm], [b_stQm], b_stQm)
            S.op("pool", lambda e: e.tensor_tensor(out=qm[:, :, :], in0=qm[:, :, :], in1=st2[:, 84:88].unsqueeze(2).broadcast_to([128, 4, 192]), op=ALU.mult),
                 reads=[b_stQm], writes=[b_qm])
            S.op("pool", lambda e: e.tensor_tensor(out=km[:, :, :], in0=qm[:, :, :], in1=gqk_m[:, :].unsqueeze(1).broadcast_to([128, 4, 192]), op=ALU.mult),
                 reads=[b_qm, b_gqkm], writes=[b_km])
            S.op("pe", trm, reads=[b_km, b_identb], writes=[PB[7]])
            S.op("dve", lambda e: e.tensor_copy(out=QmTn[:, :, qc], in_=ps_bf(7)[:, 0:512].rearrange("p (h t) -> p h t", h=4)),
                 reads=[PB[7]], writes=[b_QmTn])
            S.op("act", lambda e: e.activation(out=QmTp[0:64, :, qc], in_=ps_bf(7)[0:64, 512:1024].rearrange("p (h t) -> p h t", h=4), func=AF.Copy),
                 reads=[PB[7]], writes=[b_QmTp])


    def back_s(t):
        p = t % 2
        slot = t % ST
        stile = t // ST
        kb, cb, ckvnT, b_kb, b_cb, b_ckvnT = kb2[p], cb2[p], ckvnT2[p], b_kb2[p], b_cb2[p], b_ckvnT2[p]
        if slot == ST - 1:
            c0, c1 = stile * ST * 128, (stile + 1) * ST * 128
            dma("sp", KdT_d[:, :, c0:c1].rearrange("g d t -> d g t"), stgKd[:, :, :], sem_sKd, reads=[b_sKd])
            dma("sp", KmTn_d[:, :, c0:c1].rearrange("h d t -> d h t"), stgKn[:, :, :], sem_sKn, reads=[b_sKn])
            dma("sp", KmTp_d[:, :, c0:c1].rearrange("h d t -> d h t"), stgKp[:, :, :], sem_sKp, reads=[b_sKp])
            for h in range(4):
                dma("sp", Vd_d[h, :, stile * ST:(stile + 1) * ST, :], stgVd[:, (stile % 2) * ST:(stile % 2 + 1) * ST, h, :], sem_sVd, reads=[b_sVd])
                dma("sp", Vm_d[h, :, stile * ST:(stile + 1) * ST, :], stgVm[:, :, h, :], sem_sVm, reads=[b_sVm])

    load_x(x_pad[0:128, :], 0)
    load_x(x_pad[128:256, :], 1)
    front1(0)
    if NT > 1:
        front1(1)
    front(0)
    for t in range(NT):
        f1 = S.record(lambda: front1(t + 2)) if t + 2 < NT else []
        f2 = S.record(lambda: front(t + 1)) if t + 1 < NT else []
        bk = S.record(lambda: back_k(t))
        bm = S.record(lambda: back_m(t))
        S.replay_merged(f1, f2, bk, bm)
        if t % NCORES == NCORES - 1:
            back_q(t)
        back_s(t)

    S.flush_block(nc)
    P1.close()
    S12.close()

    P3 = ExitStack()
    NKB = 3
    kc = [sb(P3, "kc%d" % i, [128, 2 * CH * 128], BF16) for i in range(NKB)]
    vc = [sb(P3, "vc%d" % i, [128, CH, VW], BF16) for i in range(NKB)]
    b_kc = [Buf("kc%d" % i) for i in range(NKB)]
    sem_kc = [dsem("kc%d" % i) for i in range(NKB)]
    NPB = 4
    Pt = [sb(P3, "Pt%d" % i, [128, GQ * 128], BF16) for i in range(NPB)]
    b_Pt = [Buf("Pt%d" % i) for i in range(NPB)]
    btmp = [sb(P3, "btmp%d" % i, [128, 128]) for i in range(2)]
    b_btmp = [Buf("btmp0"), Buf("btmp1")]
    mixed = sb(P3, "mixed", [128, GQ, D])
    b_mixed = Buf("mixed")
    o0 = sb(P3, "o0", [128, 128])
    od = sb(P3, "od", [128, 128])
    st3 = sb(P3, "st3", [128, 16])
    gb = sb(P3, "gb", [128, D])
    mg = sb(P3, "mg", [128, D], BF16)
    mgT = sb(P3, "mgT", [128, KC, 128], BF16)
    yt = sb(P3, "yt", [128, D])
    b_o0, b_od, b_st3, b_gb, b_mg, b_mgT, b_yt = [Buf(n) for n in ("o0", "od", "st3", "gb", "mg", "mgT", "yt")]
    sem_y = dsem("y")
    wout_b = sb(P3, "wout_b", [128, KC, 1024], BF16)
    b_wout = Buf("wout", True)
    Wg = sb(P3, "Wg", [128, KC, 1024], BF16)
    bWg = sb(P3, "bWg", [128, 1024])
    for k in range(KC):
        sl = k % 2
        dma("sp", xt[sl][:, :], w_in[k * 128:(k + 1) * 128, NA:INC], sem_xt[sl], writes=[b_xt[sl]])

        def mmg(e, k=k, sl=sl):
            e.matmul(PS[0][:, :], lhsT=shT[:, k, :], rhs=xt[sl][:, 0:512], start=(k == 0), stop=(k == KC - 1))
            return e.matmul(PS[1][:, :], lhsT=shT[:, k, :], rhs=xt[sl][:, 512:1024], start=(k == 0), stop=(k == KC - 1))
        S.op("pe", mmg, reads=[b_shT, b_xt[sl]], writes=PB[0:2])
        S.op("act", lambda e, k=k, sl=sl: e.activation(out=Wg[:, k, :], in_=xt[sl][:, :], func=AF.Copy, scale=A_col[:, k:k + 1]),
             reads=[b_xt[sl], b_Acol], writes=[b_Wg])
    for hf in range(2):
        S.op("dve", lambda e, hf=hf: e.tensor_copy(out=bWg[:, hf * 512:(hf + 1) * 512], in_=PS[hf][:, :]), reads=[PB[hf]], writes=[b_bWg])
    for k in range(KC):
        sl = k % 2
        dma("sp", xt[sl][:, :], w_out[k * 128:(k + 1) * 128, :], sem_xt[sl], writes=[b_xt[sl]])
        if k < 4:
            S.op("act", lambda e, k=k, sl=sl: e.activation(out=wout_b[:, k, :], in_=xt[sl][:, :], func=AF.Copy, scale=wsc[:, 0:1]),
                 reads=[b_xt[sl], b_wsc], writes=[b_wout])
        else:
            S.op("act", lambda e, k=k, sl=sl: e.activation(out=wout_b[:, k, :], in_=xt[sl][:, :], func=AF.Copy),
                 reads=[b_xt[sl]], writes=[b_wout])

    def acc_ap(jj, c):
        return PS[4 + jj][:, c * 132:c * 132 + 129]

    units, chunks = [], []
    for g in range(NG):
        jlo, jhi = g * GQ, (g + 1) * GQ
        kt_end = NCORES * (jhi - 1) + NCORES
        for u in range(8):
            is_d = u < 4
            h = u % 4
            nmap = 2 if is_d else 1
            nch = (kt_end + CH - 1) // CH
            for ci in range(nch):
                k0 = ci * CH
                nk = min(CH, kt_end - k0)
                q = len(chunks)
                chunks.append(dict(is_d=is_d, h=h, k0=k0, nk=nk, first=len(units)))
                for kl in range(nk):
                    kt = k0 + kl
                    jmin = max(jlo, (kt - (NCORES - 1) + NCORES - 1) // NCORES)
                    for c in range(nmap):
                        units.append(dict(g=g, jlo=jlo, jhi=jhi, is_d=is_d, h=h, nmap=nmap, q=q, kl=kl, kt=kt, c=c, jmin=jmin,
                                          fin=(c == nmap - 1 and kt == NCORES * jmin + NCORES - 1), last_of_group=False))
                chunks[q]["last"] = len(units) - 1
        units[-1]["last_of_group"] = True
    n_units = len(units)
    LAG = 2
    FILL = 0.0
    WARM = 20
    WARM_F = 12
    load_at = {}
    for q, chd in enumerate(chunks):
        it = 0 if q < NKB else chunks[q - NKB]["last"] + LAG + 1
        assert it <= chd["first"], (q, it, chd)
        load_at.setdefault(it, []).append(q)

    def emit_load(q):
        chd = chunks[q]
        cs, h, k0, nk = q % NKB, chd["h"], chd["k0"], chd["nk"]
        if chd["is_d"]:
            dma("sp", kc[cs][0:64, :].rearrange("p (c t) -> p c t", c=2)[:, :, 0:nk * 128],
                KdT_d[2 * h:2 * h + 2, :, k0 * 128:(k0 + nk) * 128].rearrange("c d t -> d c t"),
                sem_kc[cs], reads=[b_scr], writes=[b_kc[cs]])
            dma("sp", vc[cs][:, 0:nk, :], Vd_d[h, :, k0:k0 + nk, :], sem_kc[cs], reads=[b_scr], writes=[b_kc[cs]])
        else:
            dma("sp", kc[cs][:, 0:nk * 128], KmTn_d[h, :, k0 * 128:(k0 + nk) * 128], sem_kc[cs], reads=[b_scr], writes=[b_kc[cs]])
            dma("sp", kc[cs][0:64, CH * 128:CH * 128 + nk * 128], KmTp_d[h, :, k0 * 128:(k0 + nk) * 128], sem_kc[cs], reads=[b_scr], writes=[b_kc[cs]])
            dma("sp", vc[cs][:, 0:nk, :], Vm_d[h, :, k0:k0 + nk, :], sem_kc[cs], reads=[b_scr], writes=[b_kc[cs]])

    def warm_burst(nburst):
        def burst(e):
            ins = None
            for i in range(nburst):
                ins = e.matmul(PS[3][:, :], lhsT=ident_b[:, :], rhs=wout_b[:, i % KC, 0:512], start=True, stop=True)
            return ins
        S.op("pe", burst, reads=[b_identb, b_wout], writes=[PB[3]])

    def stage_ab(ud, it):
        if WARM and ud["kt"] == 0 and ud["c"] == 0:
            warm_burst(WARM)
        cs, h, c, kt, kl, jmin, jhi, is_d = ud["q"] % NKB, ud["h"], ud["c"], ud["kt"], ud["kl"], ud["jmin"], ud["jhi"], ud["is_d"]
        N = (jhi - jmin) * 128
        qcols = slice(jmin * 128, jhi * 128)
        kcols = slice(kl * 128, (kl + 1) * 128)
        diag_kt = NCORES * jmin + NCORES - 1
        near = kt in (diag_kt, diag_kt - 1)
        ty = 0 if kt == diag_kt else 1
        sb_i = it % 3
        pb_i = it % NPB
        if is_d:
            S.op("pe", lambda e: e.matmul(PS[sb_i][:, 0:N], lhsT=kc[cs][0:64, c * CH * 128 + kl * 128:c * CH * 128 + (kl + 1) * 128],
                                          rhs=QdT[0:64, 2 * h + c, qcols], start=True, stop=True),
                 reads=[b_kc[cs], b_QdT], writes=[PB[sb_i]])
        else:
            def mmm(e):
                e.matmul(PS[sb_i][:, 0:N], lhsT=kc[cs][:, kcols], rhs=QmTn[:, h, qcols], start=True, stop=False)
                return e.matmul(PS[sb_i][:, 0:N], lhsT=kc[cs][0:64, CH * 128 + kl * 128:CH * 128 + (kl + 1) * 128], rhs=QmTp[0:64, h, qcols], start=False, stop=True)
            S.op("pe", mmm, reads=[b_kc[cs], b_QmTn, b_QmTp], writes=[PB[sb_i]])
        if near and (is_d or ty == 0):
            bi = it % 2
            bias_ap = Bd[:, ty, h, :] if is_d else Bm[:, :]
            S.op("dve", lambda e: e.tensor_tensor(out=btmp[bi][:, :], in0=PS[sb_i][:, 0:128], in1=bias_ap, op=ALU.add),
                 reads=[PB[sb_i], b_Bd, b_Bm], writes=[b_btmp[bi]])
            S.op("act", lambda e: e.activation(out=Pt[pb_i][:, 0:128], in_=btmp[bi][:, :], func=AF.Exp),
                 reads=[b_btmp[bi]], writes=[b_Pt[pb_i]])
            if N > 128:
                S.op("act", lambda e: e.activation(out=Pt[pb_i][:, 128:N], in_=PS[sb_i][:, 128:N], func=AF.Exp),
                     reads=[PB[sb_i]], writes=[b_Pt[pb_i]])
        else:
            S.op("act", lambda e: e.activation(out=Pt[pb_i][:, 0:N], in_=PS[sb_i][:, 0:N], func=AF.Exp),
                 reads=[PB[sb_i]], writes=[b_Pt[pb_i]])

    def stage_c(ud, it):
        cs, h, c, kt, kl, jmin, jhi, jlo, is_d, nmap = (ud["q"] % NKB, ud["h"], ud["c"], ud["kt"], ud["kl"], ud["jmin"], ud["jhi"],
                                                        ud["jlo"], ud["is_d"], ud["nmap"])
        nq = jhi - jmin
        pb_i = it % NPB
        first = (kt == 0)
        flags = [(jmin + i - jlo, first and c == 0, kt == NCORES * (jmin + i) + NCORES - 1) for i in range(nq)]

        def pv(e):
            ins = None
            for i, (jj_, st_f, last) in enumerate(flags):
                ins = e.matmul(acc_ap(jj_, c), lhsT=Pt[pb_i][:, i * 128:(i + 1) * 128], rhs=vc[cs][:, kl, 0:129],
                               start=st_f, stop=last, skip_group_check=True)
            return ins
        accs_w = []
        if first:
            accs_w = [PB[4 + jj_] for jj_ in range(GQ)]
        if kt == NCORES * jmin + NCORES - 1 and PB[4 + jmin - jlo] not in accs_w:
            accs_w = accs_w + [PB[4 + jmin - jlo]]
        S.op("pe", pv, reads=[b_Pt[pb_i], b_kc[cs]], writes=accs_w)
        if FILL:
            N_ = nq * 128
            act_ns = (224 + N_) / 1.4
            pe_ns = (1 if is_d else 2) * max(N_ / 2.4, 60.0) + nq * 57.0
            nf = int(min(512, (act_ns - pe_ns) * 2.4 * FILL))
            if nf >= 64:
                S.op("pe", lambda e: e.matmul(PS[3][:, 0:nf], lhsT=ident_b[:, :], rhs=wout_b[:, 0, 0:nf], start=True, stop=True),
                     reads=[b_identb, b_wout], writes=[PB[3]])
        if ud["fin"]:
            if WARM_F:
                warm_burst(WARM_F)
            jj = jmin - jlo
            mcol = (0 if is_d else 512) + h * 128
            if is_d:
                a0, a1 = acc_ap(jj, 0), acc_ap(jj, 1)
                S.op("dve", lambda e: e.reciprocal(out=st3[:, 0:1], in_=a0[:, 128:129]), reads=[PB[4 + jj]], writes=[b_st3])
                S.op("dve", lambda e: e.reciprocal(out=st3[:, 1:2], in_=a1[:, 128:129]), reads=[PB[4 + jj]], writes=[b_st3])
                S.op("dve", lambda e: e.tensor_tensor(out=st3[:, 2:3], in0=st3[:, 1:2], in1=nlam[:, :], op=ALU.mult), reads=[b_st3, b_nlam], writes=[b_st3])
                S.op("dve", lambda e: e.tensor_scalar(out=o0[:, :], in0=a0[:, 0:128], scalar1=st3[:, 0:1], scalar2=None, op0=ALU.mult),
                     reads=[PB[4 + jj], b_st3], writes=[b_o0])
                S.op("dve", lambda e: e.scalar_tensor_tensor(out=od[:, :], in0=a1[:, 0:128], scalar=st3[:, 2:3], in1=o0[:, :], op0=ALU.mult, op1=ALU.add),
                     reads=[PB[4 + jj], b_st3, b_o0], writes=[b_od])
                S.op("act", lambda e: e.activation(out=junk[:, 0:128], in_=od[:, :], func=AF.Square, accum_out=st3[:, 3:4]), reads=[b_od], writes=[b_junk, b_st3])
                rstd_act(st3[:, 5:6], st3[:, 3:4], 128, st3[:, 4:5], [b_st3], [b_st3], b_st3)
                S.op("act", lambda e: e.activation(out=mixed[:, jj, mcol:mcol + 128], in_=od[:, :], func=AF.Copy, scale=st3[:, 5:6]),
                     reads=[b_od, b_st3], writes=[b_mixed])
            else:
                a0 = acc_ap(jj, 0)
                S.op("dve", lambda e: e.reciprocal(out=st3[:, 8:9], in_=a0[:, 128:129]), reads=[PB[4 + jj]], writes=[b_st3])
                S.op("dve", lambda e: e.tensor_scalar(out=mixed[:, jj, mcol:mcol + 128], in0=a0[:, 0:128], scalar1=st3[:, 8:9], scalar2=None, op0=ALU.mult),
                     reads=[PB[4 + jj], b_st3], writes=[b_mixed])

    def epilogue(g):
        jlo = g * GQ
        for jj in range(GQ):
            j = jlo + jj
            sl = j % 2
            norm_transpose_x(x_own[j * 128:(j + 1) * 128, :], sl)
            proj(0, 0, 512, Wg, b_Wg, 0, sl)
            proj(1, 0, 512, Wg, b_Wg, 512, sl)
            for hf in range(2):
                S.op("dve", lambda e, hf=hf: e.tensor_tensor(out=gb[:, hf * 512:(hf + 1) * 512], in0=PS[hf][:, :], in1=bWg[:, hf * 512:(hf + 1) * 512], op=ALU.add),
                     reads=[PB[hf], b_bWg], writes=[b_gb])
            S.op("act", lambda e: e.activation(out=gb[:, :], in_=gb[:, :], func=AF.Silu), reads=[], writes=[b_gb])
            S.op("pool", lambda e, jj=jj: e.tensor_tensor(out=mg[:, :], in0=gb[:, :], in1=mixed[:, jj, :], op=ALU.mult), reads=[b_gb, b_mixed], writes=[b_mg])

            def trg(e):
                ins = None
                for k in range(KC):
                    ins = e.transpose(out=ps_bf(7)[:, k * 128:(k + 1) * 128], in_=mg[:, k * 128:(k + 1) * 128], identity=ident_b[:, :])
                return ins
            S.op("pe", trg, reads=[b_mg, b_identb], writes=[PB[7]])
            S.op("dve", lambda e: e.tensor_copy(out=mgT[:, :, :].rearrange("p k t -> p (k t)"), in_=ps_bf(7)[:, :]), reads=[PB[7]], writes=[b_mgT])
            for hf in range(2):
                def mo(e, hf=hf):
                    ins = None
                    for k in range(KC):
                        ins = e.matmul(PS[2 + hf][:, :], lhsT=mgT[:, k, :], rhs=wout_b[:, k, hf * 512:(hf + 1) * 512], start=(k == 0), stop=(k == KC - 1))
                    return ins
                S.op("pe", mo, reads=[b_mgT, b_wout], writes=[PB[2 + hf]])
                S.op("dve", lambda e, hf=hf: e.tensor_tensor(out=yt[:, hf * 512:(hf + 1) * 512], in0=PS[2 + hf][:, :], in1=gate_bc[:, hf * 512:(hf + 1) * 512], op=ALU.mult),
                     reads=[PB[2 + hf], b_gate], writes=[b_yt])
            S.op("pool", lambda e, sl=sl: e.tensor_tensor(out=yt[:, :], in0=yt[:, :], in1=xt[sl][:, :], op=ALU.add), reads=[b_xt[sl]], writes=[b_yt])
            dma("sp", y_own[j * 128:(j + 1) * 128, :], yt[:, :], sem_y, reads=[b_yt])

    for it in range(n_units + LAG):
        for q in load_at.get(it, []):
            emit_load(q)
        if it < n_units:
            stage_ab(units[it], it)
        if it >= LAG:
            ud = units[it - LAG]
            stage_c(ud, it - LAG)
            if ud["last_of_group"]:
                epilogue(ud["g"])

    S.flush_block(nc)
    P3.close()
    G.close()
    sem_stack.close()
    return nc


def _t5_bucket_np(rel):
    nb, max_exact = 16, 8
    ret = np.where(rel > 0, nb, 0)
    n = np.abs(rel)
    nf = np.maximum(n, 1).astype(np.float32)
    large = max_exact + (np.log(nf / np.float32(max_exact)) / np.float32(math.log(128 / max_exact)) * np.float32(nb - max_exact)).astype(np.int32)
    large = np.minimum(large, nb - 1)
    return ret + np.where(n < max_exact, n, large)


def _constants():
    k = np.arange(128)[:, None]
    q = np.arange(128)[None, :]
    allowed = (k // 64) <= (q // 64)
    diag = np.where(allowed, _t5_bucket_np(k - q), -1)
    sub = _t5_bucket_np(k - 128 - q)
    bidx = np.concatenate([diag, sub], axis=1).astype(np.float32)
    mmask = np.where(allowed, 0.0, NEG).astype(np.float32)
    invf = (np.float32(10000.0) ** (-np.arange(32, dtype=np.float32) / np.float32(32))).astype(np.float32)[None, :]
    ident = np.eye(128, dtype=np.float32)
    return bidx, mmask, invf, ident


def make_in_maps(inp, S_len):
    NT = S_len // 128
    OT = NT // NCORES
    f = lambda a: np.ascontiguousarray(np.asarray(a, dtype=np.float32))
    x = f(inp["x"])[0]
    pos = np.asarray(inp["positions"], dtype=np.int32)[0]
    bidx, mmask, invf, ident = _constants()
    col = lambda v, k: np.ascontiguousarray(f(v).reshape(k, 128).T)
    common = {
        "c_col": col(inp["c"][0], KC), "g_col": col(inp["norm_g"][0], KC),
        "ada_w": f(inp["ada_w"][0]), "ada_b": f(inp["ada_b"][0])[None, :],
        "w_in": f(inp["w_in"][0]), "w_out": f(inp["w_out"][0]), "w_uq": f(inp["w_uq"][0]),
        "w_uk": f(inp["w_uk"][0]), "w_uv": f(inp["w_uv"][0]),
        "qa_col": col(inp["mla_q_a_norm"][0], 2), "kva_col": col(inp["mla_kv_a_norm"][0], 1),
        "sub_col": col(inp["diff_sub_norm"][0], 1),
        "dqn": f(inp["diff_q_norm"][0])[None, :], "dkn": f(inp["diff_k_norm"][0])[None, :],
        "mqn": f(inp["mla_q_norm"][0])[None, :], "mkn": f(inp["mla_k_norm"][0])[None, :],
        "lam": f(inp["diff_lambda"][0]).reshape(1, 256), "relb": f(inp["rel_bias"]).reshape(1, 128),
        "invf": invf, "ident": ident, "bidx": bidx, "mmask": mmask,
    }
    maps = []
    for c in range(NCORES):
        sh = NCORES - 1 - c
        x_pad = np.zeros((NT * 128, D), np.float32)
        x_pad[sh * 128:] = x[:(NT - sh) * 128]
        pos_pad = np.zeros((NT * 128,), np.int32)
        pos_pad[sh * 128:] = pos[:(NT - sh) * 128]
        valid = np.ones((128, NT), np.float32)
        valid[:, :sh] = 0.0
        rows = np.concatenate([np.arange((NCORES * j + c) * 128, (NCORES * j + c + 1) * 128) for j in range(OT)])
        m = dict(common)
        m.update({"x_pad": x_pad, "x_own": np.ascontiguousarray(x[rows]), "pos_pad": pos_pad.reshape(NT, 128), "valid": valid})
        maps.append(m)
    return maps


def gather_out(results, S_len):
    NT = S_len // 128
    OT = NT // NCORES
    out = np.zeros((1, S_len, D), np.float32)
    for c in range(NCORES):
        y = np.asarray(results[c]["y_own"], dtype=np.float32)
        for j in range(OT):
            g = NCORES * j + c
            out[0, g * 128:(g + 1) * 128] = y[j * 128:(j + 1) * 128]
    return out


def kernel(**inputs):
    S_len = int(np.asarray(inputs["x"]).shape[1])
    nc = build(S_len)
    in_maps = make_in_maps(inputs, S_len)
    res = run_bass_kernel_spmd(nc, in_maps, core_ids=list(range(NCORES)))
    return gather_out(res.results, S_len)
```
